# Optimizing a Trainium2 kernel written in Bass

```python
import math
import jax
import jax.numpy as jnp
from jax import lax
import numpy as np

D_MODEL = 1024
BATCH = 16
SEQ = 4096
DEPTH = 4
DEC_BATCH = 4
DEC_SEQ = 4096
PAST_LEN = 128

HEAD_DIM = 64
MIX_WIDTH = D_MODEL // 2
N_EVEN = (DEPTH + 1) // 2
N_ODD = DEPTH // 2
NORM_EPS = 1e-6
NEG_INF = -1e30

GRID_W = 64
NA_HEADS = MIX_WIDTH // HEAD_DIM
NA_KH = 8
NA_KW = 16

HGRN_HEADS = 4
HGRN_DK = MIX_WIDTH // HGRN_HEADS
HGRN_DV = MIX_WIDTH // HGRN_HEADS
HGRN_CHUNK = 64

RG_BLOCKS = 8
RG_BW = MIX_WIDTH // RG_BLOCKS
RG_C = 8.0
CONV_W = 4
CONV_PAD = (CONV_W // 2, CONV_W - 1 - CONV_W // 2)

DIL_HEADS = MIX_WIDTH // HEAD_DIM
DIL_PAIRS = ((128, 1), (512, 4), (2048, 16))
DIL_HALF = DIL_PAIRS[0][0] // (2 * DIL_PAIRS[0][1])
T5_BUCKETS = 32
T5_MAX_DIST = 1024

D_FF = -(-(8 * D_MODEL) // (3 * 256)) * 256
EVEN_IN = 8 * MIX_WIDTH
ODD_IN = 5 * MIX_WIDTH

kernel_name = 'hybrid_bidir_encoder_na_hgrn2_rglru_dilated'


def _rms_norm(x, g):
    xf = x.astype(jnp.float32)
    y = xf * lax.rsqrt(jnp.mean(xf * xf, axis=-1, keepdims=True) + NORM_EPS)
    return (y * g.astype(jnp.float32)).astype(x.dtype)


def _swiglu(h, w_gate, w_up, w_down):
    return (jax.nn.silu(h @ w_gate) * (h @ w_up)) @ w_down


def _neighbourhood_attention(q, k, v, rpb):
    B, S, H, hd = q.shape
    rows = S // GRID_W
    kh = min(NA_KH, rows)
    r = jnp.arange(rows)
    c = jnp.arange(GRID_W)
    key_rows = jnp.clip(r - kh // 2, 0, rows - kh)[:, None] + jnp.arange(kh)[None, :]
    col_start = jnp.clip(c - NA_KW // 2, 0, GRID_W - NA_KW)
    col_ok = (c[None, :] >= col_start[:, None]) & (c[None, :] < col_start[:, None] + NA_KW)
    dr_idx = key_rows - r[:, None] + (NA_KH - 1)
    dc_idx = jnp.clip(c[None, :] - c[:, None] + (NA_KW - 1), 0, 2 * NA_KW - 2)
    bias = rpb.astype(jnp.float32)[:, dr_idx[:, :, None, None], dc_idx[None, None, :, :]]
    bias = bias.transpose(1, 0, 3, 2, 4)

    def grid(t):
        return t.reshape(B, rows, GRID_W, H, hd)

    kg = grid(k)[:, key_rows]
    vg = grid(v)[:, key_rows]
    s = jnp.einsum('brqhd,brkwhd->brhqkw', grid(q), kg).astype(jnp.float32) + bias
    s = jnp.where(col_ok[:, None, :], s, NEG_INF)
    p = jax.nn.softmax(s.reshape(B, rows, H, GRID_W, kh * GRID_W), axis=-1).reshape(s.shape)
    o = jnp.einsum('brhqkw,brkwhd->brqhd', p, vg.astype(jnp.float32))
    return o.reshape(B, S, H, hd)


def _hgrn2_scan(q, k, v, log_f):
    B, S, H, DK = q.shape
    DV = v.shape[-1]
    nc = S // HGRN_CHUNK

    def chunks(t):
        return t.reshape(B, nc, HGRN_CHUNK, H, t.shape[-1]).transpose(1, 0, 3, 2, 4)

    lower = jnp.tril(jnp.ones((HGRN_CHUNK, HGRN_CHUNK), dtype=bool))

    def step(state, inp):
        qc, kc, vc, gc = inp
        b = jnp.cumsum(gc, axis=2)
        diff = jnp.where(lower[:, :, None], b[:, :, :, None, :] - b[:, :, None, :, :], -jnp.inf)
        attn = jnp.einsum('bhtd,bhsd,bhtsd->bhts', qc, kc, jnp.exp(diff))
        o = jnp.einsum('bhts,bhsv->bhtv', attn, vc) + jnp.einsum('bhtd,bhdv->bhtv', qc * jnp.exp(b), state)
        b_last = b[:, :, -1:, :]
        state = jnp.exp(b_last[:, :, 0, :, None]) * state + jnp.einsum('bhsd,bhsv->bhdv', kc * jnp.exp(b_last - b), vc)
        return state, o

    s0 = jnp.zeros((B, H, DK, DV), jnp.float32)
    _, o = lax.scan(step, s0, (chunks(q), chunks(k), chunks(v), chunks(log_f)))
    return o.transpose(1, 0, 3, 2, 4).reshape(B, S, H, DV)


def _hgrn2(q, f_fwd, f_bwd, i, g, lb, onorm_g):
    B, S, _ = q.shape

    def heads(t):
        return t.astype(jnp.float32).reshape(B, S, HGRN_HEADS, -1)

    qh = jax.nn.silu(heads(q))
    vh = heads(i)
    lbh = lb.reshape(2, HGRN_HEADS, HGRN_DK)
    o = jnp.zeros((B, S, HGRN_HEADS, HGRN_DV), jnp.float32)
    for d, fz in enumerate((f_fwd, f_bwd)):
        z = heads(fz)
        log_f = jnp.logaddexp(jnp.log(lbh[d]), jnp.log1p(-lbh[d]) + jax.nn.log_sigmoid(z))
        kk = (1.0 - lbh[d]) * jax.nn.sigmoid(-z)
        args = (qh, kk, vh, log_f)
        if d == 1:
            args = tuple(jnp.flip(t, 1) for t in args)
        od = _hgrn2_scan(*args)
        o = o + (jnp.flip(od, 1) if d == 1 else od)
    o = o * lax.rsqrt(jnp.mean(o * o, axis=-1, keepdims=True) + NORM_EPS) * onorm_g.astype(jnp.float32)
    o = o * jax.nn.silu(heads(g))
    return o.reshape(B, S, MIX_WIDTH)


def _lin_combine(left, right):
    a_l, u_l = left
    a_r, u_r = right
    return a_l * a_r, a_r * u_l + u_r


def _rglru(y, w_a, b_a, w_x, b_x, lam, reverse):
    B, S, W = y.shape
    yf = y.astype(jnp.float32)
    yb = yf.reshape(B, S, RG_BLOCKS, RG_BW)

    def block_linear(w, b):
        return jnp.einsum('bsnc,ncd->bsnd', yb, w.astype(jnp.float32)).reshape(B, S, W) + b.astype(jnp.float32)

    r = jax.nn.sigmoid(block_linear(w_a, b_a))
    i = jax.nn.sigmoid(block_linear(w_x, b_x))
    log_a = -RG_C * r * jax.nn.softplus(-lam.astype(jnp.float32))
    scale = jnp.sqrt(-jnp.expm1(2.0 * log_a))
    first = S - 1 if reverse else 0
    scale = jnp.where((jnp.arange(S) == first)[None, :, None], 1.0, scale)
    _, h = lax.associative_scan(_lin_combine, (jnp.exp(log_a), scale * i * yf), axis=1, reverse=reverse)
    return h


def _t5_bucket(rel):
    half = T5_BUCKETS // 2
    exact = half // 2
    n = jnp.abs(rel)
    large = exact + (jnp.log(jnp.maximum(n, 1).astype(jnp.float32) / exact)
                     / math.log(T5_MAX_DIST / exact) * (half - exact)).astype(jnp.int32)
    large = jnp.minimum(large, half - 1)
    return jnp.where(rel > 0, half, 0) + jnp.where(n < exact, n, large)


def _dilated_branch(q, k, v, t5_bias, dil):
    B, S, H, hd = q.shape
    L = S // dil
    nb = -(-L // DIL_HALF)
    lp = nb * DIL_HALF
    bd = B * dil

    def to_sub(t):
        return t.reshape(B, L, dil, H, hd).transpose(0, 2, 1, 3, 4).reshape(bd, L, H, hd)

    qs = jnp.pad(to_sub(q), ((0, 0), (0, lp - L), (0, 0), (0, 0))).reshape(bd, nb, DIL_HALF, H, hd)

    def key_windows(t):
        tp = jnp.pad(to_sub(t), ((0, 0), (DIL_HALF, lp - L + DIL_HALF), (0, 0), (0, 0)))
        tp = tp.reshape(bd, nb + 2, DIL_HALF, H, hd)
        return jnp.concatenate([tp[:, :-2], tp[:, 1:-1], tp[:, 2:]], axis=2)

    kw = key_windows(k)
    vw = key_windows(v)
    qi = jnp.arange(DIL_HALF)
    ki = jnp.arange(3 * DIL_HALF)
    rel = ki[None, :] - DIL_HALF - qi[:, None]
    key_idx = jnp.arange(nb)[:, None] * DIL_HALF + ki[None, :] - DIL_HALF
    valid = ((key_idx >= 0) & (key_idx < L))[:, None, :] & (jnp.abs(rel) <= DIL_HALF)[None]
    bias = t5_bias.astype(jnp.float32)[_t5_bucket(rel * dil)].transpose(2, 0, 1)
    s = jnp.einsum('nbqhd,nbkhd->nbhqk', qs, kw).astype(jnp.float32) + bias
    s = jnp.where(valid[None, :, None], s, NEG_INF)
    m = jnp.max(s, axis=-1, keepdims=True)
    p = jnp.exp(s - m)
    den = jnp.sum(p, axis=-1)
    out = jnp.einsum('nbhqk,nbkhd->nbqhd', p, vw.astype(jnp.float32)) / den.transpose(0, 1, 3, 2)[..., None]
    lse = (m[..., 0] + jnp.log(den)).transpose(0, 1, 3, 2)
    out = out.reshape(bd, lp, H, hd)[:, :L].reshape(B, dil, L, H, hd).transpose(0, 2, 1, 3, 4).reshape(B, S, H, hd)
    lse = lse.reshape(bd, lp, H)[:, :L].reshape(B, dil, L, H).transpose(0, 2, 1, 3).reshape(B, S, H)
    return out, lse


def _dilated_attention(q, k, v, t5_bias):
    outs = []
    lses = []
    for _, dil in DIL_PAIRS:
        o, l = _dilated_branch(q, k, v, t5_bias, dil)
        outs.append(o)
        lses.append(l)
    wts = jax.nn.softmax(jnp.stack(lses), axis=0)
    return jnp.einsum('gbsh,gbshd->bshd', wts, jnp.stack(outs))


def _even_mixer(h, w_in, w_out, rpb, lb, onorm_g):
    B, S, _ = h.shape
    z = h @ w_in
    qa, ka, va, qb, fb_f, fb_b, ib, gb = jnp.split(z, 8, axis=-1)

    def heads(t):
        return t.reshape(B, S, NA_HEADS, HEAD_DIM)

    oa = _neighbourhood_attention(heads(qa) * HEAD_DIM ** -0.5, heads(ka), heads(va), rpb)
    ob = _hgrn2(qb, fb_f, fb_b, ib, gb, lb, onorm_g)
    mixed = jnp.concatenate([oa.reshape(B, S, MIX_WIDTH), ob], axis=-1).astype(h.dtype)
    return mixed @ w_out


def _odd_mixer(h, w_in, w_out, conv_w, conv_b, wa, ba, wx, bx, lam, t5_bias):
    B, S, _ = h.shape
    z = h @ w_in
    gc, xc, qd, kd, vd = jnp.split(z, 5, axis=-1)
    y = lax.conv_general_dilated(xc, conv_w[:, None, :], window_strides=(1,), padding=[CONV_PAD],
                                 dimension_numbers=('NWC', 'WIO', 'NWC'),
                                 feature_group_count=MIX_WIDTH) + conv_b
    hc = _rglru(y, wa[0], ba[0], wx[0], bx[0], lam[0], False) + _rglru(y, wa[1], ba[1], wx[1], bx[1], lam[1], True)
    oc = hc * jax.nn.gelu(gc.astype(jnp.float32))

    def heads(t):
        return t.reshape(B, S, DIL_HEADS, HEAD_DIM)

    od = _dilated_attention(heads(qd) * HEAD_DIM ** -0.5, heads(kd), heads(vd), t5_bias)
    mixed = jnp.concatenate([oc, od.reshape(B, S, MIX_WIDTH)], axis=-1).astype(h.dtype)
    return mixed @ w_out


def setup_inputs(seed: int = 0) -> dict:
    key = jax.random.key(seed)
    ks = jax.random.split(key, 22)

    def normal(k, shape, scale):
        return jax.random.normal(k, shape, jnp.float32) * scale

    a_pow = jax.random.uniform(ks[16], (N_ODD, 2, MIX_WIDTH), jnp.float32, 0.9, 0.999)
    a_base = a_pow ** (1.0 / RG_C)
    return {
        'x_prompt': normal(ks[0], (BATCH, SEQ, D_MODEL), 1.0),
        'x_sample': normal(ks[1], (DEC_BATCH, DEC_SEQ, D_MODEL), 1.0),
        'norm_g': 1.0 + normal(ks[2], (DEPTH, 4, D_MODEL), 0.02),
        'w_in_even': normal(ks[3], (N_EVEN, D_MODEL, EVEN_IN), D_MODEL ** -0.5),
        'w_out_even': normal(ks[4], (N_EVEN, 2 * MIX_WIDTH, D_MODEL), (2 * MIX_WIDTH) ** -0.5),
        'na_rpb': normal(ks[5], (N_EVEN, NA_HEADS, 2 * NA_KH - 1, 2 * NA_KW - 1), 0.1),
        'hgrn_lb': normal(ks[6], (N_EVEN, 2, MIX_WIDTH), 1.0),
        'hgrn_onorm': 1.0 + normal(ks[7], (N_EVEN, HGRN_DV), 0.02),
        'w_in_odd': normal(ks[8], (N_ODD, D_MODEL, ODD_IN), D_MODEL ** -0.5),
        'w_out_odd': normal(ks[9], (N_ODD, 2 * MIX_WIDTH, D_MODEL), (2 * MIX_WIDTH) ** -0.5),
        'conv_w': normal(ks[10], (N_ODD, CONV_W, MIX_WIDTH), CONV_W ** -0.5),
        'conv_b': normal(ks[11], (N_ODD, MIX_WIDTH), 0.01),
        'rg_wa': normal(ks[12], (N_ODD, 2, RG_BLOCKS, RG_BW, RG_BW), RG_BW ** -0.5),
        'rg_ba': normal(ks[13], (N_ODD, 2, MIX_WIDTH), 0.01),
        'rg_wx': normal(ks[14], (N_ODD, 2, RG_BLOCKS, RG_BW, RG_BW), RG_BW ** -0.5),
        'rg_bx': normal(ks[15], (N_ODD, 2, MIX_WIDTH), 0.01),
        'rg_lambda': jnp.log(a_base) - jnp.log1p(-a_base),
        't5_bias': normal(ks[17], (T5_BUCKETS, DIL_HEADS), 0.1),
        'w_gate': normal(ks[18], (DEPTH, D_MODEL, D_FF), D_MODEL ** -0.5),
        'w_up': normal(ks[19], (DEPTH, D_MODEL, D_FF), D_MODEL ** -0.5),
        'w_down': normal(ks[20], (DEPTH, D_FF, D_MODEL), D_FF ** -0.5),
    }


def reference(x_prompt, x_sample, norm_g, w_in_even, w_out_even, na_rpb, hgrn_lb, hgrn_onorm,
              w_in_odd, w_out_odd, conv_w, conv_b, rg_wa, rg_ba, rg_wx, rg_bx, rg_lambda,
              t5_bias, w_gate, w_up, w_down):
    lb_all = jnp.cumsum(jax.nn.softmax(hgrn_lb.astype(jnp.float32), axis=0), axis=0)
    lb_all = lb_all - lb_all[0:1]

    def trunk(x):
        for layer in range(DEPTH):
            idx = layer // 2
            h = _rms_norm(x, norm_g[layer, 0])
            if layer % 2 == 0:
                mix = _even_mixer(h, w_in_even[idx], w_out_even[idx], na_rpb[idx], lb_all[idx], hgrn_onorm[idx])
            else:
                mix = _odd_mixer(h, w_in_odd[idx], w_out_odd[idx], conv_w[idx], conv_b[idx], rg_wa[idx],
                                 rg_ba[idx], rg_wx[idx], rg_bx[idx], rg_lambda[idx], t5_bias)
            x = x + _rms_norm(mix, norm_g[layer, 1])
            h = _rms_norm(x, norm_g[layer, 2])
            x = x + _rms_norm(_swiglu(h, w_gate[layer], w_up[layer], w_down[layer]), norm_g[layer, 3])
        return x

    y_prompt = trunk(x_prompt)
    y_sample = trunk(x_sample)
    return (y_prompt, y_sample)
```

```python
import numpy as np
from contextlib import ExitStack
import concourse.bass as bass
import concourse.mybir as mybir
from concourse.bass_utils import run_bass_kernel_spmd

F32 = mybir.dt.float32
BF16 = mybir.dt.bfloat16
AF = mybir.ActivationFunctionType
ALU = mybir.AluOpType

T = 4096
D = 1024
NT = 32
DFF = 2816
EPS = 1e-6
SPC = 3
NCORES = 8
NEG = -30000.0


class Buf:
    __slots__ = ("w", "r")

    def __init__(self):
        self.w = None
        self.r = []


class Prog:
    def __init__(self, nc, stack):
        self.nc = nc
        self.stack = stack
        self.names = ("pe", "act", "dve", "pool", "sp")
        self.lists = {k: [] for k in self.names}
        self.esem = {}
        self.ecnt = {k: 0 for k in self.names}
        for k in ("pe", "act", "dve", "pool"):
            self.esem[k] = stack.enter_context(nc.semaphore("s_" + k))
        self.seen = {k: {} for k in self.names}
        self.bufs = {}
        self.chans = {}
        self.ninst = 0

    def B(self, *key):
        b = self.bufs.get(key)
        if b is None:
            b = self.bufs[key] = Buf()
        return b

    def chan(self, name):
        c = self.chans.get(name)
        if c is None:
            c = self.chans[name] = [self.stack.enter_context(self.nc.semaphore("c_" + name)), 0]
        return c

    def _deps(self, e, reads, writes):
        evs = []
        for b in reads:
            if b.w is not None:
                evs.append(b.w)
        for b in writes:
            if b.w is not None:
                evs.append(b.w)
            evs.extend(b.r)
        waits = {}
        seen = self.seen[e]
        for (sem, val, src) in evs:
            if src == "pe" and e == "pe":
                continue
            k = id(sem)
            if seen.get(k, 0) >= val:
                continue
            if k not in waits or waits[k][1] < val:
                waits[k] = (sem, val)
        for k, (sem, val) in waits.items():
            seen[k] = val
        return list(waits.values())

    def _commit(self, ev, reads, writes):
        for b in reads:
            b.r.append(ev)
        for b in writes:
            b.w = ev
            b.r = []

    def op(self, e, fn, reads=(), writes=()):
        waits = self._deps(e, reads, writes)
        self.ecnt[e] += 1
        ev = (self.esem[e], self.ecnt[e], e)
        self.lists[e].append((waits, fn, self.esem[e], 1))
        self._commit(ev, reads, writes)
        self.ninst += 1

    def dma(self, q, out, in_, reads, writes, chan, **kw):
        c = self.chan(chan)
        waits = self._deps(q, reads, writes)
        if c[1] > 0:
            k = id(c[0])
            if self.seen[q].get(k, 0) < c[1]:
                waits.append((c[0], c[1]))
                self.seen[q][k] = c[1]
        c[1] += 16
        ev = (c[0], c[1], "dma")
        self.lists[q].append((waits, lambda eng: eng.dma_start(out=out, in_=in_, **kw), c[0], 16))
        self._commit(ev, reads, writes)
        self.ninst += 1

    def barrier(self):
        for e in self.names:
            seen = self.seen[e]
            waits = []
            for k in ("pe", "act", "dve", "pool"):
                if k == e:
                    continue
                s, v = self.esem[k], self.ecnt[k]
                if v > 0 and seen.get(id(s), 0) < v:
                    waits.append((s, v))
                    seen[id(s)] = v
            for name, (s, v) in self.chans.items():
                if v > 0 and seen.get(id(s), 0) < v:
                    waits.append((s, v))
                    seen[id(s)] = v
            if waits:
                self.lists[e].append((waits, None, None, 0))
        for b in self.bufs.values():
            b.w = None
            b.r = []

    def emit(self):
        nc = self.nc
        lists = self.lists
        with nc.Block() as block:
            def mk(name):
                def body(eng):
                    for (waits, fn, sem, inc) in lists[name]:
                        for (s, v) in waits:
                            eng.wait_ge(s, v)
                        if fn is not None:
                            fn(eng).then_inc(sem, inc)
                return body
            block.tensor(mk("pe"))
            block.scalar(mk("act"))
            block.vector(mk("dve"))
            block.gpsimd(mk("pool"))
            block.sync(mk("sp"))
        self.lists = {k: [] for k in self.names}


def na_patterns():
    kk = np.arange(128)
    rk_l, ck = kk // 64, kk % 64
    pats = {}
    pairs = []
    drs, dcs, masks = [], [], []
    for j in range(NT):
        lst = []
        for i in range(NT):
            rq = 2 * j + rk_l[None, :]
            cq = ck[None, :]
            rk = 2 * i + rk_l[:, None]
            ckk = ck[:, None]
            start = np.clip(rq - 4, 0, 56)
            visr = (rk >= start) & (rk < start + 8)
            cs = np.clip(cq - 8, 0, 48)
            visc = (ckk >= cs) & (ckk < cs + 16)
            vis = visr & visc
            if not vis.any():
                continue
            dr = np.clip(rk - rq + 7, 0, 14) + 0 * cq
            dc = np.clip(ckk - cq + 15, 0, 30) + 0 * rq
            dr = np.where(vis, dr, 0)
            dc = np.where(vis, dc, 0)
            key = (vis.tobytes(), dr.astype(np.int8).tobytes(), dc.astype(np.int8).tobytes())
            if key not in pats:
                pats[key] = len(pats)
                drs.append(dr)
                dcs.append(dc)
                masks.append(np.where(vis, 0.0, NEG).astype(np.float32))
            lst.append((i, pats[key]))
        pairs.append(lst)
    return pairs, np.stack(drs), np.stack(dcs), np.stack(masks)


def t5_patterns():
    kk = np.arange(128)
    bks, cnts = [], []
    for dlt in range(-8, 9):
        d = 128 * dlt + kk[:, None] - kk[None, :]
        n = np.abs(d)
        cnt = (n <= 64).astype(np.int32) + ((d % 4 == 0) & (n <= 256)) + ((d % 16 == 0) & (n <= 1024))
        large = 8 + (np.log(np.maximum(n, 1).astype(np.float32) / np.float32(8)) / np.float32(np.log(1024 / 8)) * 8).astype(np.int32)
        large = np.minimum(large, 15)
        bk = np.where(d > 0, 16, 0) + np.where(n < 8, n, large)
        bks.append(bk)
        cnts.append(np.where(cnt > 0, np.log(np.maximum(cnt, 1)), NEG).astype(np.float32))
    return np.stack(bks), np.stack(cnts)


_UID = [0]


def _uniq(name):
    _UID[0] += 1
    return "%s_u%d" % (name, _UID[0])


NA_PAIRS, NA_DR, NA_DC, NA_MASK = na_patterns()
NPAT = NA_MASK.shape[0]
T5_BK, T5_CNT = t5_patterns()


def build(layers=(0, 1, 2, 3), spc=SPC, debug=False):
    nc = bass.Bass("TRN2", target_bir_lowering=False)

    def din(name, shape, dt=F32):
        return nc.dram_tensor(name, list(shape), dt, kind="ExternalInput").ap()

    def dscr(name, shape, dt):
        return nc.dram_tensor(name, list(shape), dt, kind="ExternalOutput" if debug else "Internal").ap()

    x_in = din("x", [spc, T, D])
    norm_g = din("norm_g", [4, 4, D])
    w_in_even = din("w_in_even", [2, D, 4096])
    w_out_even = din("w_out_even", [2, D, D])
    w_in_odd = din("w_in_odd", [2, D, 2560])
    w_out_odd = din("w_out_odd", [2, D, D])
    w_gate = din("w_gate", [4, D, DFF])
    w_up = din("w_up", [4, D, DFF])
    w_down = din("w_down", [4, DFF, D])
    hgrn_lb = din("hgrn_lb", [2, 2, 512])
    hgrn_onorm = din("hgrn_onorm", [2, 128])
    conv_w = din("conv_w", [2, 4, 512])
    conv_b = din("conv_b", [2, 512])
    rg_wa = din("rg_wa", [2, 2, 8, 64, 64])
    rg_ba = din("rg_ba", [2, 2, 512])
    rg_wx = din("rg_wx", [2, 2, 8, 64, 64])
    rg_bx = din("rg_bx", [2, 2, 512])
    rg_lambda = din("rg_lambda", [2, 2, 512])
    na_tab = din("na_tab", [2, NPAT, 8, 128, 128])
    na_mask = din("na_mask", [NPAT, 128, 128])
    t5_tab = din("t5_tab", [17, 8, 128, 128])
    t5_cnt = din("t5_cnt", [17, 128, 128])
    cmask_f = din("cmask_f", [1, T])
    cmask_b = din("cmask_b", [1, T])
    tri_f = din("tri_f", [64, 64])
    tri_b = din("tri_b", [64, 64])
    y_out = nc.dram_tensor("y", [spc, T, D], F32, kind="ExternalOutput").ap()

    xres = [dscr("xres0", [spc, T, D], F32), dscr("xres1", [spc, T, D], F32)]
    zqT = dscr("zqT", [spc, 512, T], BF16)
    zkT = dscr("zkT", [spc, 512, T], BF16)
    zv = dscr("zv", [spc, T, 512], BF16)
    zr = dscr("zr", [spc, 4, 512, T], F32)
    zi = dscr("zi", [spc, T, 512], F32)
    mix = dscr("mix", [spc, T, 512], BF16)
    mixT = dscr("mixT", [spc, 512, T], BF16)

    with ExitStack() as glob:
        P = Prog(nc, glob)

        def sb(stk, name, shape, dt):
            return stk.enter_context(nc.sbuf_tensor(_uniq(name), list(shape), dt))

        def pst(stk, name, shape, dt):
            return stk.enter_context(nc.psum_tensor(_uniq(name), list(shape), dt))

        identf = sb(glob, "identf", [128, 128], F32)
        ident = sb(glob, "ident", [128, 128], BF16)
        ones_b = sb(glob, "ones_b", [128, 128], BF16)
        Bid = P.B("ident")
        P.op("pool", lambda e: e.memset(identf[:], 0.0), [], [Bid])
        P.op("pool", lambda e: e.affine_select(out=identf[:], in_=identf[:], pattern=[[-1, 128]],
                                                compare_op=ALU.not_equal, fill=1.0, base=0,
                                                channel_multiplier=1), [Bid], [Bid])
        P.op("dve", lambda e: e.tensor_copy(out=ident[:], in_=identf[:]), [Bid], [Bid])
        P.op("pool", lambda e: e.memset(ones_b[:], 1.0), [], [Bid])
        P.barrier()
        P.emit()

        nlay = len(layers)
        for li, L in enumerate(layers):
            even = (L % 2 == 0)
            idx = L // 2
            xsrc = x_in if li == 0 else xres[(li - 1) % 2]
            xdst = y_out if li == nlay - 1 else xres[li % 2]
            with ExitStack() as ph:
                ncol = 4096 if even else 2560
                w_in = (w_in_even if even else w_in_odd)[idx]
                win = sb(ph, "win", [128, 8, ncol], BF16)
                g0 = sb(ph, "g0", [128, D], F32)
                Bw = P.B("win")
                wsrc = w_in.rearrange("(c p) n -> p c n", p=128)
                for dc in range(8):
                    P.dma("pool", win[:, dc, :], wsrc[:, dc, :], [], [Bw], "w%d" % dc)
                P.dma("sp", g0[:], norm_g[L, 0].partition_broadcast(128), [], [Bw], "gA")
                xt = [sb(ph, "xtA%d" % i, [128, D], F32) for i in range(4)]
                junk = sb(ph, "junkA", [128, D], F32)
                hb = [sb(ph, "hbA%d" % i, [128, D], BF16) for i in range(4)]
                hT = [sb(ph, "hTA%d" % i, [128, 8, 512], BF16) for i in range(2)]
                ss = sb(ph, "ssA", [128, 8], F32)
                stF = [sb(ph, "stF%d" % i, [128, 4, 512], F32) for i in range(4)]
                stH = [sb(ph, "stH%d" % i, [128, 4, 512], BF16) for i in range(4)]
                pT = [pst(ph, "pTA%d" % i, [128, 1024], BF16) for i in range(2)]
                pm = [pst(ph, "pmA%d" % i, [128, 512], F32) for i in range(4)]
                if even:
                    fm_parts = [(0, "qk", zqT, 0.125), (512, "qk", zkT, 1.0), (1536, "r", 0, 1.0),
                                (2048, "r", 1, 1.0), (2560, "r", 2, 1.0), (3584, "r", 3, 1.0)]
                    tm_parts = [(1024, "v"), (3072, "i")]
                else:
                    fm_parts = [(0, "r", 0, 1.0), (512, "r", 1, 1.0), (1024, "qk", zqT, 0.125),
                                (1536, "qk", zkT, 1.0)]
                    tm_parts = [(2048, "v")]
                cnt = 0
                gcnt = 0
                ecnt = 0
                scnt = {"F": 0, "H": 0}
                for s in range(spc):
                    for g in range(8):
                        hTg = hT[gcnt % 2]
                        BhT = P.B("hTA", gcnt % 2)
                        for i in range(4):
                            b = cnt % 4
                            pb2 = cnt % 2
                            tok0 = g * 512 + i * 128
                            Bx, Bh, Bss, BpT = P.B("xtA", b), P.B("hbA", b), P.B("ssA", b), P.B("pTA", pb2)
                            Bj = P.B("junkA")
                            P.dma("sp", xt[b][:], xsrc[s, tok0:tok0 + 128, :], [], [Bx], "xA%d" % b)
                            sc = ss[:, 2 * b:2 * b + 1]
                            rs = ss[:, 2 * b + 1:2 * b + 2]
                            P.op("pool", lambda e, sc=sc: e.memset(sc, 0.0), [], [Bss])
                            P.op("act", lambda e, b=b, sc=sc: e.activation(out=junk[:], in_=xt[b][:], func=AF.Square,
                                                                           accum_out=sc), [Bx, Bss], [Bj, Bss])
                            P.op("dve", lambda e, sc=sc, rs=rs: e.tensor_scalar(out=rs, in0=sc, scalar1=1.0 / D, scalar2=EPS,
                                                                               op0=ALU.mult, op1=ALU.add), [Bss], [Bss])
                            P.op("act", lambda e, rs=rs: e.activation(out=rs, in_=rs, func=AF.Sqrt), [Bss], [Bss])
                            P.op("dve", lambda e, rs=rs: e.reciprocal(out=rs, in_=rs), [Bss], [Bss])
                            P.op("dve", lambda e, b=b, rs=rs: e.scalar_tensor_tensor(out=hb[b][:], in0=xt[b][:], scalar=rs,
                                                                                    in1=g0[:], op0=ALU.mult, op1=ALU.mult),
                                 [Bx, Bss, Bw], [Bh])
                            for dc in range(8):
                                P.op("pe", lambda e, b=b, pb2=pb2, dc=dc: e.transpose(pT[pb2][:, dc * 128:(dc + 1) * 128],
                                                                                      hb[b][:, dc * 128:(dc + 1) * 128], ident[:]),
                                     [Bh], [BpT])
                            P.op("act", lambda e, pb2=pb2, i=i, hTg=hTg: e.copy(
                                out=hTg[:, :, i * 128:(i + 1) * 128],
                                in_=pT[pb2][:].rearrange("p (c t) -> p c t", c=8)), [BpT], [BhT])
                            cnt += 1
                        for (c0, kind, dst, scale) in fm_parts:
                            if kind == "qk":
                                k = scnt["H"] % 4
                                scnt["H"] += 1
                                st, Bst, chn = stH[k], P.B("stH", k), "sH%d" % k
                                dst_ap = dst[s].rearrange("(c p) t -> p c t", p=128)[:, :, g * 512:(g + 1) * 512]
                            else:
                                k = scnt["F"] % 4
                                scnt["F"] += 1
                                st, Bst, chn = stF[k], P.B("stF", k), "sF%d" % k
                                dst_ap = zr[s, dst].rearrange("(c p) t -> p c t", p=128)[:, :, g * 512:(g + 1) * 512]
                            for fc in range(4):
                                pb = ecnt % 4
                                Bpm = P.B("pmA", pb)
                                col = c0 + fc * 128
                                for dc in range(8):
                                    P.op("pe", lambda e, pb=pb, dc=dc, col=col, hTg=hTg: e.matmul(
                                        pm[pb][:], lhsT=win[:, dc, col:col + 128], rhs=hTg[:, dc, :],
                                        start=(dc == 0), stop=(dc == 7)), [Bw, BhT], [Bpm])
                                if ecnt % 2 == 0:
                                    P.op("act", lambda e, st=st, fc=fc, pb=pb, scale=scale: e.activation(
                                        out=st[:, fc, :], in_=pm[pb][:], func=AF.Copy, scale=scale), [Bpm], [Bst])
                                else:
                                    P.op("dve", lambda e, st=st, fc=fc, pb=pb, scale=scale: e.tensor_scalar(
                                        out=st[:, fc, :], in0=pm[pb][:], scalar1=scale, scalar2=None, op0=ALU.mult),
                                        [Bpm], [Bst])
                                ecnt += 1
                            P.dma("sp", dst_ap, st[:], [Bst], [], chn)
                        for (c0, kind) in tm_parts:
                            if kind == "v":
                                k = scnt["H"] % 4
                                scnt["H"] += 1
                                st, Bst, chn = stH[k], P.B("stH", k), "sH%d" % k
                                dst_ap = zv[s, g * 512:(g + 1) * 512, :].rearrange("(i p) n -> p i n", p=128)
                            else:
                                k = scnt["F"] % 4
                                scnt["F"] += 1
                                st, Bst, chn = stF[k], P.B("stF", k), "sF%d" % k
                                dst_ap = zi[s, g * 512:(g + 1) * 512, :].rearrange("(i p) n -> p i n", p=128)
                            for i in range(4):
                                pb = ecnt % 4
                                Bpm = P.B("pmA", pb)
                                for dc in range(8):
                                    P.op("pe", lambda e, pb=pb, dc=dc, i=i, c0=c0, hTg=hTg: e.matmul(
                                        pm[pb][:], lhsT=hTg[:, dc, i * 128:(i + 1) * 128], rhs=win[:, dc, c0:c0 + 512],
                                        start=(dc == 0), stop=(dc == 7)), [Bw, BhT], [Bpm])
                                if ecnt % 2 == 0:
                                    P.op("act", lambda e, st=st, i=i, pb=pb: e.copy(out=st[:, i, :], in_=pm[pb][:]),
                                         [Bpm], [Bst])
                                else:
                                    P.op("dve", lambda e, st=st, i=i, pb=pb: e.tensor_copy(out=st[:, i, :], in_=pm[pb][:]),
                                         [Bpm], [Bst])
                                ecnt += 1
                            P.dma("sp", dst_ap, st[:], [Bst], [], chn)
                        gcnt += 1
                P.barrier()
                P.emit()

            if even:
                mixer_attention(P, nc, spc, "na", idx, zqT, zkT, zv, mix, na_tab, na_mask, ident)
                mixer_hgrn2(P, nc, spc, idx, zr, zi, mixT, hgrn_lb, hgrn_onorm, cmask_f, cmask_b, tri_f, tri_b,
                            ident, ones_b)
            else:
                mixer_attention(P, nc, spc, "dil", idx, zqT, zkT, zv, mix, t5_tab, t5_cnt, ident)
                mixer_rglru(P, nc, spc, idx, zr, mixT, conv_w, conv_b, rg_wa, rg_ba, rg_wx, rg_bx, rg_lambda)

            with ExitStack() as ph:
                w_out = (w_out_even if even else w_out_odd)[idx]
                wout = sb(ph, "wout", [128, 8, D], BF16)
                wg = sb(ph, "wg", [128, 8, DFF], BF16)
                wu = sb(ph, "wu", [128, 8, DFF], BF16)
                wd = sb(ph, "wd", [128, 22, D], BF16)
                gam = sb(ph, "gamC", [128, 3, D], F32)
                Bw = P.B("wC")
                for dc in range(8):
                    P.dma("pool", wout[:, dc, :], w_out.rearrange("(c p) n -> p c n", p=128)[:, dc, :], [], [Bw], "w%d" % dc)
                for dc in range(8):
                    P.dma("pool", wg[:, dc, :], w_gate[L].rearrange("(c p) n -> p c n", p=128)[:, dc, :], [], [Bw], "w%d" % dc)
                    P.dma("pool", wu[:, dc, :], w_up[L].rearrange("(c p) n -> p c n", p=128)[:, dc, :], [], [Bw],
                          "w%d" % ((dc + 4) % 8))
                for fc in range(22):
                    P.dma("pool", wd[:, fc, :], w_down[L].rearrange("(c p) n -> p c n", p=128)[:, fc, :], [], [Bw],
                          "w%d" % (fc % 8))
                for k in range(3):
                    P.dma("sp", gam[:, k, :], norm_g[L, k + 1].partition_broadcast(128), [], [Bw], "gA")
                NX = 3
                xt = [sb(ph, "xtC%d" % i, [128, D], F32) for i in range(2)]
                xn = [sb(ph, "xnC%d" % i, [128, D], F32) for i in range(NX)]
                mtok = [sb(ph, "mtokC%d" % i, [128, 512], BF16) for i in range(2)]
                mfm = [sb(ph, "mfmC%d" % i, [128, 4, 128], BF16) for i in range(2)]
                matt = sb(ph, "mattC", [128, 4, 128], BF16)
                junk = sb(ph, "junkC", [128, D], BF16)
                tmp = [sb(ph, "tmpC%d" % i, [128, 512], F32) for i in range(2)]
                h2 = sb(ph, "h2C", [128, D], BF16)
                h2T = sb(ph, "h2TC", [128, 8, 128], BF16)
                aa = sb(ph, "aC", [128, DFF], BF16)
                aT = sb(ph, "aTC", [128, 22, 128], BF16)
                ss = sb(ph, "ssC", [128, 8 * NX], F32)
                epst = sb(ph, "epsC", [128, 1], F32)
                pT = [pst(ph, "pTC%d" % i, [128, 1024], BF16) for i in range(2)]
                pout = [pst(ph, "poC%d" % i, [128, 512], F32) for i in range(2)]
                pdn = [pst(ph, "pdC%d" % i, [128, 512], F32) for i in range(2)]
                pgu = [pst(ph, "pguC%d" % i, [128, 512], F32) for i in range(2)]
                Bj, Bh2, Bh2T, Ba, BaT, Bmatt = (P.B("junkC"), P.B("h2C"), P.B("h2TC"), P.B("aC"), P.B("aTC"),
                                                 P.B("mattC"))
                BpT = [P.B("pTC", i) for i in range(2)]
                Bpo = [P.B("poC", i) for i in range(2)]
                Bpd = [P.B("pdC", i) for i in range(2)]
                Bpg = [P.B("pguC", i) for i in range(2)]
                Btmp = [P.B("tmpC", i) for i in range(2)]
                Beps = P.B("epsC")
                P.op("pool", lambda e: e.memset(epst[:], EPS), [], [Beps])
                ptc = [0]
                tiles = [(s, j) for s in range(spc) for j in range(NT)]

                def rstd_chain(Bss, sq0, sq1, out_col):
                    if sq1 is not None:
                        P.op("dve", lambda e: e.tensor_tensor(out=sq0, in0=sq0, in1=sq1, op=ALU.add), [Bss], [Bss])
                    P.op("act", lambda e: e.activation(out=out_col, in_=sq0, func=AF.Sqrt, bias=epst[:], scale=1.0 / D),
                         [Bss, Beps], [Bss])
                    P.op("dve", lambda e: e.reciprocal(out=out_col, in_=out_col), [Bss], [Bss])

                def stage1(n):
                    s, j = tiles[n]
                    b = n % 2
                    k3 = n % NX
                    tok0 = j * 128
                    Bx, Bmt, Bmf, Bxn, Bss = P.B("xtC", b), P.B("mtokC", b), P.B("mfmC", b), P.B("xnC", k3), P.B("ssC", k3)
                    sc = ss[:, 8 * k3:8 * k3 + 8]
                    P.dma("sp", xt[b][:], xsrc[s, tok0:tok0 + 128, :], [], [Bx], "xA%d" % b)
                    P.dma("sp", mtok[b][:], mix[s, tok0:tok0 + 128, :], [], [Bmt], "mtC%d" % b)
                    P.dma("sp", mfm[b][:], mixT[s].rearrange("(c p) t -> p c t", p=128)[:, :, tok0:tok0 + 128],
                          [], [Bmf], "mfC%d" % b)
                    P.op("pool", lambda e: e.memset(sc, 0.0), [], [Bss])
                    pt = ptc[0] % 2
                    ptc[0] += 1
                    for c in range(4):
                        P.op("pe", lambda e, pt=pt, c=c, b=b: e.transpose(pT[pt][:, c * 128:(c + 1) * 128],
                                                                          mtok[b][:, c * 128:(c + 1) * 128], ident[:]),
                             [Bmt], [BpT[pt]])
                    P.op("act", lambda e, pt=pt: e.copy(out=matt[:], in_=pT[pt][:, 0:512].rearrange("p (c t) -> p c t", c=4)),
                         [BpT[pt]], [Bmatt])
                    if even:
                        chunks = [matt[:, c, :] for c in range(4)] + [mfm[b][:, c, :] for c in range(4)]
                    else:
                        chunks = [mfm[b][:, c, :] for c in range(4)] + [matt[:, c, :] for c in range(4)]
                    for half in range(2):
                        for c in range(8):
                            P.op("pe", lambda e, half=half, c=c, ch=chunks[c]: e.matmul(
                                pout[half][:], lhsT=ch, rhs=wout[:, c, half * 512:(half + 1) * 512],
                                start=(c == 0), stop=(c == 7)), [Bmatt, Bmf, Bw], [Bpo[half]])
                    for half in range(2):
                        P.op("act", lambda e, half=half: e.activation(out=junk[:, half * 512:(half + 1) * 512],
                                                                      in_=pout[half][:], func=AF.Square,
                                                                      accum_out=sc[:, half:half + 1]),
                             [Bpo[half], Bss], [Bj, Bss])
                    rstd_chain(Bss, sc[:, 0:1], sc[:, 1:2], sc[:, 2:3])
                    for half in range(2):
                        P.op("dve", lambda e, half=half: e.scalar_tensor_tensor(
                            out=tmp[half][:], in0=pout[half][:], scalar=sc[:, 2:3],
                            in1=gam[:, 0, half * 512:(half + 1) * 512], op0=ALU.mult, op1=ALU.mult),
                            [Bpo[half], Bss, Bw], [Btmp[half]])
                        P.op("dve", lambda e, half=half, b=b, k3=k3: e.tensor_tensor(
                            out=xn[k3][:, half * 512:(half + 1) * 512], in0=tmp[half][:],
                            in1=xt[b][:, half * 512:(half + 1) * 512], op=ALU.add), [Btmp[half], Bx], [Bxn])
                    P.op("act", lambda e, k3=k3: e.activation(out=junk[:], in_=xn[k3][:], func=AF.Square, accum_out=sc[:, 3:4]),
                         [Bxn, Bss], [Bj, Bss])
                    rstd_chain(Bss, sc[:, 3:4], None, sc[:, 4:5])
                    P.op("dve", lambda e, k3=k3: e.scalar_tensor_tensor(out=h2[:], in0=xn[k3][:], scalar=sc[:, 4:5],
                                                                        in1=gam[:, 1, :], op0=ALU.mult, op1=ALU.mult),
                         [Bxn, Bss, Bw], [Bh2])

                def stage2(n):
                    pt = ptc[0] % 2
                    ptc[0] += 1
                    for dc in range(8):
                        P.op("pe", lambda e, pt=pt, dc=dc: e.transpose(pT[pt][:, dc * 128:(dc + 1) * 128],
                                                                       h2[:, dc * 128:(dc + 1) * 128], ident[:]),
                             [Bh2], [BpT[pt]])
                    P.op("act", lambda e, pt=pt: e.copy(out=h2T[:], in_=pT[pt][:].rearrange("p (c t) -> p c t", c=8)),
                         [BpT[pt]], [Bh2T])
                    for fb in range(6):
                        f0 = fb * 512
                        wdt = min(512, DFF - f0)
                        for dc in range(8):
                            P.op("pe", lambda e, dc=dc, f0=f0, wdt=wdt: e.matmul(
                                pgu[0][:, 0:wdt], lhsT=h2T[:, dc, :], rhs=wg[:, dc, f0:f0 + wdt],
                                start=(dc == 0), stop=(dc == 7)), [Bh2T, Bw], [Bpg[0]])
                        for dc in range(8):
                            P.op("pe", lambda e, dc=dc, f0=f0, wdt=wdt: e.matmul(
                                pgu[1][:, 0:wdt], lhsT=h2T[:, dc, :], rhs=wu[:, dc, f0:f0 + wdt],
                                start=(dc == 0), stop=(dc == 7)), [Bh2T, Bw], [Bpg[1]])
                        tb = fb % 2
                        P.op("act", lambda e, tb=tb, wdt=wdt: e.activation(out=tmp[tb][:, 0:wdt], in_=pgu[0][:, 0:wdt],
                                                                           func=AF.Silu), [Bpg[0]], [Btmp[tb]])
                        P.op("dve", lambda e, tb=tb, f0=f0, wdt=wdt: e.tensor_tensor(
                            out=aa[:, f0:f0 + wdt], in0=tmp[tb][:, 0:wdt], in1=pgu[1][:, 0:wdt], op=ALU.mult),
                            [Btmp[tb], Bpg[1]], [Ba])

                def stage3(n):
                    s, j = tiles[n]
                    k3 = n % NX
                    tok0 = j * 128
                    Bxn, Bss = P.B("xnC", k3), P.B("ssC", k3)
                    sc = ss[:, 8 * k3:8 * k3 + 8]
                    for r0 in range(0, 22, 8):
                        nch = min(8, 22 - r0)
                        pt = ptc[0] % 2
                        ptc[0] += 1
                        for c in range(nch):
                            fc = r0 + c
                            P.op("pe", lambda e, pt=pt, c=c, fc=fc: e.transpose(pT[pt][:, c * 128:(c + 1) * 128],
                                                                                aa[:, fc * 128:(fc + 1) * 128], ident[:]),
                                 [Ba], [BpT[pt]])
                        if r0 != 8:
                            P.op("act", lambda e, pt=pt, r0=r0, nch=nch: e.copy(
                                out=aT[:, r0:r0 + nch, :],
                                in_=pT[pt][:, 0:nch * 128].rearrange("p (c t) -> p c t", c=nch)), [BpT[pt]], [BaT])
                        else:
                            P.op("dve", lambda e, pt=pt, r0=r0, nch=nch: e.tensor_copy(
                                out=aT[:, r0:r0 + nch, :],
                                in_=pT[pt][:, 0:nch * 128].rearrange("p (c t) -> p c t", c=nch)), [BpT[pt]], [BaT])
                    for half in range(2):
                        for fc in range(22):
                            P.op("pe", lambda e, half=half, fc=fc: e.matmul(
                                pdn[half][:], lhsT=aT[:, fc, :], rhs=wd[:, fc, half * 512:(half + 1) * 512],
                                start=(fc == 0), stop=(fc == 21)), [BaT, Bw], [Bpd[half]])
                    for half in range(2):
                        P.op("act", lambda e, half=half: e.activation(out=junk[:, half * 512:(half + 1) * 512],
                                                                      in_=pdn[half][:], func=AF.Square,
                                                                      accum_out=sc[:, 5 + half:6 + half]),
                             [Bpd[half], Bss], [Bj, Bss])
                    rstd_chain(Bss, sc[:, 5:6], sc[:, 6:7], sc[:, 7:8])
                    for half in range(2):
                        P.op("dve", lambda e, half=half: e.scalar_tensor_tensor(
                            out=tmp[half][:], in0=pdn[half][:], scalar=sc[:, 7:8],
                            in1=gam[:, 2, half * 512:(half + 1) * 512], op0=ALU.mult, op1=ALU.mult),
                            [Bpd[half], Bss, Bw], [Btmp[half]])
                        P.op("dve", lambda e, half=half, k3=k3: e.tensor_tensor(
                            out=xn[k3][:, half * 512:(half + 1) * 512], in0=tmp[half][:],
                            in1=xn[k3][:, half * 512:(half + 1) * 512], op=ALU.add), [Btmp[half], Bxn], [Bxn])
                    P.dma("sp", xdst[s, tok0:tok0 + 128, :], xn[k3][:], [Bxn], [], "xoC%d" % k3)

                ntile = len(tiles)
                stage1(0)
                for n in range(ntile):
                    stage2(n)
                    if n + 1 < ntile:
                        stage1(n + 1)
                    stage3(n)
                P.barrier()
                P.emit()
    return nc


def mixer_attention(P, nc, spc, kind, idx, zqT, zkT, zv, mix, tab, msk, ident):
    with ExitStack() as ph:
        def sb(name, shape, dt):
            return ph.enter_context(nc.sbuf_tensor(_uniq(name), list(shape), dt))

        def pst(name, shape, dt):
            return ph.enter_context(nc.psum_tensor(_uniq(name), list(shape), dt))

        if kind == "na":
            npat = NPAT
            pairs = NA_PAIRS
            tabl = tab[idx]
        else:
            npat = 17
            pairs = [[(i, i - j + 8) for i in range(max(0, j - 8), min(NT, j + 9))] for j in range(NT)]
            tabl = tab
        bias = sb("biasE", [128, npat * 8, 128], BF16)
        tb32 = [sb("tb32_%d" % i, [128, 8, 128], F32) for i in range(2)]
        mk32 = [sb("mk32_%d" % i, [128, 128], F32) for i in range(2)]
        Bbias = P.B("biasT")
        for p in range(npat):
            k = p % 2
            Bt, Bm = P.B("tb32", k), P.B("mk32", k)
            P.dma("sp", tb32[k][:], tabl[p].rearrange("h k q -> k h q"), [], [Bt], "tb%d" % k)
            P.dma("sp", mk32[k][:], msk[p], [], [Bm], "mk%d" % k)
            P.op("dve", lambda e, k=k, p=p: e.tensor_tensor(
                out=tb32[k][:], in0=tb32[k][:],
                in1=mk32[k][:].unsqueeze(1).to_broadcast([128, 8, 128]), op=ALU.add), [Bt, Bm], [Bt])
            P.op("act", lambda e, k=k, p=p: e.activation(out=bias[:, p * 8:(p + 1) * 8, :], in_=tb32[k][:], func=AF.Exp),
                 [Bt], [Bbias])
        QZ = sb("QZ", [128, 4, NT, 2, 128], BF16)
        KT = sb("KT", [128, 4, T], BF16)
        VA = sb("VA", [128, NT, 8, 65], BF16)
        NB = 4
        XS = [sb("XS%d" % i, [128, 512], BF16) for i in range(NB)]
        PT = [sb("PT%d" % i, [128, 512], BF16) for i in range(NB)]
        rden = sb("rden", [128, 8], F32)
        mo = [sb("mo%d" % i, [128, 512], BF16) for i in range(2)]
        pS = [pst("pS%d" % i, [128, 512], F32) for i in range(NB)]
        pO = [pst("pO%d" % i, [128, 512], F32) for i in range(4)]
        BQ, BK, BV, Brd = P.B("QZ"), P.B("KT"), P.B("VA"), P.B("rden")
        P.op("pool", lambda e: e.memset(VA[:, :, :, 64:65], 1.0), [], [BV])
        P.op("pool", lambda e: e.memset(QZ[:, 0:2], 0.0), [], [BQ])
        P.op("pool", lambda e: e.memset(QZ[:, 2:4], 0.0), [], [BQ])
        it = 0
        oc = 0
        for s in range(spc):
            for c in range(4):
                P.dma("sp", QZ[0:64, c, :, 0, :], zqT[s, c * 128:c * 128 + 64, :].rearrange("p (j q) -> p j q", q=128),
                      [], [BQ], "ldq%d" % c)
                P.dma("sp", QZ[64:128, c, :, 1, :], zqT[s, c * 128 + 64:c * 128 + 128, :].rearrange("p (j q) -> p j q", q=128),
                      [], [BQ], "ldqb%d" % c)
                P.dma("sp", KT[:, c, :], zkT[s, c * 128:(c + 1) * 128, :], [], [BK], "ldk%d" % c)
            for i4 in range(NT):
                P.dma("sp", VA[:, i4, :, 0:64],
                      zv[s, i4 * 128:(i4 + 1) * 128, :].rearrange("p (h d) -> p h d", d=64),
                      [], [BV], "ldv%d" % (i4 % 8))
            iters = []
            for j in range(NT):
                for hg in range(2):
                    lst = pairs[j]
                    for n, (i, pat) in enumerate(lst):
                        iters.append((j, hg, n, i, pat, len(lst)))
            state = {}

            def emit_scores(itn, j, hg, n, i, pat, ln):
                sbk = itn % NB
                BS, BX, BP = P.B("pS", sbk), P.B("XS", sbk), P.B("PT", sbk)
                for pp in range(2):
                    c = hg * 2 + pp
                    P.op("pe", lambda e, sbk=sbk, pp=pp, c=c, i=i, j=j: e.matmul(
                        pS[sbk][:, pp * 256:(pp + 1) * 256], lhsT=KT[:, c, i * 128:(i + 1) * 128],
                        rhs=QZ[:, c, j].rearrange("p a q -> p (a q)"), start=(pp == 0), stop=True,
                        skip_group_check=True), [BQ, BK], [BS])
                P.op("act", lambda e, sbk=sbk: e.activation(out=XS[sbk][:], in_=pS[sbk][:], func=AF.Exp), [BS], [BX])
                P.op("dve", lambda e, sbk=sbk, pat=pat, hg=hg: e.tensor_tensor(
                    out=PT[sbk][:].rearrange("p (h q) -> p h q", h=4), in0=XS[sbk][:].rearrange("p (h q) -> p h q", h=4),
                    in1=bias[:, pat * 8 + hg * 4:pat * 8 + hg * 4 + 4, :], op=ALU.mult), [BX, Bbias], [BP])

            def emit_pv(itn, j, hg, n, i, pat, ln):
                sbk = itn % NB
                BP = P.B("PT", sbk)
                if n == 0:
                    state["ob"] = state.get("oc", 0) % 4
                    state["oc"] = state.get("oc", 0) + 1
                ob = state["ob"]
                BO = P.B("pO", ob)
                for hh in range(4):
                    h = hg * 4 + hh
                    P.op("pe", lambda e, sbk=sbk, hh=hh, h=h, i=i, ob=ob, n=n, ln=ln: e.matmul(
                        pO[ob][:, hh * 65:(hh + 1) * 65], lhsT=PT[sbk][:, hh * 128:(hh + 1) * 128],
                        rhs=VA[:, i, h, :], start=(n == 0 and hh == 0), stop=(n == ln - 1),
                        skip_group_check=True), [BP, BV], [BO])
                if n == ln - 1:
                    mb = j % 2
                    Bmo = P.B("mo", mb)
                    P.op("dve", lambda e, ob=ob, hg=hg: e.reciprocal(
                        out=rden[:, hg * 4:(hg + 1) * 4],
                        in_=pO[ob][:, 0:260].rearrange("p (h d) -> p h d", d=65)[:, :, 64]), [BO], [Brd])
                    for hh in range(4):
                        h = hg * 4 + hh
                        P.op("dve", lambda e, ob=ob, hh=hh, h=h, mb=mb: e.tensor_scalar(
                            out=mo[mb][:, h * 64:(h + 1) * 64], in0=pO[ob][:, hh * 65:hh * 65 + 64],
                            scalar1=rden[:, h:h + 1], scalar2=None, op0=ALU.mult), [BO, Brd], [Bmo])
                    if hg == 1:
                        P.dma("sp", mix[s, j * 128:(j + 1) * 128, :], mo[mb][:], [Bmo], [], "mo%d" % mb)

            base = it
            LA = 2
            for q in range(min(LA, len(iters))):
                emit_scores(base + q, *iters[q])
            for n_it in range(len(iters)):
                if n_it + LA < len(iters):
                    emit_scores(base + n_it + LA, *iters[n_it + LA])
                emit_pv(base + n_it, *iters[n_it])
            it = base + len(iters)
        P.barrier()
        P.emit()


def mixer_hgrn2(P, nc, spc, idx, zr, zi, mixT, hgrn_lb, hgrn_onorm, cmask_f, cmask_b, tri_f, tri_b, ident, ones_b):
    with ExitStack() as ph:
        def sb(name, shape, dt):
            return ph.enter_context(nc.sbuf_tensor(_uniq(name), list(shape), dt))

        def pst(name, shape, dt):
            return ph.enter_context(nc.psum_tensor(_uniq(name), list(shape), dt))

        NCH = 64
        qs = sb("hq", [128, T], F32)
        zin = sb("hz", [128, T], F32)
        tA = sb("hA", [128, T], F32)
        tB = sb("hB", [128, T], F32)
        tC = sb("hC", [128, T], F32)
        Qt = [sb("hQt%d" % d, [128, T], BF16) for d in range(2)]
        Kt = [sb("hKt%d" % d, [128, T], BF16) for d in range(2)]
        Qh = [sb("hQh%d" % d, [128, T], BF16) for d in range(2)]
        Khf = sb("hKhf", [128, T], BF16)
        Kh = [sb("hKh%d" % d, [64, NCH, 128], BF16) for d in range(2)]
        Vt = sb("hV", [64, NCH, 128], BF16)
        dec = [sb("hdec%d" % d, [128, NCH], F32) for d in range(2)]
        cm = [sb("hcm%d" % d, [128, T], BF16) for d in range(2)]
        tri = [sb("htri%d" % d, [64, 64], F32) for d in range(2)]
        St = [sb("hS%d" % d, [128, 128], F32) for d in range(2)]
        Sb = [sb("hSb%d" % d, [128, 128], BF16) for d in range(2)]
        aT = [sb("haT%d_%d" % (d, k), [64, 64], BF16) for d in range(2) for k in range(2)]
        par = sb("hpar", [128, 8], F32)
        rstd = sb("hrstd", [128, 512], F32)
        ob16 = Qt[0]
        pa = [pst("hpa%d" % k, [64, 64], F32) for k in range(2)]
        po = [pst("hpo%d" % d, [128, 512], F32) for d in range(2)]
        pss = [pst("hps%d" % d, [128, 128], F32) for d in range(2)]
        ptr = pst("hptr", [64, 1024], BF16)
        pn = pst("hpn", [128, 512], F32)
        Bq, Bz, BA, BB, BC = P.B("hq"), P.B("hz"), P.B("hA"), P.B("hB"), P.B("hC")
        Bc = P.B("hconst")
        Bpar = P.B("hpar")
        P.dma("pool", cm[0][:], cmask_f[0].partition_broadcast(128), [], [Bc], "hc0")
        P.dma("pool", cm[1][:], cmask_b[0].partition_broadcast(128), [], [Bc], "hc1")
        P.dma("sp", tri[0][:], tri_f, [], [Bc], "hc2")
        P.dma("sp", tri[1][:], tri_b, [], [Bc], "hc3")
        for hd in range(4):
            f0 = hd * 128
            for d in range(2):
                P.dma("sp", par[:, d:d + 1], hgrn_lb[0, d, f0:f0 + 128].rearrange("(p o) -> p o", o=1), [], [Bpar], "hp0")
                P.dma("sp", par[:, 2 + d:3 + d], hgrn_lb[1, d, f0:f0 + 128].rearrange("(p o) -> p o", o=1), [], [Bpar], "hp1")
            P.dma("sp", par[:, 6:7], hgrn_onorm[idx].rearrange("(p o) -> p o", o=1), [], [Bpar], "hp2")
            if idx == 0:
                P.op("pool", lambda e: e.memset(par[:, 2:4], 0.0), [Bpar], [Bpar])
            else:
                P.op("dve", lambda e: e.tensor_tensor(out=par[:, 2:4], in0=par[:, 2:4], in1=par[:, 0:2], op=ALU.subtract),
                     [Bpar], [Bpar])
                P.op("act", lambda e: e.activation(out=par[:, 2:4], in_=par[:, 2:4], func=AF.Sigmoid), [Bpar], [Bpar])
            P.op("dve", lambda e: e.tensor_scalar(out=par[:, 4:6], in0=par[:, 2:4], scalar1=-1.0, scalar2=1.0,
                                                  op0=ALU.mult, op1=ALU.add), [Bpar], [Bpar])
            for s in range(spc):
                P.dma("sp", zin[:], zr[s, 0, f0:f0 + 128, :], [], [Bz], "hz")
                P.op("act", lambda e: e.activation(out=qs[:], in_=zin[:], func=AF.Silu), [Bz], [Bq])
                P.dma("pool", Vt[:], zi[s, :, f0:f0 + 128].rearrange("(c p) v -> p c v", p=64), [], [P.B("hV")], "hv")
                for d in range(2):
                    BQt, BKt, BQh, BKhf, BKh, Bdec = (P.B("hQt", d), P.B("hKt", d), P.B("hQh", d), P.B("hKhf"),
                                                      P.B("hKh", d), P.B("hdec", d))
                    P.dma("sp", zin[:], zr[s, 1 + d, f0:f0 + 128, :], [], [Bz], "hz")
                    P.op("act", lambda e: e.activation(out=tA[:], in_=zin[:], func=AF.Sigmoid), [Bz], [BA])
                    P.op("dve", lambda e, d=d: e.tensor_scalar(out=tA[:], in0=tA[:], scalar1=par[:, 4 + d:5 + d],
                                                               scalar2=par[:, 2 + d:3 + d], op0=ALU.mult, op1=ALU.add),
                         [BA, Bpar], [BA])
                    P.op("act", lambda e: e.activation(out=tB[:], in_=tA[:], func=AF.Ln), [BA], [BB])
                    P.op("dve", lambda e: e.tensor_scalar(out=tA[:], in0=tA[:], scalar1=-1.0, scalar2=1.0,
                                                          op0=ALU.mult, op1=ALU.add), [BA], [BA])
                    if d == 0:
                        P.op("dve", lambda e: e.tensor_tensor_scan(out=zin[:], data0=cm[0][:], data1=tB[:], initial=0.0,
                                                                   op0=ALU.mult, op1=ALU.add), [BB, Bc, Bz], [Bz])
                    else:
                        P.op("dve", lambda e: e.tensor_tensor_scan(out=zin[:, ::-1], data0=cm[1][:, ::-1],
                                                                   data1=tB[:, ::-1], initial=0.0,
                                                                   op0=ALU.mult, op1=ALU.add), [BB, Bc, Bz], [Bz])
                    b3 = zin[:].rearrange("p (c t) -> p c t", t=64)
                    B3 = tB[:].rearrange("p (c t) -> p c t", t=64)
                    C3 = tC[:].rearrange("p (c t) -> p c t", t=64)
                    mid = 31 if d == 0 else 32
                    last = 63 if d == 0 else 0
                    P.op("dve", lambda e, mid=mid: e.tensor_tensor(out=B3, in0=b3,
                                                                   in1=b3[:, :, mid:mid + 1].to_broadcast([128, NCH, 64]),
                                                                   op=ALU.subtract), [Bz, BB], [BB])
                    P.op("act", lambda e: e.activation(out=tC[:], in_=tB[:], func=AF.Exp), [BB], [BC])
                    P.op("dve", lambda e, d=d: e.tensor_tensor(out=Qt[d][:], in0=qs[:], in1=tC[:], op=ALU.mult),
                         [Bq, BC], [BQt])
                    P.op("act", lambda e: e.activation(out=tC[:], in_=tB[:], func=AF.Exp, scale=-1.0), [BB, BC], [BC])
                    P.op("dve", lambda e, d=d: e.tensor_tensor(out=Kt[d][:], in0=tA[:], in1=tC[:], op=ALU.mult),
                         [BA, BC], [BKt])
                    P.op("act", lambda e: e.activation(out=tC[:], in_=zin[:], func=AF.Exp), [Bz, BC], [BC])
                    P.op("dve", lambda e, d=d: e.tensor_tensor(out=Qh[d][:], in0=qs[:], in1=tC[:], op=ALU.mult),
                         [Bq, BC], [BQh])
                    P.op("dve", lambda e, d=d, last=last: e.tensor_copy(out=dec[d][:], in_=C3[:, :, last]), [BC], [Bdec])
                    P.op("dve", lambda e, last=last: e.tensor_tensor(out=B3, in0=b3,
                                                                     in1=b3[:, :, last:last + 1].to_broadcast([128, NCH, 64]),
                                                                     op=ALU.subtract), [Bz, BB], [BB])
                    P.op("act", lambda e: e.activation(out=tC[:], in_=tB[:], func=AF.Exp, scale=-1.0), [BB, BC], [BC])
                    P.op("dve", lambda e: e.tensor_tensor(out=Khf[:], in0=tA[:], in1=tC[:], op=ALU.mult), [BA, BC], [BKhf])
                    Bptr = P.B("hptr")
                    for c8 in range(8):
                        for cc in range(8):
                            c = c8 * 8 + cc
                            P.op("pe", lambda e, c=c, cc=cc: e.transpose(ptr[:, cc * 128:(cc + 1) * 128],
                                                                         Khf[:, c * 64:(c + 1) * 64], ident[:]),
                                 [BKhf], [Bptr])
                        P.op("act", lambda e, d=d, c8=c8: e.copy(out=Kh[d][:, c8 * 8:(c8 + 1) * 8, :],
                                                                 in_=ptr[:].rearrange("p (c v) -> p c v", c=8)),
                             [Bptr], [BKh])
                BO = [BA, BB]
                osb = [tA, tB]
                for d in range(2):
                    P.op("pool", lambda e, d=d: e.memset(St[d][:], 0.0), [], [P.B("hS", d)])
                    P.op("pool", lambda e, d=d: e.memset(Sb[d][:], 0.0), [], [P.B("hSb", d)])
                def chunk_of(step, d):
                    return step if d == 0 else NCH - 1 - step

                def emit_pa(step, d):
                    c = chunk_of(step, d)
                    k = step % 2
                    cs = slice(c * 64, (c + 1) * 64)
                    Bpa, BaT = P.B("hpa", d), P.B("haT", d, k)
                    P.op("pe", lambda e, d=d, cs=cs: e.matmul(pa[d][:], lhsT=Kt[d][:, cs], rhs=Qt[d][:, cs],
                                                              start=True, stop=True),
                         [P.B("hKt", d), P.B("hQt", d)], [Bpa])
                    P.op("dve", lambda e, d=d, k=k: e.tensor_tensor(out=aT[d * 2 + k][:], in0=pa[d][:], in1=tri[d][:],
                                                                    op=ALU.mult), [Bpa, Bc], [BaT])

                for d in range(2):
                    emit_pa(0, d)
                for step in range(NCH):
                    if step + 1 < NCH:
                        for d in range(2):
                            emit_pa(step + 1, d)
                    for d in range(2):
                        c = chunk_of(step, d)
                        k = step % 2
                        BaT, Bpo = P.B("haT", d, k), P.B("hpo", d)
                        slot = (step % 8) if d == 0 else 7 - (step % 8)
                        P.op("pe", lambda e, d=d, k=k, c=c, slot=slot: e.matmul(
                            po[d][:, slot * 64:(slot + 1) * 64], lhsT=Vt[:, c, :], rhs=aT[d * 2 + k][:],
                            start=True, stop=False), [P.B("hV"), BaT], [Bpo])
                    for d in range(2):
                        c = chunk_of(step, d)
                        Bpo, Bps = P.B("hpo", d), P.B("hps", d)
                        BS, BSb = P.B("hS", d), P.B("hSb", d)
                        cs = slice(c * 64, (c + 1) * 64)
                        slot = (step % 8) if d == 0 else 7 - (step % 8)
                        P.op("pe", lambda e, d=d, cs=cs, slot=slot: e.matmul(
                            po[d][:, slot * 64:(slot + 1) * 64], lhsT=Sb[d][:], rhs=Qh[d][:, cs],
                            start=False, stop=True), [BSb, P.B("hQh", d)], [Bpo])
                        P.op("pe", lambda e, d=d, c=c: e.matmul(pss[d][:], lhsT=Kh[d][:, c, :], rhs=Vt[:, c, :],
                                                                start=True, stop=True),
                             [P.B("hKh", d), P.B("hV")], [Bps])
                        P.op("dve", lambda e, d=d, c=c: e.scalar_tensor_tensor(out=St[d][:], in0=St[d][:],
                                                                               scalar=dec[d][:, c:c + 1], in1=pss[d][:],
                                                                               op0=ALU.mult, op1=ALU.add),
                             [BS, Bps, P.B("hdec", d)], [BS])
                        P.op("act", lambda e, d=d: e.copy(out=Sb[d][:], in_=St[d][:]), [BS], [BSb])
                        if step % 8 == 7:
                            g8 = step // 8
                            t0 = g8 * 512 if d == 0 else (7 - g8) * 512
                            P.op("act", lambda e, d=d, t0=t0: e.copy(out=osb[d][:, t0:t0 + 512], in_=po[d][:]),
                                 [Bpo], [BO[d]])
                P.op("dve", lambda e: e.tensor_tensor(out=tA[:], in0=tA[:], in1=tB[:], op=ALU.add), [BA, BB], [BA])
                P.op("act", lambda e: e.activation(out=ob16[:], in_=tA[:], func=AF.Square), [BA], [P.B("hQt", 0)])
                P.dma("sp", zin[:], zr[s, 3, f0:f0 + 128, :], [], [Bz], "hz")
                P.op("act", lambda e: e.activation(out=tB[:], in_=zin[:], func=AF.Silu), [Bz, BB], [BB])
                Bpn, Brs = P.B("hpn"), P.B("hrstd")
                for blk in range(8):
                    bs = slice(blk * 512, (blk + 1) * 512)
                    P.op("pe", lambda e, bs=bs: e.matmul(pn[:], lhsT=ones_b[:], rhs=ob16[:, bs], start=True, stop=True),
                         [P.B("hQt", 0)], [Bpn])
                    P.op("dve", lambda e: e.tensor_scalar(out=rstd[:], in0=pn[:], scalar1=1.0 / 128, scalar2=EPS,
                                                          op0=ALU.mult, op1=ALU.add), [Bpn], [Brs])
                    P.op("act", lambda e: e.activation(out=rstd[:], in_=rstd[:], func=AF.Sqrt), [Brs], [Brs])
                    P.op("dve", lambda e: e.reciprocal(out=rstd[:], in_=rstd[:]), [Brs], [Brs])
                    P.op("dve", lambda e, bs=bs: e.tensor_tensor(out=tA[:, bs], in0=tA[:, bs], in1=rstd[:], op=ALU.mult),
                         [BA, Brs], [BA])
                P.op("dve", lambda e: e.scalar_tensor_tensor(out=Khf[:], in0=tA[:], scalar=par[:, 6:7], in1=tB[:],
                                                             op0=ALU.mult, op1=ALU.mult), [BA, BB, Bpar], [P.B("hKhf")])
                P.dma("sp", mixT[s, f0:f0 + 128, :], Khf[:], [P.B("hKhf")], [], "hout")
        P.barrier()
        P.emit()


def mixer_rglru(P, nc, spc, idx, zr, mixT, conv_w, conv_b, rg_wa, rg_ba, rg_wx, rg_bx, rg_lambda):
    with ExitStack() as ph:
        def sb(name, shape, dt):
            return ph.enter_context(nc.sbuf_tensor(_uniq(name), list(shape), dt))

        def pst(name, shape, dt):
            return ph.enter_context(nc.psum_tensor(_uniq(name), list(shape), dt))

        xp = sb("rxp", [128, T + 4], F32)
        yy = sb("ry", [128, T], F32)
        yb = sb("ryb", [128, T], BF16)
        rr = sb("rr", [128, T], F32)
        ii = sb("ri", [128, T], F32)
        a1 = sb("ra", [128, T], F32)
        t1 = sb("rt", [128, T], F32)
        hh = [sb("rh%d" % d, [128, T], F32) for d in range(2)]
        ob = sb("rob", [128, T], BF16)
        wbd32 = sb("rw32", [128, 4, 128], F32)
        wbd = sb("rwbd", [128, 4, 128], BF16)
        par = sb("rpar", [128, 24], F32)
        pm = [pst("rpm%d" % i, [128, 512], F32) for i in range(4)]
        Bxp, By, Byb, Br, Bi, Ba, Bt, Bob = (P.B("rxp"), P.B("ry"), P.B("ryb"), P.B("rr"), P.B("ri"), P.B("ra"),
                                             P.B("rt"), P.B("rob"))
        Bh = [P.B("rh", d) for d in range(2)]
        Bw, Bpar = P.B("rw"), P.B("rpar")
        P.op("pool", lambda e: e.memset(xp[:], 0.0), [], [Bxp])
        col = lambda ap: ap.rearrange("(p o) -> p o", o=1)
        ec = 0
        for ch in range(4):
            c0 = ch * 128
            for k in range(4):
                P.dma("sp", par[:, k:k + 1], col(conv_w[idx, k, c0:c0 + 128]), [], [Bpar], "rp0")
            P.dma("sp", par[:, 4:5], col(conv_b[idx, c0:c0 + 128]), [], [Bpar], "rp1")
            for d in range(2):
                P.dma("sp", par[:, 5 + d:6 + d], col(rg_ba[idx, d, c0:c0 + 128]), [], [Bpar], "rp2")
                P.dma("sp", par[:, 7 + d:8 + d], col(rg_bx[idx, d, c0:c0 + 128]), [], [Bpar], "rp3")
                P.dma("sp", par[:, 9 + d:10 + d], col(rg_lambda[idx, d, c0:c0 + 128]), [], [Bpar], "rp4")
            P.op("act", lambda e: e.activation(out=par[:, 9:11], in_=par[:, 9:11], func=AF.Exp, scale=-1.0), [Bpar], [Bpar])
            P.op("dve", lambda e: e.tensor_scalar(out=par[:, 9:11], in0=par[:, 9:11], scalar1=1.0, scalar2=None, op0=ALU.add),
                 [Bpar], [Bpar])
            P.op("act", lambda e: e.activation(out=par[:, 9:11], in_=par[:, 9:11], func=AF.Ln), [Bpar], [Bpar])
            P.op("dve", lambda e: e.tensor_scalar(out=par[:, 11:13], in0=par[:, 9:11], scalar1=-16.0, scalar2=None,
                                                  op0=ALU.mult), [Bpar], [Bpar])
            P.op("dve", lambda e: e.tensor_scalar(out=par[:, 9:11], in0=par[:, 9:11], scalar1=-8.0, scalar2=None,
                                                  op0=ALU.mult), [Bpar], [Bpar])
            P.op("pool", lambda e: e.memset(wbd32[:], 0.0), [], [Bw])
            for d in range(2):
                for m, wsrc in enumerate((rg_wa, rg_wx)):
                    for blk in range(2):
                        P.dma("sp", wbd32[blk * 64:(blk + 1) * 64, d * 2 + m, blk * 64:(blk + 1) * 64],
                              wsrc[idx, d, ch * 2 + blk], [], [Bw], "rw")
            P.op("dve", lambda e: e.tensor_copy(out=wbd[:], in_=wbd32[:]), [Bw], [Bw])
            for s in range(spc):
                P.dma("sp", xp[:, 2:T + 2], zr[s, 1, c0:c0 + 128, :], [], [Bxp], "rx")
                P.op("dve", lambda e: e.tensor_scalar(out=yy[:], in0=xp[:, 0:T], scalar1=par[:, 0:1], scalar2=par[:, 4:5],
                                                      op0=ALU.mult, op1=ALU.add), [Bxp, Bpar], [By])
                for k in range(1, 4):
                    P.op("dve", lambda e, k=k: e.scalar_tensor_tensor(out=yy[:], in0=xp[:, k:k + T], scalar=par[:, k:k + 1],
                                                                      in1=yy[:], op0=ALU.mult, op1=ALU.add),
                         [Bxp, Bpar, By], [By])
                P.op("act", lambda e: e.copy(out=yb[:], in_=yy[:]), [By], [Byb])
                for d in range(2):
                    for blk in range(8):
                        bs = slice(blk * 512, (blk + 1) * 512)
                        for m, (dst, Bd, bc) in enumerate(((rr, Br, 5 + d), (ii, Bi, 7 + d))):
                            pb = ec % 4
                            ec += 1
                            Bpm = P.B("rpm", pb)
                            P.op("pe", lambda e, pb=pb, d=d, m=m, bs=bs: e.matmul(pm[pb][:], lhsT=wbd[:, d * 2 + m, :],
                                                                                  rhs=yb[:, bs], start=True, stop=True),
                                 [Bw, Byb], [Bpm])
                            P.op("act", lambda e, pb=pb, dst=dst, bs=bs, bc=bc: e.activation(
                                out=dst[:, bs], in_=pm[pb][:], func=AF.Sigmoid, bias=par[:, bc:bc + 1]),
                                [Bpm, Bpar], [Bd])
                    P.op("act", lambda e, d=d: e.activation(out=a1[:], in_=rr[:], func=AF.Exp, scale=par[:, 9 + d:10 + d]),
                         [Br, Bpar], [Ba])
                    P.op("act", lambda e, d=d: e.activation(out=t1[:], in_=rr[:], func=AF.Exp, scale=par[:, 11 + d:12 + d]),
                         [Br, Bpar], [Bt])
                    P.op("dve", lambda e: e.tensor_scalar(out=t1[:], in0=t1[:], scalar1=-1.0, scalar2=1.0,
                                                          op0=ALU.mult, op1=ALU.add), [Bt], [Bt])
                    P.op("act", lambda e: e.activation(out=t1[:], in_=t1[:], func=AF.Sqrt), [Bt], [Bt])
                    first = 0 if d == 0 else T - 1
                    P.op("pool", lambda e, first=first: e.memset(t1[:, first:first + 1], 1.0), [Bt], [Bt])
                    P.op("dve", lambda e: e.tensor_tensor(out=t1[:], in0=t1[:], in1=ii[:], op=ALU.mult), [Bt, Bi], [Bt])
                    P.op("dve", lambda e: e.tensor_tensor(out=t1[:], in0=t1[:], in1=yy[:], op=ALU.mult), [Bt, By], [Bt])
                    if d == 0:
                        P.op("dve", lambda e: e.tensor_tensor_scan(out=hh[0][:], data0=a1[:], data1=t1[:], initial=0.0,
                                                                   op0=ALU.mult, op1=ALU.add), [Ba, Bt], [Bh[0]])
                    else:
                        P.op("dve", lambda e: e.tensor_tensor_scan(out=hh[1][:, ::-1], data0=a1[:, ::-1],
                                                                   data1=t1[:, ::-1], initial=0.0,
                                                                   op0=ALU.mult, op1=ALU.add), [Ba, Bt], [Bh[1]])
                P.dma("sp", rr[:], zr[s, 0, c0:c0 + 128, :], [Br], [Br], "rg")
                P.op("act", lambda e: e.activation(out=ii[:], in_=rr[:], func=AF.Gelu_apprx_tanh), [Br, Bi], [Bi])
                P.op("dve", lambda e: e.tensor_tensor(out=hh[0][:], in0=hh[0][:], in1=hh[1][:], op=ALU.add),
                     [Bh[0], Bh[1]], [Bh[0]])
                P.op("dve", lambda e: e.tensor_tensor(out=ob[:], in0=hh[0][:], in1=ii[:], op=ALU.mult), [Bh[0], Bi], [Bob])
                P.dma("sp", mixT[s, c0:c0 + 128, :], ob[:], [Bob], [], "rout")
        P.barrier()
        P.emit()


def host_tables(na_rpb, t5_bias):
    hidx = np.arange(8)[None, :, None, None]
    na_tab = np.ascontiguousarray(
        np.stack([na_rpb[l][hidx, NA_DR[:, None], NA_DC[:, None]] for l in range(2)]).astype(np.float32))
    t5_tab = np.ascontiguousarray(np.transpose(t5_bias[T5_BK], (0, 3, 1, 2)).astype(np.float32))
    t = np.arange(T)
    consts = {
        "na_mask": NA_MASK, "t5_cnt": T5_CNT,
        "cmask_f": (t % 64 != 0).astype(np.float32)[None, :],
        "cmask_b": (t % 64 != 63).astype(np.float32)[None, :],
        "tri_f": (np.arange(64)[:, None] <= np.arange(64)[None, :]).astype(np.float32),
        "tri_b": (np.arange(64)[:, None] >= np.arange(64)[None, :]).astype(np.float32),
    }
    return na_tab, t5_tab, consts


_NC_CACHE = {}


def run(inputs, xs_per_core, layers=(0, 1, 2, 3), spc=SPC, debug=False, ncores=NCORES, trace=False):
    key = (tuple(layers), spc, debug)
    if key not in _NC_CACHE:
        _NC_CACHE[key] = build(layers, spc, debug)
    nc = _NC_CACHE[key]
    na_tab, t5_tab, consts = host_tables(np.asarray(inputs["na_rpb"]), np.asarray(inputs["t5_bias"]))
    shared = {k: np.ascontiguousarray(np.asarray(inputs[k], dtype=np.float32)) for k in (
        "norm_g", "w_in_even", "w_out_even", "w_in_odd", "w_out_odd", "w_gate", "w_up", "w_down", "hgrn_lb",
        "hgrn_onorm", "conv_w", "conv_b", "rg_wa", "rg_ba", "rg_wx", "rg_bx", "rg_lambda")}
    shared["na_tab"] = na_tab
    shared["t5_tab"] = t5_tab
    shared.update(consts)
    in_maps = []
    for c in range(ncores):
        m = dict(shared)
        m["x"] = xs_per_core[c]
        in_maps.append(m)
    res = run_bass_kernel_spmd(nc, in_maps, core_ids=list(range(ncores)), **({"trace": True} if trace else {}))
    return res


def kernel(**inputs):
    xp = np.asarray(inputs["x_prompt"], dtype=np.float32)
    xs = np.asarray(inputs["x_sample"], dtype=np.float32)
    allx = np.concatenate([xp, xs], axis=0)
    nseq = allx.shape[0]
    slots = np.zeros((NCORES * SPC, T, D), np.float32)
    slots[:nseq] = allx
    xs_per_core = [np.ascontiguousarray(slots[c * SPC:(c + 1) * SPC]) for c in range(NCORES)]
    res = run(inputs, xs_per_core)
    yall = np.concatenate([r["y"] for r in res.results], axis=0)[:nseq]
    return (np.ascontiguousarray(yall[:xp.shape[0]]), np.ascontiguousarray(yall[xp.shape[0]:]))
```

```python
import numpy as np
from contextlib import ExitStack
import concourse.bass as bass
import concourse.mybir as mybir
from concourse.bass_utils import run_bass_kernel_spmd

F32 = mybir.dt.float32
BF16 = mybir.dt.bfloat16
AF = mybir.ActivationFunctionType
ALU = mybir.AluOpType

T = 4096
D = 1024
NT = 32
DFF = 2816
EPS = 1e-6
SPC = 3
NCORES = 8
NEG = -30000.0


class Buf:
    __slots__ = ("w", "r")

    def __init__(self):
        self.w = None
        self.r = []


class Prog:
    def __init__(self, nc, stack):
        self.nc = nc
        self.stack = stack
        self.names = ("pe", "act", "dve", "pool", "sp")
        self.lists = {k: [] for k in self.names}
        self.esem = {}
        self.ecnt = {k: 0 for k in self.names}
        for k in ("pe", "act", "dve", "pool"):
            self.esem[k] = stack.enter_context(nc.semaphore("s_" + k))
        self.seen = {k: {} for k in self.names}
        self.bufs = {}
        self.chans = {}
        self.ninst = 0

    def B(self, *key):
        b = self.bufs.get(key)
        if b is None:
            b = self.bufs[key] = Buf()
        return b

    def chan(self, name):
        c = self.chans.get(name)
        if c is None:
            c = self.chans[name] = [self.stack.enter_context(self.nc.semaphore("c_" + name)), 0]
        return c

    def _deps(self, e, reads, writes):
        evs = []
        for b in reads:
            if b.w is not None:
                evs.append(b.w)
        for b in writes:
            if b.w is not None:
                evs.append(b.w)
            evs.extend(b.r)
        waits = {}
        seen = self.seen[e]
        for (sem, val, src) in evs:
            if src == "pe" and e == "pe":
                continue
            k = id(sem)
            if seen.get(k, 0) >= val:
                continue
            if k not in waits or waits[k][1] < val:
                waits[k] = (sem, val)
        for k, (sem, val) in waits.items():
            seen[k] = val
        return list(waits.values())

    def _commit(self, ev, reads, writes):
        for b in reads:
            b.r.append(ev)
        for b in writes:
            b.w = ev
            b.r = []

    def op(self, e, fn, reads=(), writes=()):
        waits = self._deps(e, reads, writes)
        self.ecnt[e] += 1
        ev = (self.esem[e], self.ecnt[e], e)
        self.lists[e].append((waits, fn, self.esem[e], 1))
        self._commit(ev, reads, writes)
        self.ninst += 1

    def dma(self, q, out, in_, reads, writes, chan, **kw):
        c = self.chan(chan)
        waits = self._deps(q, reads, writes)
        if c[1] > 0:
            k = id(c[0])
            if self.seen[q].get(k, 0) < c[1]:
                waits.append((c[0], c[1]))
                self.seen[q][k] = c[1]
        c[1] += 16
        ev = (c[0], c[1], "dma")
        self.lists[q].append((waits, lambda eng: eng.dma_start(out=out, in_=in_, **kw), c[0], 16))
        self._commit(ev, reads, writes)
        self.ninst += 1

    def barrier(self):
        for e in self.names:
            seen = self.seen[e]
            waits = []
            for k in ("pe", "act", "dve", "pool"):
                if k == e:
                    continue
                s, v = self.esem[k], self.ecnt[k]
                if v > 0 and seen.get(id(s), 0) < v:
                    waits.append((s, v))
                    seen[id(s)] = v
            for name, (s, v) in self.chans.items():
                if v > 0 and seen.get(id(s), 0) < v:
                    waits.append((s, v))
                    seen[id(s)] = v
            if waits:
                self.lists[e].append((waits, None, None, 0))
        for b in self.bufs.values():
            b.w = None
            b.r = []

    def emit(self):
        nc = self.nc
        lists = self.lists
        with nc.Block() as block:
            def mk(name):
                def body(eng):
                    for (waits, fn, sem, inc) in lists[name]:
                        for (s, v) in waits:
                            eng.wait_ge(s, v)
                        if fn is not None:
                            fn(eng).then_inc(sem, inc)
                return body
            block.tensor(mk("pe"))
            block.scalar(mk("act"))
            block.vector(mk("dve"))
            block.gpsimd(mk("pool"))
            block.sync(mk("sp"))
        self.lists = {k: [] for k in self.names}


def na_patterns():
    kk = np.arange(128)
    rk_l, ck = kk // 64, kk % 64
    pats = {}
    pairs = []
    drs, dcs, masks = [], [], []
    for j in range(NT):
        lst = []
        for i in range(NT):
            rq = 2 * j + rk_l[None, :]
            cq = ck[None, :]
            rk = 2 * i + rk_l[:, None]
            ckk = ck[:, None]
            start = np.clip(rq - 4, 0, 56)
            visr = (rk >= start) & (rk < start + 8)
            cs = np.clip(cq - 8, 0, 48)
            visc = (ckk >= cs) & (ckk < cs + 16)
            vis = visr & visc
            if not vis.any():
                continue
            dr = np.clip(rk - rq + 7, 0, 14) + 0 * cq
            dc = np.clip(ckk - cq + 15, 0, 30) + 0 * rq
            dr = np.where(vis, dr, 0)
            dc = np.where(vis, dc, 0)
            key = (vis.tobytes(), dr.astype(np.int8).tobytes(), dc.astype(np.int8).tobytes())
            if key not in pats:
                pats[key] = len(pats)
                drs.append(dr)
                dcs.append(dc)
                masks.append(np.where(vis, 0.0, NEG).astype(np.float32))
            lst.append((i, pats[key]))
        pairs.append(lst)
    return pairs, np.stack(drs), np.stack(dcs), np.stack(masks)


def t5_patterns():
    kk = np.arange(128)
    bks, cnts = [], []
    for dlt in range(-8, 9):
        d = 128 * dlt + kk[:, None] - kk[None, :]
        n = np.abs(d)
        cnt = (n <= 64).astype(np.int32) + ((d % 4 == 0) & (n <= 256)) + ((d % 16 == 0) & (n <= 1024))
        large = 8 + (np.log(np.maximum(n, 1).astype(np.float32) / np.float32(8)) / np.float32(np.log(1024 / 8)) * 8).astype(np.int32)
        large = np.minimum(large, 15)
        bk = np.where(d > 0, 16, 0) + np.where(n < 8, n, large)
        bks.append(bk)
        cnts.append(np.where(cnt > 0, np.log(np.maximum(cnt, 1)), NEG).astype(np.float32))
    return np.stack(bks), np.stack(cnts)


_UID = [0]


def _uniq(name):
    _UID[0] += 1
    return "%s_u%d" % (name, _UID[0])


NA_PAIRS, NA_DR, NA_DC, NA_MASK = na_patterns()
NPAT = NA_MASK.shape[0]
T5_BK, T5_CNT = t5_patterns()


def build(layers=(0, 1, 2, 3), spc=SPC, debug=False):
    nc = bass.Bass("TRN2", target_bir_lowering=False)

    def din(name, shape, dt=F32):
        return nc.dram_tensor(name, list(shape), dt, kind="ExternalInput").ap()

    def dscr(name, shape, dt):
        return nc.dram_tensor(name, list(shape), dt, kind="ExternalOutput" if debug else "Internal").ap()

    x_in = din("x", [spc, T, D])
    norm_g = din("norm_g", [4, 4, D])
    w_in_even = din("w_in_even", [2, D, 4096])
    w_out_even = din("w_out_even", [2, D, D])
    w_in_odd = din("w_in_odd", [2, D, 2560])
    w_out_odd = din("w_out_odd", [2, D, D])
    w_gate = din("w_gate", [4, D, DFF])
    w_up = din("w_up", [4, D, DFF])
    w_down = din("w_down", [4, DFF, D])
    hgrn_lb = din("hgrn_lb", [2, 2, 512])
    hgrn_onorm = din("hgrn_onorm", [2, 128])
    conv_w = din("conv_w", [2, 4, 512])
    conv_b = din("conv_b", [2, 512])
    rg_wa = din("rg_wa", [2, 2, 8, 64, 64])
    rg_ba = din("rg_ba", [2, 2, 512])
    rg_wx = din("rg_wx", [2, 2, 8, 64, 64])
    rg_bx = din("rg_bx", [2, 2, 512])
    rg_lambda = din("rg_lambda", [2, 2, 512])
    na_tab = din("na_tab", [2, NPAT, 8, 128, 128])
    na_mask = din("na_mask", [NPAT, 128, 128])
    t5_tab = din("t5_tab", [17, 8, 128, 128])
    t5_cnt = din("t5_cnt", [17, 128, 128])
    cmask_f = din("cmask_f", [1, T])
    cmask_b = din("cmask_b", [1, T])
    tri_f = din("tri_f", [64, 64])
    tri_b = din("tri_b", [64, 64])
    y_out = nc.dram_tensor("y", [spc, T, D], F32, kind="ExternalOutput").ap()

    xres = [dscr("xres0", [spc, T, D], F32), dscr("xres1", [spc, T, D], F32)]
    zqT = dscr("zqT", [spc, 512, T], BF16)
    zkT = dscr("zkT", [spc, 512, T], BF16)
    zv = dscr("zv", [spc, T, 512], BF16)
    zr = dscr("zr", [spc, 4, 512, T], F32)
    zi = dscr("zi", [spc, T, 512], F32)
    mix = dscr("mix", [spc, T, 512], BF16)
    mixT = dscr("mixT", [spc, 512, T], BF16)

    with ExitStack() as glob:
        P = Prog(nc, glob)

        def sb(stk, name, shape, dt):
            return stk.enter_context(nc.sbuf_tensor(_uniq(name), list(shape), dt))

        def pst(stk, name, shape, dt):
            return stk.enter_context(nc.psum_tensor(_uniq(name), list(shape), dt))

        identf = sb(glob, "identf", [128, 128], F32)
        ident = sb(glob, "ident", [128, 128], BF16)
        ones_b = sb(glob, "ones_b", [128, 128], BF16)
        Bid = P.B("ident")
        P.op("pool", lambda e: e.memset(identf[:], 0.0), [], [Bid])
        P.op("pool", lambda e: e.affine_select(out=identf[:], in_=identf[:], pattern=[[-1, 128]],
                                                compare_op=ALU.not_equal, fill=1.0, base=0,
                                                channel_multiplier=1), [Bid], [Bid])
        P.op("dve", lambda e: e.tensor_copy(out=ident[:], in_=identf[:]), [Bid], [Bid])
        P.op("pool", lambda e: e.memset(ones_b[:], 1.0), [], [Bid])
        P.barrier()
        P.emit()

        nlay = len(layers)
        for li, L in enumerate(layers):
            even = (L % 2 == 0)
            idx = L // 2
            xsrc = x_in if li == 0 else xres[(li - 1) % 2]
            xdst = y_out if li == nlay - 1 else xres[li % 2]
            with ExitStack() as ph:
                ncol = 4096 if even else 2560
                w_in = (w_in_even if even else w_in_odd)[idx]
                win = sb(ph, "win", [128, 8, ncol], BF16)
                g0 = sb(ph, "g0", [128, D], F32)
                Bw = P.B("win")
                wsrc = w_in.rearrange("(c p) n -> p c n", p=128)
                for dc in range(8):
                    P.dma("pool", win[:, dc, :], wsrc[:, dc, :], [], [Bw], "w%d" % dc)
                P.dma("sp", g0[:], norm_g[L, 0].partition_broadcast(128), [], [Bw], "gA")
                xt = [sb(ph, "xtA%d" % i, [128, D], F32) for i in range(4)]
                junk = sb(ph, "junkA", [128, D], F32)
                hb = [sb(ph, "hbA%d" % i, [128, D], BF16) for i in range(4)]
                hT = [sb(ph, "hTA%d" % i, [128, 8, 512], BF16) for i in range(2)]
                ss = sb(ph, "ssA", [128, 8], F32)
                stF = [sb(ph, "stF%d" % i, [128, 4, 512], F32) for i in range(4)]
                stH = [sb(ph, "stH%d" % i, [128, 4, 512], BF16) for i in range(4)]
                pT = [pst(ph, "pTA%d" % i, [128, 1024], BF16) for i in range(2)]
                pm = [pst(ph, "pmA%d" % i, [128, 512], F32) for i in range(4)]
                if even:
                    fm_parts = [(0, "qk", zqT, 0.125), (512, "qk", zkT, 1.0), (1536, "r", 0, 1.0),
                                (2048, "r", 1, 1.0), (2560, "r", 2, 1.0), (3584, "r", 3, 1.0)]
                    tm_parts = [(1024, "v"), (3072, "i")]
                else:
                    fm_parts = [(0, "r", 0, 1.0), (512, "r", 1, 1.0), (1024, "qk", zqT, 0.125),
                                (1536, "qk", zkT, 1.0)]
                    tm_parts = [(2048, "v")]
                hb = hb + [sb(ph, "hbA%d" % i, [128, D], BF16) for i in range(4, 8)]
                groups = [(s_, g_) for s_ in range(spc) for g_ in range(8)]
                ecnt_box = [0]
                scnt = {"F": 0, "H": 0}

                def stageN(gi):
                    s, g = groups[gi]
                    for i in range(4):
                        cnt = gi * 4 + i
                        b = cnt % 4
                        hbi = (gi % 2) * 4 + i
                        tok0 = g * 512 + i * 128
                        Bx, Bh, Bss = P.B("xtA", b), P.B("hbA", hbi), P.B("ssA", b)
                        Bj = P.B("junkA")
                        P.dma("sp", xt[b][:], xsrc[s, tok0:tok0 + 128, :], [], [Bx], "xA%d" % b)
                        sc = ss[:, 2 * b:2 * b + 1]
                        rs = ss[:, 2 * b + 1:2 * b + 2]
                        P.op("pool", lambda e, sc=sc: e.memset(sc, 0.0), [], [Bss])
                        P.op("act", lambda e, b=b, sc=sc: e.activation(out=junk[:], in_=xt[b][:], func=AF.Square,
                                                                       accum_out=sc), [Bx, Bss], [Bj, Bss])
                        P.op("dve", lambda e, sc=sc, rs=rs: e.tensor_scalar(out=rs, in0=sc, scalar1=1.0 / D, scalar2=EPS,
                                                                           op0=ALU.mult, op1=ALU.add), [Bss], [Bss])
                        P.op("act", lambda e, rs=rs: e.activation(out=rs, in_=rs, func=AF.Sqrt), [Bss], [Bss])
                        P.op("dve", lambda e, rs=rs: e.reciprocal(out=rs, in_=rs), [Bss], [Bss])
                        P.op("dve", lambda e, b=b, hbi=hbi, rs=rs: e.scalar_tensor_tensor(
                            out=hb[hbi][:], in0=xt[b][:], scalar=rs, in1=g0[:], op0=ALU.mult, op1=ALU.mult),
                            [Bx, Bss, Bw], [Bh])

                def stageM(gi):
                    s, g = groups[gi]
                    hTg = hT[gi % 2]
                    BhT = P.B("hTA", gi % 2)
                    for i in range(4):
                        cnt = gi * 4 + i
                        pb2 = cnt % 2
                        hbi = (gi % 2) * 4 + i
                        Bh, BpT = P.B("hbA", hbi), P.B("pTA", pb2)
                        for dc in range(8):
                            P.op("pe", lambda e, hbi=hbi, pb2=pb2, dc=dc: e.transpose(
                                pT[pb2][:, dc * 128:(dc + 1) * 128], hb[hbi][:, dc * 128:(dc + 1) * 128], ident[:]),
                                [Bh], [BpT])
                        P.op("act", lambda e, pb2=pb2, i=i, hTg=hTg: e.copy(
                            out=hTg[:, :, i * 128:(i + 1) * 128],
                            in_=pT[pb2][:].rearrange("p (c t) -> p c t", c=8)), [BpT], [BhT])
                    ecnt = ecnt_box[0]
                    for (c0, kind, dst, scale) in fm_parts:
                        if kind == "qk":
                            k = scnt["H"] % 4
                            scnt["H"] += 1
                            st, Bst, chn = stH[k], P.B("stH", k), "sH%d" % k
                            dst_ap = dst[s].rearrange("(c p) t -> p c t", p=128)[:, :, g * 512:(g + 1) * 512]
                        else:
                            k = scnt["F"] % 4
                            scnt["F"] += 1
                            st, Bst, chn = stF[k], P.B("stF", k), "sF%d" % k
                            dst_ap = zr[s, dst].rearrange("(c p) t -> p c t", p=128)[:, :, g * 512:(g + 1) * 512]
                        for fc in range(4):
                            pb = ecnt % 4
                            Bpm = P.B("pmA", pb)
                            col = c0 + fc * 128
                            for dc in range(8):
                                P.op("pe", lambda e, pb=pb, dc=dc, col=col, hTg=hTg: e.matmul(
                                    pm[pb][:], lhsT=win[:, dc, col:col + 128], rhs=hTg[:, dc, :],
                                    start=(dc == 0), stop=(dc == 7)), [Bw, BhT], [Bpm])
                            if ecnt % 2 == 0:
                                P.op("act", lambda e, st=st, fc=fc, pb=pb, scale=scale: e.activation(
                                    out=st[:, fc, :], in_=pm[pb][:], func=AF.Copy, scale=scale), [Bpm], [Bst])
                            else:
                                P.op("dve", lambda e, st=st, fc=fc, pb=pb, scale=scale: e.tensor_scalar(
                                    out=st[:, fc, :], in0=pm[pb][:], scalar1=scale, scalar2=None, op0=ALU.mult),
                                    [Bpm], [Bst])
                            ecnt += 1
                        P.dma("sp", dst_ap, st[:], [Bst], [], chn)
                    for (c0, kind) in tm_parts:
                        if kind == "v":
                            k = scnt["H"] % 4
                            scnt["H"] += 1
                            st, Bst, chn = stH[k], P.B("stH", k), "sH%d" % k
                            dst_ap = zv[s, g * 512:(g + 1) * 512, :].rearrange("(i p) n -> p i n", p=128)
                        else:
                            k = scnt["F"] % 4
                            scnt["F"] += 1
                            st, Bst, chn = stF[k], P.B("stF", k), "sF%d" % k
                            dst_ap = zi[s, g * 512:(g + 1) * 512, :].rearrange("(i p) n -> p i n", p=128)
                        for i in range(4):
                            pb = ecnt % 4
                            Bpm = P.B("pmA", pb)
                            for dc in range(8):
                                P.op("pe", lambda e, pb=pb, dc=dc, i=i, c0=c0, hTg=hTg: e.matmul(
                                    pm[pb][:], lhsT=hTg[:, dc, i * 128:(i + 1) * 128], rhs=win[:, dc, c0:c0 + 512],
                                    start=(dc == 0), stop=(dc == 7)), [Bw, BhT], [Bpm])
                            if ecnt % 2 == 0:
                                P.op("act", lambda e, st=st, i=i, pb=pb: e.copy(out=st[:, i, :], in_=pm[pb][:]),
                                     [Bpm], [Bst])
                            else:
                                P.op("dve", lambda e, st=st, i=i, pb=pb: e.tensor_copy(out=st[:, i, :], in_=pm[pb][:]),
                                     [Bpm], [Bst])
                            ecnt += 1
                        P.dma("sp", dst_ap, st[:], [Bst], [], chn)
                    ecnt_box[0] = ecnt

                stageN(0)
                for gi in range(len(groups)):
                    if gi + 1 < len(groups):
                        stageN(gi + 1)
                    stageM(gi)
                P.barrier()
                P.emit()

            if even:
                mixer_attention(P, nc, spc, "na", idx, zqT, zkT, zv, mix, na_tab, na_mask, ident)
                mixer_hgrn2(P, nc, spc, idx, zr, zi, mixT, hgrn_lb, hgrn_onorm, cmask_f, cmask_b, tri_f, tri_b,
                            ident, ones_b)
            else:
                mixer_attention(P, nc, spc, "dil", idx, zqT, zkT, zv, mix, t5_tab, t5_cnt, ident)
                mixer_rglru(P, nc, spc, idx, zr, mixT, conv_w, conv_b, rg_wa, rg_ba, rg_wx, rg_bx, rg_lambda)

            with ExitStack() as ph:
                w_out = (w_out_even if even else w_out_odd)[idx]
                wout = sb(ph, "wout", [128, 8, D], BF16)
                wg = sb(ph, "wg", [128, 8, DFF], BF16)
                wu = sb(ph, "wu", [128, 8, DFF], BF16)
                wd = sb(ph, "wd", [128, 22, D], BF16)
                gam = sb(ph, "gamC", [128, 3, D], F32)
                Bw = P.B("wC")
                for dc in range(8):
                    P.dma("pool", wout[:, dc, :], w_out.rearrange("(c p) n -> p c n", p=128)[:, dc, :], [], [Bw], "w%d" % dc)
                for dc in range(8):
                    P.dma("pool", wg[:, dc, :], w_gate[L].rearrange("(c p) n -> p c n", p=128)[:, dc, :], [], [Bw], "w%d" % dc)
                    P.dma("pool", wu[:, dc, :], w_up[L].rearrange("(c p) n -> p c n", p=128)[:, dc, :], [], [Bw],
                          "w%d" % ((dc + 4) % 8))
                for fc in range(22):
                    P.dma("pool", wd[:, fc, :], w_down[L].rearrange("(c p) n -> p c n", p=128)[:, fc, :], [], [Bw],
                          "w%d" % (fc % 8))
                for k in range(3):
                    P.dma("sp", gam[:, k, :], norm_g[L, k + 1].partition_broadcast(128), [], [Bw], "gA")
                NX = 3
                xt = [sb(ph, "xtC%d" % i, [128, D], F32) for i in range(2)]
                xn = [sb(ph, "xnC%d" % i, [128, D], F32) for i in range(NX)]
                mtok = [sb(ph, "mtokC%d" % i, [128, 512], BF16) for i in range(2)]
                mfm = [sb(ph, "mfmC%d" % i, [128, 4, 128], BF16) for i in range(2)]
                matt = sb(ph, "mattC", [128, 4, 128], BF16)
                junk = sb(ph, "junkC", [128, D], BF16)
                tmp = [sb(ph, "tmpC%d" % i, [128, 512], F32) for i in range(2)]
                h2 = sb(ph, "h2C", [128, D], BF16)
                h2T = sb(ph, "h2TC", [128, 8, 128], BF16)
                aa = sb(ph, "aC", [128, DFF], BF16)
                aT = sb(ph, "aTC", [128, 22, 128], BF16)
                ss = sb(ph, "ssC", [128, 8 * NX], F32)
                epst = sb(ph, "epsC", [128, 1], F32)
                pT = [pst(ph, "pTC%d" % i, [128, 1024], BF16) for i in range(2)]
                pout = [pst(ph, "poC%d" % i, [128, 512], F32) for i in range(2)]
                pdn = [pst(ph, "pdC%d" % i, [128, 512], F32) for i in range(2)]
                pgu = [pst(ph, "pguC%d" % i, [128, 512], F32) for i in range(2)]
                Bj, Bh2, Bh2T, Ba, BaT, Bmatt = (P.B("junkC"), P.B("h2C"), P.B("h2TC"), P.B("aC"), P.B("aTC"),
                                                 P.B("mattC"))
                BpT = [P.B("pTC", i) for i in range(2)]
                Bpo = [P.B("poC", i) for i in range(2)]
                Bpd = [P.B("pdC", i) for i in range(2)]
                Bpg = [P.B("pguC", i) for i in range(2)]
                Btmp = [P.B("tmpC", i) for i in range(2)]
                Beps = P.B("epsC")
                P.op("pool", lambda e: e.memset(epst[:], EPS), [], [Beps])
                ptc = [0]
                tiles = [(s, j) for s in range(spc) for j in range(NT)]

                def rstd_chain(Bss, sq0, sq1, out_col):
                    if sq1 is not None:
                        P.op("dve", lambda e: e.tensor_tensor(out=sq0, in0=sq0, in1=sq1, op=ALU.add), [Bss], [Bss])
                    P.op("act", lambda e: e.activation(out=out_col, in_=sq0, func=AF.Sqrt, bias=epst[:], scale=1.0 / D),
                         [Bss, Beps], [Bss])
                    P.op("dve", lambda e: e.reciprocal(out=out_col, in_=out_col), [Bss], [Bss])

                def stage1(n):
                    s, j = tiles[n]
                    b = n % 2
                    k3 = n % NX
                    tok0 = j * 128
                    Bx, Bmt, Bmf, Bxn, Bss = P.B("xtC", b), P.B("mtokC", b), P.B("mfmC", b), P.B("xnC", k3), P.B("ssC", k3)
                    sc = ss[:, 8 * k3:8 * k3 + 8]
                    P.dma("sp", xt[b][:], xsrc[s, tok0:tok0 + 128, :], [], [Bx], "xA%d" % b)
                    P.dma("sp", mtok[b][:], mix[s, tok0:tok0 + 128, :], [], [Bmt], "mtC%d" % b)
                    P.dma("sp", mfm[b][:], mixT[s].rearrange("(c p) t -> p c t", p=128)[:, :, tok0:tok0 + 128],
                          [], [Bmf], "mfC%d" % b)
                    P.op("pool", lambda e: e.memset(sc, 0.0), [], [Bss])
                    pt = ptc[0] % 2
                    ptc[0] += 1
                    for c in range(4):
                        P.op("pe", lambda e, pt=pt, c=c, b=b: e.transpose(pT[pt][:, c * 128:(c + 1) * 128],
                                                                          mtok[b][:, c * 128:(c + 1) * 128], ident[:]),
                             [Bmt], [BpT[pt]])
                    P.op("act", lambda e, pt=pt: e.copy(out=matt[:], in_=pT[pt][:, 0:512].rearrange("p (c t) -> p c t", c=4)),
                         [BpT[pt]], [Bmatt])
                    if even:
                        chunks = [matt[:, c, :] for c in range(4)] + [mfm[b][:, c, :] for c in range(4)]
                    else:
                        chunks = [mfm[b][:, c, :] for c in range(4)] + [matt[:, c, :] for c in range(4)]
                    for half in range(2):
                        for c in range(8):
                            P.op("pe", lambda e, half=half, c=c, ch=chunks[c]: e.matmul(
                                pout[half][:], lhsT=ch, rhs=wout[:, c, half * 512:(half + 1) * 512],
                                start=(c == 0), stop=(c == 7)), [Bmatt, Bmf, Bw], [Bpo[half]])
                    for half in range(2):
                        P.op("act", lambda e, half=half: e.activation(out=junk[:, half * 512:(half + 1) * 512],
                                                                      in_=pout[half][:], func=AF.Square,
                                                                      accum_out=sc[:, half:half + 1]),
                             [Bpo[half], Bss], [Bj, Bss])
                    rstd_chain(Bss, sc[:, 0:1], sc[:, 1:2], sc[:, 2:3])
                    for half in range(2):
                        P.op("dve", lambda e, half=half: e.scalar_tensor_tensor(
                            out=tmp[half][:], in0=pout[half][:], scalar=sc[:, 2:3],
                            in1=gam[:, 0, half * 512:(half + 1) * 512], op0=ALU.mult, op1=ALU.mult),
                            [Bpo[half], Bss, Bw], [Btmp[half]])
                        P.op("dve", lambda e, half=half, b=b, k3=k3: e.tensor_tensor(
                            out=xn[k3][:, half * 512:(half + 1) * 512], in0=tmp[half][:],
                            in1=xt[b][:, half * 512:(half + 1) * 512], op=ALU.add), [Btmp[half], Bx], [Bxn])
                    P.op("act", lambda e, k3=k3: e.activation(out=junk[:], in_=xn[k3][:], func=AF.Square, accum_out=sc[:, 3:4]),
                         [Bxn, Bss], [Bj, Bss])
                    rstd_chain(Bss, sc[:, 3:4], None, sc[:, 4:5])
                    P.op("dve", lambda e, k3=k3: e.scalar_tensor_tensor(out=h2[:], in0=xn[k3][:], scalar=sc[:, 4:5],
                                                                        in1=gam[:, 1, :], op0=ALU.mult, op1=ALU.mult),
                         [Bxn, Bss, Bw], [Bh2])

                def stage2(n):
                    pt = ptc[0] % 2
                    ptc[0] += 1
                    for dc in range(8):
                        P.op("pe", lambda e, pt=pt, dc=dc: e.transpose(pT[pt][:, dc * 128:(dc + 1) * 128],
                                                                       h2[:, dc * 128:(dc + 1) * 128], ident[:]),
                             [Bh2], [BpT[pt]])
                    P.op("act", lambda e, pt=pt: e.copy(out=h2T[:], in_=pT[pt][:].rearrange("p (c t) -> p c t", c=8)),
                         [BpT[pt]], [Bh2T])
                    for fb in range(6):
                        f0 = fb * 512
                        wdt = min(512, DFF - f0)
                        for dc in range(8):
                            P.op("pe", lambda e, dc=dc, f0=f0, wdt=wdt: e.matmul(
                                pgu[0][:, 0:wdt], lhsT=h2T[:, dc, :], rhs=wg[:, dc, f0:f0 + wdt],
                                start=(dc == 0), stop=(dc == 7)), [Bh2T, Bw], [Bpg[0]])
                        for dc in range(8):
                            P.op("pe", lambda e, dc=dc, f0=f0, wdt=wdt: e.matmul(
                                pgu[1][:, 0:wdt], lhsT=h2T[:, dc, :], rhs=wu[:, dc, f0:f0 + wdt],
                                start=(dc == 0), stop=(dc == 7)), [Bh2T, Bw], [Bpg[1]])
                        tb = fb % 2
                        P.op("act", lambda e, tb=tb, wdt=wdt: e.activation(out=tmp[tb][:, 0:wdt], in_=pgu[0][:, 0:wdt],
                                                                           func=AF.Silu), [Bpg[0]], [Btmp[tb]])
                        P.op("dve", lambda e, tb=tb, f0=f0, wdt=wdt: e.tensor_tensor(
                            out=aa[:, f0:f0 + wdt], in0=tmp[tb][:, 0:wdt], in1=pgu[1][:, 0:wdt], op=ALU.mult),
                            [Btmp[tb], Bpg[1]], [Ba])

                def stage3(n):
                    s, j = tiles[n]
                    k3 = n % NX
                    tok0 = j * 128
                    Bxn, Bss = P.B("xnC", k3), P.B("ssC", k3)
                    sc = ss[:, 8 * k3:8 * k3 + 8]
                    for r0 in range(0, 22, 8):
                        nch = min(8, 22 - r0)
                        pt = ptc[0] % 2
                        ptc[0] += 1
                        for c in range(nch):
                            fc = r0 + c
                            P.op("pe", lambda e, pt=pt, c=c, fc=fc: e.transpose(pT[pt][:, c * 128:(c + 1) * 128],
                                                                                aa[:, fc * 128:(fc + 1) * 128], ident[:]),
                                 [Ba], [BpT[pt]])
                        if r0 != 8:
                            P.op("act", lambda e, pt=pt, r0=r0, nch=nch: e.copy(
                                out=aT[:, r0:r0 + nch, :],
                                in_=pT[pt][:, 0:nch * 128].rearrange("p (c t) -> p c t", c=nch)), [BpT[pt]], [BaT])
                        else:
                            P.op("dve", lambda e, pt=pt, r0=r0, nch=nch: e.tensor_copy(
                                out=aT[:, r0:r0 + nch, :],
                                in_=pT[pt][:, 0:nch * 128].rearrange("p (c t) -> p c t", c=nch)), [BpT[pt]], [BaT])
                    for half in range(2):
                        for fc in range(22):
                            P.op("pe", lambda e, half=half, fc=fc: e.matmul(
                                pdn[half][:], lhsT=aT[:, fc, :], rhs=wd[:, fc, half * 512:(half + 1) * 512],
                                start=(fc == 0), stop=(fc == 21)), [BaT, Bw], [Bpd[half]])
                    for half in range(2):
                        P.op("act", lambda e, half=half: e.activation(out=junk[:, half * 512:(half + 1) * 512],
                                                                      in_=pdn[half][:], func=AF.Square,
                                                                      accum_out=sc[:, 5 + half:6 + half]),
                             [Bpd[half], Bss], [Bj, Bss])
                    rstd_chain(Bss, sc[:, 5:6], sc[:, 6:7], sc[:, 7:8])
                    for half in range(2):
                        P.op("dve", lambda e, half=half: e.scalar_tensor_tensor(
                            out=tmp[half][:], in0=pdn[half][:], scalar=sc[:, 7:8],
                            in1=gam[:, 2, half * 512:(half + 1) * 512], op0=ALU.mult, op1=ALU.mult),
                            [Bpd[half], Bss, Bw], [Btmp[half]])
                        P.op("dve", lambda e, half=half, k3=k3: e.tensor_tensor(
                            out=xn[k3][:, half * 512:(half + 1) * 512], in0=tmp[half][:],
                            in1=xn[k3][:, half * 512:(half + 1) * 512], op=ALU.add), [Btmp[half], Bxn], [Bxn])
                    P.dma("sp", xdst[s, tok0:tok0 + 128, :], xn[k3][:], [Bxn], [], "xoC%d" % k3)

                ntile = len(tiles)
                stage1(0)
                for n in range(ntile):
                    stage2(n)
                    if n + 1 < ntile:
                        stage1(n + 1)
                    stage3(n)
                P.barrier()
                P.emit()
    return nc


def mixer_attention(P, nc, spc, kind, idx, zqT, zkT, zv, mix, tab, msk, ident):
    with ExitStack() as ph:
        def sb(name, shape, dt):
            return ph.enter_context(nc.sbuf_tensor(_uniq(name), list(shape), dt))

        def pst(name, shape, dt):
            return ph.enter_context(nc.psum_tensor(_uniq(name), list(shape), dt))

        if kind == "na":
            npat = NPAT
            pairs = NA_PAIRS
            tabl = tab[idx]
        else:
            npat = 17
            pairs = [[(i, i - j + 8) for i in range(max(0, j - 8), min(NT, j + 9))] for j in range(NT)]
            tabl = tab
        bias = sb("biasE", [128, npat * 8, 128], BF16)
        tb32 = [sb("tb32_%d" % i, [128, 8, 128], F32) for i in range(2)]
        mk32 = [sb("mk32_%d" % i, [128, 128], F32) for i in range(2)]
        Bbias = P.B("biasT")
        for p in range(npat):
            k = p % 2
            Bt, Bm = P.B("tb32", k), P.B("mk32", k)
            P.dma("sp", tb32[k][:], tabl[p].rearrange("h k q -> k h q"), [], [Bt], "tb%d" % k)
            P.dma("sp", mk32[k][:], msk[p], [], [Bm], "mk%d" % k)
            P.op("dve", lambda e, k=k, p=p: e.tensor_tensor(
                out=tb32[k][:], in0=tb32[k][:],
                in1=mk32[k][:].unsqueeze(1).to_broadcast([128, 8, 128]), op=ALU.add), [Bt, Bm], [Bt])
            P.op("act", lambda e, k=k, p=p: e.activation(out=bias[:, p * 8:(p + 1) * 8, :], in_=tb32[k][:], func=AF.Exp),
                 [Bt], [Bbias])
        QZ = sb("QZ", [128, 4, NT, 2, 128], BF16)
        KT = sb("KT", [128, 4, T], BF16)
        VA = sb("VA", [128, NT, 8, 65], BF16)
        NB = 4
        XS = [sb("XS%d" % i, [128, 512], BF16) for i in range(NB)]
        PT = [sb("PT%d" % i, [128, 512], BF16) for i in range(NB)]
        rden = sb("rden", [128, 8], F32)
        mo = [sb("mo%d" % i, [128, 512], BF16) for i in range(2)]
        pS = [pst("pS%d" % i, [128, 512], F32) for i in range(NB)]
        pO = [pst("pO%d" % i, [128, 512], F32) for i in range(4)]
        BQ, BK, BV, Brd = P.B("QZ"), P.B("KT"), P.B("VA"), P.B("rden")
        P.op("pool", lambda e: e.memset(VA[:, :, :, 64:65], 1.0), [], [BV])
        P.op("pool", lambda e: e.memset(QZ[:, 0:2], 0.0), [], [BQ])
        P.op("pool", lambda e: e.memset(QZ[:, 2:4], 0.0), [], [BQ])
        it = 0
        oc = 0
        for s in range(spc):
            for c in range(4):
                P.dma("sp", QZ[0:64, c, :, 0, :], zqT[s, c * 128:c * 128 + 64, :].rearrange("p (j q) -> p j q", q=128),
                      [], [BQ], "ldq%d" % c)
                P.dma("sp", QZ[64:128, c, :, 1, :], zqT[s, c * 128 + 64:c * 128 + 128, :].rearrange("p (j q) -> p j q", q=128),
                      [], [BQ], "ldqb%d" % c)
                P.dma("sp", KT[:, c, :], zkT[s, c * 128:(c + 1) * 128, :], [], [BK], "ldk%d" % c)
            for i4 in range(NT):
                P.dma("sp", VA[:, i4, :, 0:64],
                      zv[s, i4 * 128:(i4 + 1) * 128, :].rearrange("p (h d) -> p h d", d=64),
                      [], [BV], "ldv%d" % (i4 % 8))
            iters = []
            for j in range(NT):
                for hg in range(2):
                    lst = pairs[j]
                    for n, (i, pat) in enumerate(lst):
                        iters.append((j, hg, n, i, pat, len(lst)))
            state = {}

            def emit_scores(itn, j, hg, n, i, pat, ln):
                sbk = itn % NB
                BS, BX, BP = P.B("pS", sbk), P.B("XS", sbk), P.B("PT", sbk)
                for pp in range(2):
                    c = hg * 2 + pp
                    P.op("pe", lambda e, sbk=sbk, pp=pp, c=c, i=i, j=j: e.matmul(
                        pS[sbk][:, pp * 256:(pp + 1) * 256], lhsT=KT[:, c, i * 128:(i + 1) * 128],
                        rhs=QZ[:, c, j].rearrange("p a q -> p (a q)"), start=(pp == 0), stop=True,
                        skip_group_check=True), [BQ, BK], [BS])
                P.op("act", lambda e, sbk=sbk: e.activation(out=XS[sbk][:], in_=pS[sbk][:], func=AF.Exp), [BS], [BX])
                P.op("dve", lambda e, sbk=sbk, pat=pat, hg=hg: e.tensor_tensor(
                    out=PT[sbk][:].rearrange("p (h q) -> p h q", h=4), in0=XS[sbk][:].rearrange("p (h q) -> p h q", h=4),
                    in1=bias[:, pat * 8 + hg * 4:pat * 8 + hg * 4 + 4, :], op=ALU.mult), [BX, Bbias], [BP])

            def emit_pv(itn, j, hg, n, i, pat, ln):
                sbk = itn % NB
                BP = P.B("PT", sbk)
                if n == 0:
                    state["ob"] = state.get("oc", 0) % 4
                    state["oc"] = state.get("oc", 0) + 1
                ob = state["ob"]
                BO = P.B("pO", ob)
                for hh in range(4):
                    h = hg * 4 + hh
                    P.op("pe", lambda e, sbk=sbk, hh=hh, h=h, i=i, ob=ob, n=n, ln=ln: e.matmul(
                        pO[ob][:, hh * 65:(hh + 1) * 65], lhsT=PT[sbk][:, hh * 128:(hh + 1) * 128],
                        rhs=VA[:, i, h, :], start=(n == 0 and hh == 0), stop=(n == ln - 1),
                        skip_group_check=True), [BP, BV], [BO])
                if n == ln - 1:
                    mb = j % 2
                    Bmo = P.B("mo", mb)
                    P.op("dve", lambda e, ob=ob, hg=hg: e.reciprocal(
                        out=rden[:, hg * 4:(hg + 1) * 4],
                        in_=pO[ob][:, 0:260].rearrange("p (h d) -> p h d", d=65)[:, :, 64]), [BO], [Brd])
                    for hh in range(4):
                        h = hg * 4 + hh
                        P.op("dve", lambda e, ob=ob, hh=hh, h=h, mb=mb: e.tensor_scalar(
                            out=mo[mb][:, h * 64:(h + 1) * 64], in0=pO[ob][:, hh * 65:hh * 65 + 64],
                            scalar1=rden[:, h:h + 1], scalar2=None, op0=ALU.mult), [BO, Brd], [Bmo])
                    if hg == 1:
                        P.dma("sp", mix[s, j * 128:(j + 1) * 128, :], mo[mb][:], [Bmo], [], "mo%d" % mb)

            base = it
            LA = 2
            for q in range(min(LA, len(iters))):
                emit_scores(base + q, *iters[q])
            for n_it in range(len(iters)):
                if n_it + LA < len(iters):
                    emit_scores(base + n_it + LA, *iters[n_it + LA])
                emit_pv(base + n_it, *iters[n_it])
            it = base + len(iters)
        P.barrier()
        P.emit()


def mixer_hgrn2(P, nc, spc, idx, zr, zi, mixT, hgrn_lb, hgrn_onorm, cmask_f, cmask_b, tri_f, tri_b, ident, ones_b):
    with ExitStack() as ph:
        def sb(name, shape, dt):
            return ph.enter_context(nc.sbuf_tensor(_uniq(name), list(shape), dt))

        def pst(name, shape, dt):
            return ph.enter_context(nc.psum_tensor(_uniq(name), list(shape), dt))

        NCH = 64
        qs = sb("hq", [128, T], F32)
        zin = sb("hz", [128, T], F32)
        tA = sb("hA", [128, T], F32)
        tB = sb("hB", [128, T], F32)
        tC = sb("hC", [128, T], F32)
        Qt = [sb("hQt%d" % d, [128, T], BF16) for d in range(2)]
        Kt = [sb("hKt%d" % d, [128, T], BF16) for d in range(2)]
        Qh = [sb("hQh%d" % d, [128, T], BF16) for d in range(2)]
        Khf = sb("hKhf", [128, T], BF16)
        Kh = [sb("hKh%d" % d, [64, NCH, 128], BF16) for d in range(2)]
        Vt = sb("hV", [64, NCH, 128], BF16)
        dec = [sb("hdec%d" % d, [128, NCH], F32) for d in range(2)]
        cm = [sb("hcm%d" % d, [128, T], BF16) for d in range(2)]
        tri = [sb("htri%d" % d, [64, 64], F32) for d in range(2)]
        St = [sb("hS%d" % d, [128, 128], F32) for d in range(2)]
        Sb = [sb("hSb%d" % d, [128, 128], BF16) for d in range(2)]
        aT = [sb("haT%d_%d" % (d, k), [64, 64], BF16) for d in range(2) for k in range(2)]
        par = sb("hpar", [128, 8], F32)
        rstd = sb("hrstd", [128, 512], F32)
        ob16 = Qt[0]
        pa = [pst("hpa%d" % k, [64, 64], F32) for k in range(2)]
        po = [pst("hpo%d" % d, [128, 512], F32) for d in range(2)]
        pss = [pst("hps%d" % d, [128, 128], F32) for d in range(2)]
        ptr = pst("hptr", [64, 1024], BF16)
        pn = pst("hpn", [128, 512], F32)
        Bq, Bz, BA, BB, BC = P.B("hq"), P.B("hz"), P.B("hA"), P.B("hB"), P.B("hC")
        Bc = P.B("hconst")
        Bpar = P.B("hpar")
        P.dma("pool", cm[0][:], cmask_f[0].partition_broadcast(128), [], [Bc], "hc0")
        P.dma("pool", cm[1][:], cmask_b[0].partition_broadcast(128), [], [Bc], "hc1")
        P.dma("sp", tri[0][:], tri_f, [], [Bc], "hc2")
        P.dma("sp", tri[1][:], tri_b, [], [Bc], "hc3")
        for hd in range(4):
            f0 = hd * 128
            for d in range(2):
                P.dma("sp", par[:, d:d + 1], hgrn_lb[0, d, f0:f0 + 128].rearrange("(p o) -> p o", o=1), [], [Bpar], "hp0")
                P.dma("sp", par[:, 2 + d:3 + d], hgrn_lb[1, d, f0:f0 + 128].rearrange("(p o) -> p o", o=1), [], [Bpar], "hp1")
            P.dma("sp", par[:, 6:7], hgrn_onorm[idx].rearrange("(p o) -> p o", o=1), [], [Bpar], "hp2")
            if idx == 0:
                P.op("pool", lambda e: e.memset(par[:, 2:4], 0.0), [Bpar], [Bpar])
            else:
                P.op("dve", lambda e: e.tensor_tensor(out=par[:, 2:4], in0=par[:, 2:4], in1=par[:, 0:2], op=ALU.subtract),
                     [Bpar], [Bpar])
                P.op("act", lambda e: e.activation(out=par[:, 2:4], in_=par[:, 2:4], func=AF.Sigmoid), [Bpar], [Bpar])
            P.op("dve", lambda e: e.tensor_scalar(out=par[:, 4:6], in0=par[:, 2:4], scalar1=-1.0, scalar2=1.0,
                                                  op0=ALU.mult, op1=ALU.add), [Bpar], [Bpar])
            for s in range(spc):
                Bzall = [P.B("hz", blk) for blk in range(4)]
                P.dma("sp", zin[:], zr[s, 0, f0:f0 + 128, :], [], Bzall, "hz")
                P.op("act", lambda e: e.activation(out=qs[:], in_=zin[:], func=AF.Silu), Bzall, [Bq])
                P.dma("pool", Vt[:], zi[s, :, f0:f0 + 128].rearrange("(c p) v -> p c v", p=64), [], [P.B("hV")], "hv")
                NBLK = 4
                BW = T // NBLK
                CPB = BW // 64
                for d in range(2):
                    BQt, BKt, BQh, BKh, Bdec = (P.B("hQt", d), P.B("hKt", d), P.B("hQh", d), P.B("hKh", d),
                                                P.B("hdec", d))
                    mid = 31 if d == 0 else 32
                    last = 63 if d == 0 else 0
                    for blk in range(NBLK):
                        P.dma("sp", zin[:, blk * BW:(blk + 1) * BW], zr[s, 1 + d, f0:f0 + 128, blk * BW:(blk + 1) * BW],
                              [], [P.B("hz", blk)], "hz%d" % blk)

                    def v3(tile_, blk):
                        return tile_[:, blk * BW:(blk + 1) * BW].rearrange("p (c t) -> p c t", t=64)

                    def bl(tile_, blk):
                        return tile_[:, blk * BW:(blk + 1) * BW]

                    ops = []
                    ops.append(("act", lambda blk: (lambda e: e.activation(out=bl(tA, blk), in_=bl(zin, blk), func=AF.Sigmoid)),
                                lambda blk: [P.B("hz", blk)], lambda blk: [P.B("hA", blk)]))
                    ops.append(("dve", lambda blk, d=d: (lambda e: e.tensor_scalar(
                        out=bl(tA, blk), in0=bl(tA, blk), scalar1=par[:, 4 + d:5 + d], scalar2=par[:, 2 + d:3 + d],
                        op0=ALU.mult, op1=ALU.add)), lambda blk: [P.B("hA", blk), Bpar], lambda blk: [P.B("hA", blk)]))
                    ops.append(("act", lambda blk: (lambda e: e.activation(out=bl(tB, blk), in_=bl(tA, blk), func=AF.Ln)),
                                lambda blk: [P.B("hA", blk)], lambda blk: [P.B("hB", blk)]))
                    ops.append(("dve", lambda blk: (lambda e: e.tensor_scalar(
                        out=bl(tA, blk), in0=bl(tA, blk), scalar1=-1.0, scalar2=1.0, op0=ALU.mult, op1=ALU.add)),
                        lambda blk: [P.B("hA", blk)], lambda blk: [P.B("hA", blk)]))
                    if d == 0:
                        ops.append(("dve", lambda blk: (lambda e: e.tensor_tensor_scan(
                            out=bl(zin, blk), data0=bl(cm[0], blk), data1=bl(tB, blk), initial=0.0,
                            op0=ALU.mult, op1=ALU.add)), lambda blk: [P.B("hB", blk), Bc, P.B("hz", blk)],
                            lambda blk: [P.B("hz", blk)]))
                    else:
                        ops.append(("dve", lambda blk: (lambda e: e.tensor_tensor_scan(
                            out=bl(zin, blk)[:, ::-1], data0=bl(cm[1], blk)[:, ::-1], data1=bl(tB, blk)[:, ::-1],
                            initial=0.0, op0=ALU.mult, op1=ALU.add)), lambda blk: [P.B("hB", blk), Bc, P.B("hz", blk)],
                            lambda blk: [P.B("hz", blk)]))
                    ops.append(("dve", lambda blk, mid=mid: (lambda e: e.tensor_tensor(
                        out=v3(tB, blk), in0=v3(zin, blk), in1=v3(zin, blk)[:, :, mid:mid + 1].to_broadcast([128, CPB, 64]),
                        op=ALU.subtract)), lambda blk: [P.B("hz", blk), P.B("hB", blk)], lambda blk: [P.B("hB", blk)]))
                    ops.append(("act", lambda blk: (lambda e: e.activation(out=bl(tC, blk), in_=bl(tB, blk), func=AF.Exp)),
                                lambda blk: [P.B("hB", blk)], lambda blk: [P.B("hC", blk)]))
                    ops.append(("dve", lambda blk, d=d: (lambda e: e.tensor_tensor(
                        out=bl(Qt[d], blk), in0=bl(qs, blk), in1=bl(tC, blk), op=ALU.mult)),
                        lambda blk: [Bq, P.B("hC", blk)], lambda blk: [P.B("hQt", d)]))
                    ops.append(("act", lambda blk: (lambda e: e.activation(out=bl(tC, blk), in_=bl(tB, blk), func=AF.Exp,
                                                                           scale=-1.0)),
                                lambda blk: [P.B("hB", blk), P.B("hC", blk)], lambda blk: [P.B("hC", blk)]))
                    ops.append(("dve", lambda blk, d=d: (lambda e: e.tensor_tensor(
                        out=bl(Kt[d], blk), in0=bl(tA, blk), in1=bl(tC, blk), op=ALU.mult)),
                        lambda blk: [P.B("hA", blk), P.B("hC", blk)], lambda blk: [P.B("hKt", d)]))
                    ops.append(("act", lambda blk: (lambda e: e.activation(out=bl(tC, blk), in_=bl(zin, blk), func=AF.Exp)),
                                lambda blk: [P.B("hz", blk), P.B("hC", blk)], lambda blk: [P.B("hC", blk)]))
                    ops.append(("dve", lambda blk, d=d: (lambda e: e.tensor_tensor(
                        out=bl(Qh[d], blk), in0=bl(qs, blk), in1=bl(tC, blk), op=ALU.mult)),
                        lambda blk: [Bq, P.B("hC", blk)], lambda blk: [P.B("hQh", d)]))
                    ops.append(("dve", lambda blk, d=d, last=last: (lambda e: e.tensor_copy(
                        out=dec[d][:, blk * CPB:(blk + 1) * CPB], in_=v3(tC, blk)[:, :, last])),
                        lambda blk: [P.B("hC", blk)], lambda blk: [Bdec]))
                    ops.append(("dve", lambda blk, last=last: (lambda e: e.tensor_tensor(
                        out=v3(tB, blk), in0=v3(zin, blk), in1=v3(zin, blk)[:, :, last:last + 1].to_broadcast([128, CPB, 64]),
                        op=ALU.subtract)), lambda blk: [P.B("hz", blk), P.B("hB", blk)], lambda blk: [P.B("hB", blk)]))
                    ops.append(("act", lambda blk: (lambda e: e.activation(out=bl(tC, blk), in_=bl(tB, blk), func=AF.Exp,
                                                                           scale=-1.0)),
                                lambda blk: [P.B("hB", blk), P.B("hC", blk)], lambda blk: [P.B("hC", blk)]))
                    ops.append(("dve", lambda blk: (lambda e: e.tensor_tensor(
                        out=bl(Khf, blk), in0=bl(tA, blk), in1=bl(tC, blk), op=ALU.mult)),
                        lambda blk: [P.B("hA", blk), P.B("hC", blk)], lambda blk: [P.B("hKhf", blk)]))
                    for (eng, mk, rd, wr) in ops:
                        for blk in range(NBLK):
                            P.op(eng, mk(blk), rd(blk), wr(blk))
                    Bptr = P.B("hptr")
                    for c8 in range(8):
                        for cc in range(8):
                            c = c8 * 8 + cc
                            P.op("pe", lambda e, c=c, cc=cc: e.transpose(ptr[:, cc * 128:(cc + 1) * 128],
                                                                         Khf[:, c * 64:(c + 1) * 64], ident[:]),
                                 [P.B("hKhf", c8 // 2)], [Bptr])
                        P.op("act", lambda e, d=d, c8=c8: e.copy(out=Kh[d][:, c8 * 8:(c8 + 1) * 8, :],
                                                                 in_=ptr[:].rearrange("p (c v) -> p c v", c=8)),
                             [Bptr], [BKh])
                BAall = [P.B("hA", blk) for blk in range(4)]
                BBall = [P.B("hB", blk) for blk in range(4)]
                Bzall = [P.B("hz", blk) for blk in range(4)]
                BKfall = [P.B("hKhf", blk) for blk in range(4)]
                BO = [BAall, BBall]
                osb = [tA, tB]
                for d in range(2):
                    P.op("pool", lambda e, d=d: e.memset(St[d][:], 0.0), [], [P.B("hS", d)])
                    P.op("pool", lambda e, d=d: e.memset(Sb[d][:], 0.0), [], [P.B("hSb", d)])
                def chunk_of(step, d):
                    return step if d == 0 else NCH - 1 - step

                def emit_pa(step, d):
                    c = chunk_of(step, d)
                    k = step % 2
                    cs = slice(c * 64, (c + 1) * 64)
                    Bpa, BaT = P.B("hpa", d), P.B("haT", d, k)
                    P.op("pe", lambda e, d=d, cs=cs: e.matmul(pa[d][:], lhsT=Kt[d][:, cs], rhs=Qt[d][:, cs],
                                                              start=True, stop=True),
                         [P.B("hKt", d), P.B("hQt", d)], [Bpa])
                    P.op("dve", lambda e, d=d, k=k: e.tensor_tensor(out=aT[d * 2 + k][:], in0=pa[d][:], in1=tri[d][:],
                                                                    op=ALU.mult), [Bpa, Bc], [BaT])

                for d in range(2):
                    emit_pa(0, d)
                for step in range(NCH):
                    if step + 1 < NCH:
                        for d in range(2):
                            emit_pa(step + 1, d)
                    for d in range(2):
                        c = chunk_of(step, d)
                        k = step % 2
                        BaT, Bpo = P.B("haT", d, k), P.B("hpo", d)
                        slot = (step % 8) if d == 0 else 7 - (step % 8)
                        P.op("pe", lambda e, d=d, k=k, c=c, slot=slot: e.matmul(
                            po[d][:, slot * 64:(slot + 1) * 64], lhsT=Vt[:, c, :], rhs=aT[d * 2 + k][:],
                            start=True, stop=False), [P.B("hV"), BaT], [Bpo])
                    for d in range(2):
                        c = chunk_of(step, d)
                        Bpo, Bps = P.B("hpo", d), P.B("hps", d)
                        BS, BSb = P.B("hS", d), P.B("hSb", d)
                        cs = slice(c * 64, (c + 1) * 64)
                        slot = (step % 8) if d == 0 else 7 - (step % 8)
                        P.op("pe", lambda e, d=d, cs=cs, slot=slot: e.matmul(
                            po[d][:, slot * 64:(slot + 1) * 64], lhsT=Sb[d][:], rhs=Qh[d][:, cs],
                            start=False, stop=True), [BSb, P.B("hQh", d)], [Bpo])
                        P.op("pe", lambda e, d=d, c=c: e.matmul(pss[d][:], lhsT=Kh[d][:, c, :], rhs=Vt[:, c, :],
                                                                start=True, stop=True),
                             [P.B("hKh", d), P.B("hV")], [Bps])
                        P.op("dve", lambda e, d=d, c=c: e.scalar_tensor_tensor(out=St[d][:], in0=St[d][:],
                                                                               scalar=dec[d][:, c:c + 1], in1=pss[d][:],
                                                                               op0=ALU.mult, op1=ALU.add),
                             [BS, Bps, P.B("hdec", d)], [BS])
                        P.op("act", lambda e, d=d: e.copy(out=Sb[d][:], in_=St[d][:]), [BS], [BSb])
                        if step % 8 == 7:
                            g8 = step // 8
                            t0 = g8 * 512 if d == 0 else (7 - g8) * 512
                            P.op("act", lambda e, d=d, t0=t0: e.copy(out=osb[d][:, t0:t0 + 512], in_=po[d][:]),
                                 [Bpo], BO[d])
                P.op("dve", lambda e: e.tensor_tensor(out=tA[:], in0=tA[:], in1=tB[:], op=ALU.add), BAall + BBall, BAall)
                P.op("act", lambda e: e.activation(out=ob16[:], in_=tA[:], func=AF.Square), BAall, [P.B("hQt", 0)])
                P.dma("sp", zin[:], zr[s, 3, f0:f0 + 128, :], [], Bzall, "hz")
                P.op("act", lambda e: e.activation(out=tB[:], in_=zin[:], func=AF.Silu), Bzall + BBall, BBall)
                Bpn, Brs = P.B("hpn"), P.B("hrstd")
                for blk in range(8):
                    bs = slice(blk * 512, (blk + 1) * 512)
                    P.op("pe", lambda e, bs=bs: e.matmul(pn[:], lhsT=ones_b[:], rhs=ob16[:, bs], start=True, stop=True),
                         [P.B("hQt", 0)], [Bpn])
                    P.op("dve", lambda e: e.tensor_scalar(out=rstd[:], in0=pn[:], scalar1=1.0 / 128, scalar2=EPS,
                                                          op0=ALU.mult, op1=ALU.add), [Bpn], [Brs])
                    P.op("act", lambda e: e.activation(out=rstd[:], in_=rstd[:], func=AF.Sqrt), [Brs], [Brs])
                    P.op("dve", lambda e: e.reciprocal(out=rstd[:], in_=rstd[:]), [Brs], [Brs])
                    P.op("dve", lambda e, bs=bs: e.tensor_tensor(out=tA[:, bs], in0=tA[:, bs], in1=rstd[:], op=ALU.mult),
                         BAall + [Brs], BAall)
                P.op("dve", lambda e: e.scalar_tensor_tensor(out=Khf[:], in0=tA[:], scalar=par[:, 6:7], in1=tB[:],
                                                             op0=ALU.mult, op1=ALU.mult), BAall + BBall + [Bpar], BKfall)
                P.dma("sp", mixT[s, f0:f0 + 128, :], Khf[:], BKfall, [], "hout")
        P.barrier()
        P.emit()


def mixer_rglru(P, nc, spc, idx, zr, mixT, conv_w, conv_b, rg_wa, rg_ba, rg_wx, rg_bx, rg_lambda):
    with ExitStack() as ph:
        def sb(name, shape, dt):
            return ph.enter_context(nc.sbuf_tensor(_uniq(name), list(shape), dt))

        def pst(name, shape, dt):
            return ph.enter_context(nc.psum_tensor(_uniq(name), list(shape), dt))

        xp = sb("rxp", [128, T + 4], F32)
        yy = sb("ry", [128, T], F32)
        yb = sb("ryb", [128, T], BF16)
        rr = sb("rr", [128, T], F32)
        ii = sb("ri", [128, T], F32)
        a1 = sb("ra", [128, T], F32)
        t1 = sb("rt", [128, T], F32)
        hh = [sb("rh%d" % d, [128, T], F32) for d in range(2)]
        ob = sb("rob", [128, T], BF16)
        wbd32 = sb("rw32", [128, 4, 128], F32)
        wbd = sb("rwbd", [128, 4, 128], BF16)
        par = sb("rpar", [128, 24], F32)
        pm = [pst("rpm%d" % i, [128, 512], F32) for i in range(4)]
        Bxp, By, Byb, Br, Bi, Ba, Bt, Bob = (P.B("rxp"), P.B("ry"), P.B("ryb"), P.B("rr"), P.B("ri"), P.B("ra"),
                                             P.B("rt"), P.B("rob"))
        Bh = [P.B("rh", d) for d in range(2)]
        Bw, Bpar = P.B("rw"), P.B("rpar")
        P.op("pool", lambda e: e.memset(xp[:], 0.0), [], [Bxp])
        col = lambda ap: ap.rearrange("(p o) -> p o", o=1)
        ec = 0
        for ch in range(4):
            c0 = ch * 128
            for k in range(4):
                P.dma("sp", par[:, k:k + 1], col(conv_w[idx, k, c0:c0 + 128]), [], [Bpar], "rp0")
            P.dma("sp", par[:, 4:5], col(conv_b[idx, c0:c0 + 128]), [], [Bpar], "rp1")
            for d in range(2):
                P.dma("sp", par[:, 5 + d:6 + d], col(rg_ba[idx, d, c0:c0 + 128]), [], [Bpar], "rp2")
                P.dma("sp", par[:, 7 + d:8 + d], col(rg_bx[idx, d, c0:c0 + 128]), [], [Bpar], "rp3")
                P.dma("sp", par[:, 9 + d:10 + d], col(rg_lambda[idx, d, c0:c0 + 128]), [], [Bpar], "rp4")
            P.op("act", lambda e: e.activation(out=par[:, 9:11], in_=par[:, 9:11], func=AF.Exp, scale=-1.0), [Bpar], [Bpar])
            P.op("dve", lambda e: e.tensor_scalar(out=par[:, 9:11], in0=par[:, 9:11], scalar1=1.0, scalar2=None, op0=ALU.add),
                 [Bpar], [Bpar])
            P.op("act", lambda e: e.activation(out=par[:, 9:11], in_=par[:, 9:11], func=AF.Ln), [Bpar], [Bpar])
            P.op("dve", lambda e: e.tensor_scalar(out=par[:, 11:13], in0=par[:, 9:11], scalar1=-16.0, scalar2=None,
                                                  op0=ALU.mult), [Bpar], [Bpar])
            P.op("dve", lambda e: e.tensor_scalar(out=par[:, 9:11], in0=par[:, 9:11], scalar1=-8.0, scalar2=None,
                                                  op0=ALU.mult), [Bpar], [Bpar])
            P.op("pool", lambda e: e.memset(wbd32[:], 0.0), [], [Bw])
            for d in range(2):
                for m, wsrc in enumerate((rg_wa, rg_wx)):
                    for blk in range(2):
                        P.dma("sp", wbd32[blk * 64:(blk + 1) * 64, d * 2 + m, blk * 64:(blk + 1) * 64],
                              wsrc[idx, d, ch * 2 + blk], [], [Bw], "rw")
            P.op("dve", lambda e: e.tensor_copy(out=wbd[:], in_=wbd32[:]), [Bw], [Bw])
            for s in range(spc):
                P.dma("sp", xp[:, 2:T + 2], zr[s, 1, c0:c0 + 128, :], [], [Bxp], "rx")
                P.op("dve", lambda e: e.tensor_scalar(out=yy[:], in0=xp[:, 0:T], scalar1=par[:, 0:1], scalar2=par[:, 4:5],
                                                      op0=ALU.mult, op1=ALU.add), [Bxp, Bpar], [By])
                for k in range(1, 4):
                    P.op("dve", lambda e, k=k: e.scalar_tensor_tensor(out=yy[:], in0=xp[:, k:k + T], scalar=par[:, k:k + 1],
                                                                      in1=yy[:], op0=ALU.mult, op1=ALU.add),
                         [Bxp, Bpar, By], [By])
                P.op("act", lambda e: e.copy(out=yb[:], in_=yy[:]), [By], [Byb])
                for d in range(2):
                    for blk in range(8):
                        bs = slice(blk * 512, (blk + 1) * 512)
                        for m, (dst, Bd, bc) in enumerate(((rr, Br, 5 + d), (ii, Bi, 7 + d))):
                            pb = ec % 4
                            ec += 1
                            Bpm = P.B("rpm", pb)
                            P.op("pe", lambda e, pb=pb, d=d, m=m, bs=bs: e.matmul(pm[pb][:], lhsT=wbd[:, d * 2 + m, :],
                                                                                  rhs=yb[:, bs], start=True, stop=True),
                                 [Bw, Byb], [Bpm])
                            P.op("act", lambda e, pb=pb, dst=dst, bs=bs, bc=bc: e.activation(
                                out=dst[:, bs], in_=pm[pb][:], func=AF.Sigmoid, bias=par[:, bc:bc + 1]),
                                [Bpm, Bpar], [Bd])
                    P.op("act", lambda e, d=d: e.activation(out=a1[:], in_=rr[:], func=AF.Exp, scale=par[:, 9 + d:10 + d]),
                         [Br, Bpar], [Ba])
                    P.op("act", lambda e, d=d: e.activation(out=t1[:], in_=rr[:], func=AF.Exp, scale=par[:, 11 + d:12 + d]),
                         [Br, Bpar], [Bt])
                    P.op("dve", lambda e: e.tensor_scalar(out=t1[:], in0=t1[:], scalar1=-1.0, scalar2=1.0,
                                                          op0=ALU.mult, op1=ALU.add), [Bt], [Bt])
                    P.op("act", lambda e: e.activation(out=t1[:], in_=t1[:], func=AF.Sqrt), [Bt], [Bt])
                    first = 0 if d == 0 else T - 1
                    P.op("pool", lambda e, first=first: e.memset(t1[:, first:first + 1], 1.0), [Bt], [Bt])
                    P.op("dve", lambda e: e.tensor_tensor(out=t1[:], in0=t1[:], in1=ii[:], op=ALU.mult), [Bt, Bi], [Bt])
                    P.op("dve", lambda e: e.tensor_tensor(out=t1[:], in0=t1[:], in1=yy[:], op=ALU.mult), [Bt, By], [Bt])
                    if d == 0:
                        P.op("dve", lambda e: e.tensor_tensor_scan(out=hh[0][:], data0=a1[:], data1=t1[:], initial=0.0,
                                                                   op0=ALU.mult, op1=ALU.add), [Ba, Bt], [Bh[0]])
                    else:
                        P.op("dve", lambda e: e.tensor_tensor_scan(out=hh[1][:, ::-1], data0=a1[:, ::-1],
                                                                   data1=t1[:, ::-1], initial=0.0,
                                                                   op0=ALU.mult, op1=ALU.add), [Ba, Bt], [Bh[1]])
                P.dma("sp", rr[:], zr[s, 0, c0:c0 + 128, :], [Br], [Br], "rg")
                P.op("act", lambda e: e.activation(out=ii[:], in_=rr[:], func=AF.Gelu_apprx_tanh), [Br, Bi], [Bi])
                P.op("dve", lambda e: e.tensor_tensor(out=hh[0][:], in0=hh[0][:], in1=hh[1][:], op=ALU.add),
                     [Bh[0], Bh[1]], [Bh[0]])
                P.op("dve", lambda e: e.tensor_tensor(out=ob[:], in0=hh[0][:], in1=ii[:], op=ALU.mult), [Bh[0], Bi], [Bob])
                P.dma("sp", mixT[s, c0:c0 + 128, :], ob[:], [Bob], [], "rout")
        P.barrier()
        P.emit()


def host_tables(na_rpb, t5_bias):
    hidx = np.arange(8)[None, :, None, None]
    na_tab = np.ascontiguousarray(
        np.stack([na_rpb[l][hidx, NA_DR[:, None], NA_DC[:, None]] for l in range(2)]).astype(np.float32))
    t5_tab = np.ascontiguousarray(np.transpose(t5_bias[T5_BK], (0, 3, 1, 2)).astype(np.float32))
    t = np.arange(T)
    consts = {
        "na_mask": NA_MASK, "t5_cnt": T5_CNT,
        "cmask_f": (t % 64 != 0).astype(np.float32)[None, :],
        "cmask_b": (t % 64 != 63).astype(np.float32)[None, :],
        "tri_f": (np.arange(64)[:, None] <= np.arange(64)[None, :]).astype(np.float32),
        "tri_b": (np.arange(64)[:, None] >= np.arange(64)[None, :]).astype(np.float32),
    }
    return na_tab, t5_tab, consts


_NC_CACHE = {}


def run(inputs, xs_per_core, layers=(0, 1, 2, 3), spc=SPC, debug=False, ncores=NCORES, trace=False):
    key = (tuple(layers), spc, debug)
    if key not in _NC_CACHE:
        _NC_CACHE[key] = build(layers, spc, debug)
    nc = _NC_CACHE[key]
    na_tab, t5_tab, consts = host_tables(np.asarray(inputs["na_rpb"]), np.asarray(inputs["t5_bias"]))
    shared = {k: np.ascontiguousarray(np.asarray(inputs[k], dtype=np.float32)) for k in (
        "norm_g", "w_in_even", "w_out_even", "w_in_odd", "w_out_odd", "w_gate", "w_up", "w_down", "hgrn_lb",
        "hgrn_onorm", "conv_w", "conv_b", "rg_wa", "rg_ba", "rg_wx", "rg_bx", "rg_lambda")}
    shared["na_tab"] = na_tab
    shared["t5_tab"] = t5_tab
    shared.update(consts)
    in_maps = []
    for c in range(ncores):
        m = dict(shared)
        m["x"] = xs_per_core[c]
        in_maps.append(m)
    res = run_bass_kernel_spmd(nc, in_maps, core_ids=list(range(ncores)), **({"trace": True} if trace else {}))
    return res


def kernel(**inputs):
    xp = np.asarray(inputs["x_prompt"], dtype=np.float32)
    xs = np.asarray(inputs["x_sample"], dtype=np.float32)
    allx = np.concatenate([xp, xs], axis=0)
    nseq = allx.shape[0]
    slots = np.zeros((NCORES * SPC, T, D), np.float32)
    slots[:nseq] = allx
    xs_per_core = [np.ascontiguousarray(slots[c * SPC:(c + 1) * SPC]) for c in range(NCORES)]
    res = run(inputs, xs_per_core)
    yall = np.concatenate([r["y"] for r in res.results], axis=0)[:nseq]
    return (np.ascontiguousarray(yall[:xp.shape[0]]), np.ascontiguousarray(yall[xp.shape[0]:]))
```

```python
import numpy as np
from contextlib import ExitStack
import concourse.bass as bass
import concourse.mybir as mybir
from concourse.bass_utils import run_bass_kernel_spmd

F32 = mybir.dt.float32
BF16 = mybir.dt.bfloat16
AF = mybir.ActivationFunctionType
ALU = mybir.AluOpType

T = 4096
D = 1024
NT = 32
DFF = 2816
EPS = 1e-6
SPC = 3
NCORES = 8
NEG = -30000.0


class Buf:
    __slots__ = ("w", "r")

    def __init__(self):
        self.w = None
        self.r = []


class Prog:
    def __init__(self, nc, stack):
        self.nc = nc
        self.stack = stack
        self.names = ("pe", "act", "dve", "pool", "sp")
        self.lists = {k: [] for k in self.names}
        self.esem = {}
        self.ecnt = {k: 0 for k in self.names}
        for k in ("pe", "act", "dve", "pool"):
            self.esem[k] = stack.enter_context(nc.semaphore("s_" + k))
        self.seen = {k: {} for k in self.names}
        self.bufs = {}
        self.chans = {}
        self.ninst = 0

    def B(self, *key):
        b = self.bufs.get(key)
        if b is None:
            b = self.bufs[key] = Buf()
        return b

    def chan(self, name):
        c = self.chans.get(name)
        if c is None:
            c = self.chans[name] = [self.stack.enter_context(self.nc.semaphore("c_" + name)), 0]
        return c

    def _deps(self, e, reads, writes):
        evs = []
        for b in reads:
            if b.w is not None:
                evs.append(b.w)
        for b in writes:
            if b.w is not None:
                evs.append(b.w)
            evs.extend(b.r)
        waits = {}
        seen = self.seen[e]
        for (sem, val, src) in evs:
            if src == "pe" and e == "pe":
                continue
            k = id(sem)
            if seen.get(k, 0) >= val:
                continue
            if k not in waits or waits[k][1] < val:
                waits[k] = (sem, val)
        for k, (sem, val) in waits.items():
            seen[k] = val
        return list(waits.values())

    def _commit(self, ev, reads, writes):
        for b in reads:
            b.r.append(ev)
        for b in writes:
            b.w = ev
            b.r = []

    def op(self, e, fn, reads=(), writes=()):
        waits = self._deps(e, reads, writes)
        self.ecnt[e] += 1
        ev = (self.esem[e], self.ecnt[e], e)
        self.lists[e].append((waits, fn, self.esem[e], 1))
        self._commit(ev, reads, writes)
        self.ninst += 1

    def dma(self, q, out, in_, reads, writes, chan, **kw):
        c = self.chan(chan)
        waits = self._deps(q, reads, writes)
        if c[1] > 0:
            k = id(c[0])
            if self.seen[q].get(k, 0) < c[1]:
                waits.append((c[0], c[1]))
                self.seen[q][k] = c[1]
        c[1] += 16
        ev = (c[0], c[1], "dma")
        self.lists[q].append((waits, lambda eng: eng.dma_start(out=out, in_=in_, **kw), c[0], 16))
        self._commit(ev, reads, writes)
        self.ninst += 1

    def barrier(self):
        for e in self.names:
            seen = self.seen[e]
            waits = []
            for k in ("pe", "act", "dve", "pool"):
                if k == e:
                    continue
                s, v = self.esem[k], self.ecnt[k]
                if v > 0 and seen.get(id(s), 0) < v:
                    waits.append((s, v))
                    seen[id(s)] = v
            for name, (s, v) in self.chans.items():
                if v > 0 and seen.get(id(s), 0) < v:
                    waits.append((s, v))
                    seen[id(s)] = v
            if waits:
                self.lists[e].append((waits, None, None, 0))
        for b in self.bufs.values():
            b.w = None
            b.r = []

    def emit(self):
        nc = self.nc
        lists = self.lists
        with nc.Block() as block:
            def mk(name):
                def body(eng):
                    for (waits, fn, sem, inc) in lists[name]:
                        for (s, v) in waits:
                            eng.wait_ge(s, v)
                        if fn is not None:
                            fn(eng).then_inc(sem, inc)
                return body
            block.tensor(mk("pe"))
            block.scalar(mk("act"))
            block.vector(mk("dve"))
            block.gpsimd(mk("pool"))
            block.sync(mk("sp"))
        self.lists = {k: [] for k in self.names}


def na_patterns():
    kk = np.arange(128)
    rk_l, ck = kk // 64, kk % 64
    pats = {}
    pairs = []
    drs, dcs, masks = [], [], []
    for j in range(NT):
        lst = []
        for i in range(NT):
            rq = 2 * j + rk_l[None, :]
            cq = ck[None, :]
            rk = 2 * i + rk_l[:, None]
            ckk = ck[:, None]
            start = np.clip(rq - 4, 0, 56)
            visr = (rk >= start) & (rk < start + 8)
            cs = np.clip(cq - 8, 0, 48)
            visc = (ckk >= cs) & (ckk < cs + 16)
            vis = visr & visc
            if not vis.any():
                continue
            dr = np.clip(rk - rq + 7, 0, 14) + 0 * cq
            dc = np.clip(ckk - cq + 15, 0, 30) + 0 * rq
            dr = np.where(vis, dr, 0)
            dc = np.where(vis, dc, 0)
            key = (vis.tobytes(), dr.astype(np.int8).tobytes(), dc.astype(np.int8).tobytes())
            if key not in pats:
                pats[key] = len(pats)
                drs.append(dr)
                dcs.append(dc)
                masks.append(np.where(vis, 0.0, NEG).astype(np.float32))
            lst.append((i, pats[key]))
        pairs.append(lst)
    return pairs, np.stack(drs), np.stack(dcs), np.stack(masks)


def t5_patterns():
    kk = np.arange(128)
    bks, cnts = [], []
    for dlt in range(-8, 9):
        d = 128 * dlt + kk[:, None] - kk[None, :]
        n = np.abs(d)
        cnt = (n <= 64).astype(np.int32) + ((d % 4 == 0) & (n <= 256)) + ((d % 16 == 0) & (n <= 1024))
        large = 8 + (np.log(np.maximum(n, 1).astype(np.float32) / np.float32(8)) / np.float32(np.log(1024 / 8)) * 8).astype(np.int32)
        large = np.minimum(large, 15)
        bk = np.where(d > 0, 16, 0) + np.where(n < 8, n, large)
        bks.append(bk)
        cnts.append(np.where(cnt > 0, np.log(np.maximum(cnt, 1)), NEG).astype(np.float32))
    return np.stack(bks), np.stack(cnts)


_UID = [0]


def _uniq(name):
    _UID[0] += 1
    return "%s_u%d" % (name, _UID[0])


NA_PAIRS, NA_DR, NA_DC, NA_MASK = na_patterns()
NPAT = NA_MASK.shape[0]
T5_BK, T5_CNT = t5_patterns()


def build(layers=(0, 1, 2, 3), spc=SPC, debug=False):
    nc = bass.Bass("TRN2", target_bir_lowering=False)

    def din(name, shape, dt=F32):
        return nc.dram_tensor(name, list(shape), dt, kind="ExternalInput").ap()

    def dscr(name, shape, dt):
        return nc.dram_tensor(name, list(shape), dt, kind="ExternalOutput" if debug else "Internal").ap()

    x_in = din("x", [spc, T, D])
    norm_g = din("norm_g", [4, 4, D])
    w_in_even = din("w_in_even", [2, D, 4096])
    w_out_even = din("w_out_even", [2, D, D])
    w_in_odd = din("w_in_odd", [2, D, 2560])
    w_out_odd = din("w_out_odd", [2, D, D])
    w_gate = din("w_gate", [4, D, DFF])
    w_up = din("w_up", [4, D, DFF])
    w_down = din("w_down", [4, DFF, D])
    hgrn_lb = din("hgrn_lb", [2, 2, 512])
    hgrn_onorm = din("hgrn_onorm", [2, 128])
    conv_w = din("conv_w", [2, 4, 512])
    conv_b = din("conv_b", [2, 512])
    rg_wa = din("rg_wa", [2, 2, 8, 64, 64])
    rg_ba = din("rg_ba", [2, 2, 512])
    rg_wx = din("rg_wx", [2, 2, 8, 64, 64])
    rg_bx = din("rg_bx", [2, 2, 512])
    rg_lambda = din("rg_lambda", [2, 2, 512])
    na_tab = din("na_tab", [2, NPAT, 8, 128, 128])
    na_mask = din("na_mask", [NPAT, 128, 128])
    t5_tab = din("t5_tab", [17, 8, 128, 128])
    t5_cnt = din("t5_cnt", [17, 128, 128])
    cmask_f = din("cmask_f", [1, T])
    cmask_b = din("cmask_b", [1, T])
    tri_f = din("tri_f", [64, 64])
    tri_b = din("tri_b", [64, 64])
    y_out = nc.dram_tensor("y", [spc, T, D], F32, kind="ExternalOutput").ap()

    xres = [dscr("xres0", [spc, T, D], F32), dscr("xres1", [spc, T, D], F32)]
    zqT = dscr("zqT", [spc, 512, T], BF16)
    zkT = dscr("zkT", [spc, 512, T], BF16)
    zv = dscr("zv", [spc, T, 512], BF16)
    zr = dscr("zr", [spc, 4, 512, T], F32)
    zi = dscr("zi", [spc, T, 512], F32)
    mix = dscr("mix", [spc, T, 512], BF16)
    mixT = dscr("mixT", [spc, 512, T], BF16)

    with ExitStack() as glob:
        P = Prog(nc, glob)

        def sb(stk, name, shape, dt):
            return stk.enter_context(nc.sbuf_tensor(_uniq(name), list(shape), dt))

        def pst(stk, name, shape, dt):
            return stk.enter_context(nc.psum_tensor(_uniq(name), list(shape), dt))

        identf = sb(glob, "identf", [128, 128], F32)
        ident = sb(glob, "ident", [128, 128], BF16)
        ones_b = sb(glob, "ones_b", [128, 128], BF16)
        Bid = P.B("ident")
        P.op("pool", lambda e: e.memset(identf[:], 0.0), [], [Bid])
        P.op("pool", lambda e: e.affine_select(out=identf[:], in_=identf[:], pattern=[[-1, 128]],
                                                compare_op=ALU.not_equal, fill=1.0, base=0,
                                                channel_multiplier=1), [Bid], [Bid])
        P.op("dve", lambda e: e.tensor_copy(out=ident[:], in_=identf[:]), [Bid], [Bid])
        P.op("pool", lambda e: e.memset(ones_b[:], 1.0), [], [Bid])
        P.barrier()
        P.emit()

        nlay = len(layers)
        for li, L in enumerate(layers):
            even = (L % 2 == 0)
            idx = L // 2
            xsrc = x_in if li == 0 else xres[(li - 1) % 2]
            xdst = y_out if li == nlay - 1 else xres[li % 2]
            with ExitStack() as ph:
                ncol = 4096 if even else 2560
                w_in = (w_in_even if even else w_in_odd)[idx]
                win = sb(ph, "win", [128, 8, ncol], BF16)
                g0 = sb(ph, "g0", [128, D], F32)
                Bw = P.B("win")
                wsrc = w_in.rearrange("(c p) n -> p c n", p=128)
                for dc in range(8):
                    P.dma("pool", win[:, dc, :], wsrc[:, dc, :], [], [Bw], "w%d" % dc)
                P.dma("sp", g0[:], norm_g[L, 0].partition_broadcast(128), [], [Bw], "gA")
                xt = [sb(ph, "xtA%d" % i, [128, D], F32) for i in range(4)]
                junk = sb(ph, "junkA", [128, D], F32)
                hb = [sb(ph, "hbA%d" % i, [128, D], BF16) for i in range(4)]
                hT = [sb(ph, "hTA%d" % i, [128, 8, 512], BF16) for i in range(2)]
                ss = sb(ph, "ssA", [128, 8], F32)
                stF = [sb(ph, "stF%d" % i, [128, 4, 512], F32) for i in range(4)]
                stH = [sb(ph, "stH%d" % i, [128, 4, 512], BF16) for i in range(4)]
                pT = [pst(ph, "pTA%d" % i, [128, 1024], BF16) for i in range(2)]
                pm = [pst(ph, "pmA%d" % i, [128, 512], F32) for i in range(4)]
                if even:
                    fm_parts = [(0, "qk", zqT, 0.125), (512, "qk", zkT, 1.0), (1536, "r", 0, 1.0),
                                (2048, "r", 1, 1.0), (2560, "r", 2, 1.0), (3584, "r", 3, 1.0)]
                    tm_parts = [(1024, "v"), (3072, "i")]
                else:
                    fm_parts = [(0, "r", 0, 1.0), (512, "r", 1, 1.0), (1024, "qk", zqT, 0.125),
                                (1536, "qk", zkT, 1.0)]
                    tm_parts = [(2048, "v")]
                hb = hb + [sb(ph, "hbA%d" % i, [128, D], BF16) for i in range(4, 8)]
                groups = [(s_, g_) for s_ in range(spc) for g_ in range(8)]
                ecnt_box = [0]
                scnt = {"F": 0, "H": 0}

                def stageN(gi):
                    s, g = groups[gi]
                    for i in range(4):
                        cnt = gi * 4 + i
                        b = cnt % 4
                        hbi = (gi % 2) * 4 + i
                        tok0 = g * 512 + i * 128
                        Bx, Bh, Bss = P.B("xtA", b), P.B("hbA", hbi), P.B("ssA", b)
                        Bj = P.B("junkA")
                        P.dma("sp", xt[b][:], xsrc[s, tok0:tok0 + 128, :], [], [Bx], "xA%d" % b)
                        sc = ss[:, 2 * b:2 * b + 1]
                        rs = ss[:, 2 * b + 1:2 * b + 2]
                        P.op("pool", lambda e, sc=sc: e.memset(sc, 0.0), [], [Bss])
                        P.op("act", lambda e, b=b, sc=sc: e.activation(out=junk[:], in_=xt[b][:], func=AF.Square,
                                                                       accum_out=sc), [Bx, Bss], [Bj, Bss])
                        P.op("dve", lambda e, sc=sc, rs=rs: e.tensor_scalar(out=rs, in0=sc, scalar1=1.0 / D, scalar2=EPS,
                                                                           op0=ALU.mult, op1=ALU.add), [Bss], [Bss])
                        P.op("act", lambda e, rs=rs: e.activation(out=rs, in_=rs, func=AF.Sqrt), [Bss], [Bss])
                        P.op("dve", lambda e, rs=rs: e.reciprocal(out=rs, in_=rs), [Bss], [Bss])
                        P.op("dve", lambda e, b=b, hbi=hbi, rs=rs: e.scalar_tensor_tensor(
                            out=hb[hbi][:], in0=xt[b][:], scalar=rs, in1=g0[:], op0=ALU.mult, op1=ALU.mult),
                            [Bx, Bss, Bw], [Bh])

                def stageM(gi):
                    s, g = groups[gi]
                    hTg = hT[gi % 2]
                    BhT = P.B("hTA", gi % 2)
                    for i in range(4):
                        cnt = gi * 4 + i
                        pb2 = cnt % 2
                        hbi = (gi % 2) * 4 + i
                        Bh, BpT = P.B("hbA", hbi), P.B("pTA", pb2)
                        for dc in range(8):
                            P.op("pe", lambda e, hbi=hbi, pb2=pb2, dc=dc: e.transpose(
                                pT[pb2][:, dc * 128:(dc + 1) * 128], hb[hbi][:, dc * 128:(dc + 1) * 128], ident[:]),
                                [Bh], [BpT])
                        P.op("act", lambda e, pb2=pb2, i=i, hTg=hTg: e.copy(
                            out=hTg[:, :, i * 128:(i + 1) * 128],
                            in_=pT[pb2][:].rearrange("p (c t) -> p c t", c=8)), [BpT], [BhT])
                    ecnt = ecnt_box[0]
                    for (c0, kind, dst, scale) in fm_parts:
                        if kind == "qk":
                            k = scnt["H"] % 4
                            scnt["H"] += 1
                            st, Bst, chn = stH[k], P.B("stH", k), "sH%d" % k
                            dst_ap = dst[s].rearrange("(c p) t -> p c t", p=128)[:, :, g * 512:(g + 1) * 512]
                        else:
                            k = scnt["F"] % 4
                            scnt["F"] += 1
                            st, Bst, chn = stF[k], P.B("stF", k), "sF%d" % k
                            dst_ap = zr[s, dst].rearrange("(c p) t -> p c t", p=128)[:, :, g * 512:(g + 1) * 512]
                        for fc in range(4):
                            pb = ecnt % 4
                            Bpm = P.B("pmA", pb)
                            col = c0 + fc * 128
                            for dc in range(8):
                                P.op("pe", lambda e, pb=pb, dc=dc, col=col, hTg=hTg: e.matmul(
                                    pm[pb][:], lhsT=win[:, dc, col:col + 128], rhs=hTg[:, dc, :],
                                    start=(dc == 0), stop=(dc == 7)), [Bw, BhT], [Bpm])
                            if ecnt % 2 == 0:
                                P.op("act", lambda e, st=st, fc=fc, pb=pb, scale=scale: e.activation(
                                    out=st[:, fc, :], in_=pm[pb][:], func=AF.Copy, scale=scale), [Bpm], [Bst])
                            else:
                                P.op("dve", lambda e, st=st, fc=fc, pb=pb, scale=scale: e.tensor_scalar(
                                    out=st[:, fc, :], in0=pm[pb][:], scalar1=scale, scalar2=None, op0=ALU.mult),
                                    [Bpm], [Bst])
                            ecnt += 1
                        P.dma("sp", dst_ap, st[:], [Bst], [], chn)
                    for (c0, kind) in tm_parts:
                        if kind == "v":
                            k = scnt["H"] % 4
                            scnt["H"] += 1
                            st, Bst, chn = stH[k], P.B("stH", k), "sH%d" % k
                            dst_ap = zv[s, g * 512:(g + 1) * 512, :].rearrange("(i p) n -> p i n", p=128)
                        else:
                            k = scnt["F"] % 4
                            scnt["F"] += 1
                            st, Bst, chn = stF[k], P.B("stF", k), "sF%d" % k
                            dst_ap = zi[s, g * 512:(g + 1) * 512, :].rearrange("(i p) n -> p i n", p=128)
                        for i in range(4):
                            pb = ecnt % 4
                            Bpm = P.B("pmA", pb)
                            for dc in range(8):
                                P.op("pe", lambda e, pb=pb, dc=dc, i=i, c0=c0, hTg=hTg: e.matmul(
                                    pm[pb][:], lhsT=hTg[:, dc, i * 128:(i + 1) * 128], rhs=win[:, dc, c0:c0 + 512],
                                    start=(dc == 0), stop=(dc == 7)), [Bw, BhT], [Bpm])
                            if ecnt % 2 == 0:
                                P.op("act", lambda e, st=st, i=i, pb=pb: e.copy(out=st[:, i, :], in_=pm[pb][:]),
                                     [Bpm], [Bst])
                            else:
                                P.op("dve", lambda e, st=st, i=i, pb=pb: e.tensor_copy(out=st[:, i, :], in_=pm[pb][:]),
                                     [Bpm], [Bst])
                            ecnt += 1
                        P.dma("sp", dst_ap, st[:], [Bst], [], chn)
                    ecnt_box[0] = ecnt

                stageN(0)
                for gi in range(len(groups)):
                    if gi + 1 < len(groups):
                        stageN(gi + 1)
                    stageM(gi)
                P.barrier()
                P.emit()

            if even:
                mixer_attention(P, nc, spc, "na", idx, zqT, zkT, zv, mix, na_tab, na_mask, ident)
                mixer_hgrn2(P, nc, spc, idx, zr, zi, mixT, hgrn_lb, hgrn_onorm, cmask_f, cmask_b, tri_f, tri_b,
                            ident, ones_b)
            else:
                mixer_attention(P, nc, spc, "dil", idx, zqT, zkT, zv, mix, t5_tab, t5_cnt, ident)
                mixer_rglru(P, nc, spc, idx, zr, mixT, conv_w, conv_b, rg_wa, rg_ba, rg_wx, rg_bx, rg_lambda)

            with ExitStack() as ph:
                w_out = (w_out_even if even else w_out_odd)[idx]
                wout = sb(ph, "wout", [128, 8, D], BF16)
                wg = sb(ph, "wg", [128, 8, DFF], BF16)
                wu = sb(ph, "wu", [128, 8, DFF], BF16)
                wd = sb(ph, "wd", [128, 22, D], BF16)
                gam = sb(ph, "gamC", [128, 3, D], F32)
                Bw = P.B("wC")
                for dc in range(8):
                    P.dma("pool", wout[:, dc, :], w_out.rearrange("(c p) n -> p c n", p=128)[:, dc, :], [], [Bw], "w%d" % dc)
                for dc in range(8):
                    P.dma("pool", wg[:, dc, :], w_gate[L].rearrange("(c p) n -> p c n", p=128)[:, dc, :], [], [Bw], "w%d" % dc)
                    P.dma("pool", wu[:, dc, :], w_up[L].rearrange("(c p) n -> p c n", p=128)[:, dc, :], [], [Bw],
                          "w%d" % ((dc + 4) % 8))
                for fc in range(22):
                    P.dma("pool", wd[:, fc, :], w_down[L].rearrange("(c p) n -> p c n", p=128)[:, fc, :], [], [Bw],
                          "w%d" % (fc % 8))
                for k in range(3):
                    P.dma("sp", gam[:, k, :], norm_g[L, k + 1].partition_broadcast(128), [], [Bw], "gA")
                NX = 3
                xt = [sb(ph, "xtC%d" % i, [128, D], F32) for i in range(2)]
                xn = [sb(ph, "xnC%d" % i, [128, D], F32) for i in range(NX)]
                mtok = [sb(ph, "mtokC%d" % i, [128, 512], BF16) for i in range(2)]
                mfm = [sb(ph, "mfmC%d" % i, [128, 4, 128], BF16) for i in range(2)]
                matt = sb(ph, "mattC", [128, 4, 128], BF16)
                junk = sb(ph, "junkC", [128, D], BF16)
                tmp = [sb(ph, "tmpC%d" % i, [128, 512], F32) for i in range(2)]
                h2 = sb(ph, "h2C", [128, D], BF16)
                h2T = sb(ph, "h2TC", [128, 8, 128], BF16)
                aa = sb(ph, "aC", [128, DFF], BF16)
                aT = sb(ph, "aTC", [128, 22, 128], BF16)
                ss = sb(ph, "ssC", [128, 8 * NX], F32)
                epst = sb(ph, "epsC", [128, 1], F32)
                pT = [pst(ph, "pTC%d" % i, [128, 1024], BF16) for i in range(2)]
                pout = [pst(ph, "poC%d" % i, [128, 512], F32) for i in range(2)]
                pdn = [pst(ph, "pdC%d" % i, [128, 512], F32) for i in range(2)]
                pgu = [pst(ph, "pguC%d" % i, [128, 512], F32) for i in range(2)]
                Bj, Bh2, Bh2T, Ba, BaT, Bmatt = (P.B("junkC"), P.B("h2C"), P.B("h2TC"), P.B("aC"), P.B("aTC"),
                                                 P.B("mattC"))
                BpT = [P.B("pTC", i) for i in range(2)]
                Bpo = [P.B("poC", i) for i in range(2)]
                Bpd = [P.B("pdC", i) for i in range(2)]
                Bpg = [P.B("pguC", i) for i in range(2)]
                Btmp = [P.B("tmpC", i) for i in range(2)]
                Beps = P.B("epsC")
                P.op("pool", lambda e: e.memset(epst[:], EPS), [], [Beps])
                ptc = [0]
                tiles = [(s, j) for s in range(spc) for j in range(NT)]

                def rstd_chain(Bss, sq0, sq1, out_col):
                    if sq1 is not None:
                        P.op("dve", lambda e: e.tensor_tensor(out=sq0, in0=sq0, in1=sq1, op=ALU.add), [Bss], [Bss])
                    P.op("act", lambda e: e.activation(out=out_col, in_=sq0, func=AF.Sqrt, bias=epst[:], scale=1.0 / D),
                         [Bss, Beps], [Bss])
                    P.op("dve", lambda e: e.reciprocal(out=out_col, in_=out_col), [Bss], [Bss])

                def stage1(n):
                    s, j = tiles[n]
                    b = n % 2
                    k3 = n % NX
                    tok0 = j * 128
                    Bx, Bmt, Bmf, Bxn, Bss = P.B("xtC", b), P.B("mtokC", b), P.B("mfmC", b), P.B("xnC", k3), P.B("ssC", k3)
                    sc = ss[:, 8 * k3:8 * k3 + 8]
                    P.dma("sp", xt[b][:], xsrc[s, tok0:tok0 + 128, :], [], [Bx], "xA%d" % b)
                    P.dma("sp", mtok[b][:], mix[s, tok0:tok0 + 128, :], [], [Bmt], "mtC%d" % b)
                    P.dma("sp", mfm[b][:], mixT[s].rearrange("(c p) t -> p c t", p=128)[:, :, tok0:tok0 + 128],
                          [], [Bmf], "mfC%d" % b)
                    P.op("pool", lambda e: e.memset(sc, 0.0), [], [Bss])
                    pt = ptc[0] % 2
                    ptc[0] += 1
                    for c in range(4):
                        P.op("pe", lambda e, pt=pt, c=c, b=b: e.transpose(pT[pt][:, c * 128:(c + 1) * 128],
                                                                          mtok[b][:, c * 128:(c + 1) * 128], ident[:]),
                             [Bmt], [BpT[pt]])
                    P.op("act", lambda e, pt=pt: e.copy(out=matt[:], in_=pT[pt][:, 0:512].rearrange("p (c t) -> p c t", c=4)),
                         [BpT[pt]], [Bmatt])
                    if even:
                        chunks = [matt[:, c, :] for c in range(4)] + [mfm[b][:, c, :] for c in range(4)]
                    else:
                        chunks = [mfm[b][:, c, :] for c in range(4)] + [matt[:, c, :] for c in range(4)]
                    for half in range(2):
                        for c in range(8):
                            P.op("pe", lambda e, half=half, c=c, ch=chunks[c]: e.matmul(
                                pout[half][:], lhsT=ch, rhs=wout[:, c, half * 512:(half + 1) * 512],
                                start=(c == 0), stop=(c == 7)), [Bmatt, Bmf, Bw], [Bpo[half]])
                    for half in range(2):
                        P.op("act", lambda e, half=half: e.activation(out=junk[:, half * 512:(half + 1) * 512],
                                                                      in_=pout[half][:], func=AF.Square,
                                                                      accum_out=sc[:, half:half + 1]),
                             [Bpo[half], Bss], [Bj, Bss])
                    rstd_chain(Bss, sc[:, 0:1], sc[:, 1:2], sc[:, 2:3])
                    for half in range(2):
                        P.op("dve", lambda e, half=half: e.scalar_tensor_tensor(
                            out=tmp[half][:], in0=pout[half][:], scalar=sc[:, 2:3],
                            in1=gam[:, 0, half * 512:(half + 1) * 512], op0=ALU.mult, op1=ALU.mult),
                            [Bpo[half], Bss, Bw], [Btmp[half]])
                        P.op("dve", lambda e, half=half, b=b, k3=k3: e.tensor_tensor(
                            out=xn[k3][:, half * 512:(half + 1) * 512], in0=tmp[half][:],
                            in1=xt[b][:, half * 512:(half + 1) * 512], op=ALU.add), [Btmp[half], Bx], [Bxn])
                    P.op("act", lambda e, k3=k3: e.activation(out=junk[:], in_=xn[k3][:], func=AF.Square, accum_out=sc[:, 3:4]),
                         [Bxn, Bss], [Bj, Bss])
                    rstd_chain(Bss, sc[:, 3:4], None, sc[:, 4:5])
                    P.op("dve", lambda e, k3=k3: e.scalar_tensor_tensor(out=h2[:], in0=xn[k3][:], scalar=sc[:, 4:5],
                                                                        in1=gam[:, 1, :], op0=ALU.mult, op1=ALU.mult),
                         [Bxn, Bss, Bw], [Bh2])

                def stage2(n):
                    pt = ptc[0] % 2
                    ptc[0] += 1
                    for dc in range(8):
                        P.op("pe", lambda e, pt=pt, dc=dc: e.transpose(pT[pt][:, dc * 128:(dc + 1) * 128],
                                                                       h2[:, dc * 128:(dc + 1) * 128], ident[:]),
                             [Bh2], [BpT[pt]])
                    P.op("act", lambda e, pt=pt: e.copy(out=h2T[:], in_=pT[pt][:].rearrange("p (c t) -> p c t", c=8)),
                         [BpT[pt]], [Bh2T])
                    for fb in range(6):
                        f0 = fb * 512
                        wdt = min(512, DFF - f0)
                        for dc in range(8):
                            P.op("pe", lambda e, dc=dc, f0=f0, wdt=wdt: e.matmul(
                                pgu[0][:, 0:wdt], lhsT=h2T[:, dc, :], rhs=wg[:, dc, f0:f0 + wdt],
                                start=(dc == 0), stop=(dc == 7)), [Bh2T, Bw], [Bpg[0]])
                        for dc in range(8):
                            P.op("pe", lambda e, dc=dc, f0=f0, wdt=wdt: e.matmul(
                                pgu[1][:, 0:wdt], lhsT=h2T[:, dc, :], rhs=wu[:, dc, f0:f0 + wdt],
                                start=(dc == 0), stop=(dc == 7)), [Bh2T, Bw], [Bpg[1]])
                        tb = fb % 2
                        P.op("act", lambda e, tb=tb, wdt=wdt: e.activation(out=tmp[tb][:, 0:wdt], in_=pgu[0][:, 0:wdt],
                                                                           func=AF.Silu), [Bpg[0]], [Btmp[tb]])
                        P.op("dve", lambda e, tb=tb, f0=f0, wdt=wdt: e.tensor_tensor(
                            out=aa[:, f0:f0 + wdt], in0=tmp[tb][:, 0:wdt], in1=pgu[1][:, 0:wdt], op=ALU.mult),
                            [Btmp[tb], Bpg[1]], [Ba])

                def stage3(n):
                    s, j = tiles[n]
                    k3 = n % NX
                    tok0 = j * 128
                    Bxn, Bss = P.B("xnC", k3), P.B("ssC", k3)
                    sc = ss[:, 8 * k3:8 * k3 + 8]
                    for r0 in range(0, 22, 8):
                        nch = min(8, 22 - r0)
                        pt = ptc[0] % 2
                        ptc[0] += 1
                        for c in range(nch):
                            fc = r0 + c
                            P.op("pe", lambda e, pt=pt, c=c, fc=fc: e.transpose(pT[pt][:, c * 128:(c + 1) * 128],
                                                                                aa[:, fc * 128:(fc + 1) * 128], ident[:]),
                                 [Ba], [BpT[pt]])
                        if r0 != 8:
                            P.op("act", lambda e, pt=pt, r0=r0, nch=nch: e.copy(
                                out=aT[:, r0:r0 + nch, :],
                                in_=pT[pt][:, 0:nch * 128].rearrange("p (c t) -> p c t", c=nch)), [BpT[pt]], [BaT])
                        else:
                            P.op("dve", lambda e, pt=pt, r0=r0, nch=nch: e.tensor_copy(
                                out=aT[:, r0:r0 + nch, :],
                                in_=pT[pt][:, 0:nch * 128].rearrange("p (c t) -> p c t", c=nch)), [BpT[pt]], [BaT])

                def stage3b(n):
                    s, j = tiles[n]
                    k3 = n % NX
                    tok0 = j * 128
                    Bxn, Bss = P.B("xnC", k3), P.B("ssC", k3)
                    sc = ss[:, 8 * k3:8 * k3 + 8]
                    for half in range(2):
                        for fc in range(22):
                            P.op("pe", lambda e, half=half, fc=fc: e.matmul(
                                pdn[half][:], lhsT=aT[:, fc, :], rhs=wd[:, fc, half * 512:(half + 1) * 512],
                                start=(fc == 0), stop=(fc == 21)), [BaT, Bw], [Bpd[half]])
                    for half in range(2):
                        P.op("act", lambda e, half=half: e.activation(out=junk[:, half * 512:(half + 1) * 512],
                                                                      in_=pdn[half][:], func=AF.Square,
                                                                      accum_out=sc[:, 5 + half:6 + half]),
                             [Bpd[half], Bss], [Bj, Bss])
                    rstd_chain(Bss, sc[:, 5:6], sc[:, 6:7], sc[:, 7:8])
                    for half in range(2):
                        P.op("dve", lambda e, half=half: e.scalar_tensor_tensor(
                            out=tmp[half][:], in0=pdn[half][:], scalar=sc[:, 7:8],
                            in1=gam[:, 2, half * 512:(half + 1) * 512], op0=ALU.mult, op1=ALU.mult),
                            [Bpd[half], Bss, Bw], [Btmp[half]])
                        P.op("dve", lambda e, half=half, k3=k3: e.tensor_tensor(
                            out=xn[k3][:, half * 512:(half + 1) * 512], in0=tmp[half][:],
                            in1=xn[k3][:, half * 512:(half + 1) * 512], op=ALU.add), [Btmp[half], Bxn], [Bxn])
                    P.dma("sp", xdst[s, tok0:tok0 + 128, :], xn[k3][:], [Bxn], [], "xoC%d" % k3)

                ntile = len(tiles)
                stage1(0)
                for n in range(ntile):
                    stage2(n)
                    stage3(n)
                    if n + 1 < ntile:
                        stage1(n + 1)
                    stage3b(n)
                P.barrier()
                P.emit()
    return nc


def mixer_attention(P, nc, spc, kind, idx, zqT, zkT, zv, mix, tab, msk, ident):
    with ExitStack() as ph:
        def sb(name, shape, dt):
            return ph.enter_context(nc.sbuf_tensor(_uniq(name), list(shape), dt))

        def pst(name, shape, dt):
            return ph.enter_context(nc.psum_tensor(_uniq(name), list(shape), dt))

        if kind == "na":
            npat = NPAT
            pairs = NA_PAIRS
            tabl = tab[idx]
        else:
            npat = 17
            pairs = [[(i, i - j + 8) for i in range(max(0, j - 8), min(NT, j + 9))] for j in range(NT)]
            tabl = tab
        bias = sb("biasE", [128, npat * 8, 128], BF16)
        tb32 = [sb("tb32_%d" % i, [128, 8, 128], F32) for i in range(2)]
        mk32 = [sb("mk32_%d" % i, [128, 128], F32) for i in range(2)]
        Bbias = P.B("biasT")
        for p in range(npat):
            k = p % 2
            Bt, Bm = P.B("tb32", k), P.B("mk32", k)
            P.dma("sp", tb32[k][:], tabl[p].rearrange("h k q -> k h q"), [], [Bt], "tb%d" % k)
            P.dma("sp", mk32[k][:], msk[p], [], [Bm], "mk%d" % k)
            P.op("dve", lambda e, k=k, p=p: e.tensor_tensor(
                out=tb32[k][:], in0=tb32[k][:],
                in1=mk32[k][:].unsqueeze(1).to_broadcast([128, 8, 128]), op=ALU.add), [Bt, Bm], [Bt])
            P.op("act", lambda e, k=k, p=p: e.activation(out=bias[:, p * 8:(p + 1) * 8, :], in_=tb32[k][:], func=AF.Exp),
                 [Bt], [Bbias])
        QZ = sb("QZ", [128, 4, NT, 2, 128], BF16)
        KT = sb("KT", [128, 4, T], BF16)
        VA = sb("VA", [128, NT, 8, 65], BF16)
        NB = 4
        XS = [sb("XS%d" % i, [128, 512], BF16) for i in range(NB)]
        PT = [sb("PT%d" % i, [128, 512], BF16) for i in range(NB)]
        rden = sb("rden", [128, 8], F32)
        mo = [sb("mo%d" % i, [128, 512], BF16) for i in range(2)]
        pS = [pst("pS%d" % i, [128, 512], F32) for i in range(NB)]
        pO = [pst("pO%d" % i, [128, 512], F32) for i in range(4)]
        BQ, BK, BV, Brd = P.B("QZ"), P.B("KT"), P.B("VA"), P.B("rden")
        P.op("pool", lambda e: e.memset(VA[:, :, :, 64:65], 1.0), [], [BV])
        P.op("pool", lambda e: e.memset(QZ[:, 0:2], 0.0), [], [BQ])
        P.op("pool", lambda e: e.memset(QZ[:, 2:4], 0.0), [], [BQ])
        it = 0
        oc = 0
        for s in range(spc):
            for c in range(4):
                P.dma("sp", QZ[0:64, c, :, 0, :], zqT[s, c * 128:c * 128 + 64, :].rearrange("p (j q) -> p j q", q=128),
                      [], [BQ], "ldq%d" % c)
                P.dma("sp", QZ[64:128, c, :, 1, :], zqT[s, c * 128 + 64:c * 128 + 128, :].rearrange("p (j q) -> p j q", q=128),
                      [], [BQ], "ldqb%d" % c)
                P.dma("sp", KT[:, c, :], zkT[s, c * 128:(c + 1) * 128, :], [], [BK], "ldk%d" % c)
            for i4 in range(NT):
                P.dma("sp", VA[:, i4, :, 0:64],
                      zv[s, i4 * 128:(i4 + 1) * 128, :].rearrange("p (h d) -> p h d", d=64),
                      [], [BV], "ldv%d" % (i4 % 8))
            iters = []
            for j in range(NT):
                for hg in range(2):
                    lst = pairs[j]
                    for n, (i, pat) in enumerate(lst):
                        iters.append((j, hg, n, i, pat, len(lst)))
            state = {}

            def emit_scores(itn, j, hg, n, i, pat, ln):
                sbk = itn % NB
                BS, BX, BP = P.B("pS", sbk), P.B("XS", sbk), P.B("PT", sbk)
                for pp in range(2):
                    c = hg * 2 + pp
                    P.op("pe", lambda e, sbk=sbk, pp=pp, c=c, i=i, j=j: e.matmul(
                        pS[sbk][:, pp * 256:(pp + 1) * 256], lhsT=KT[:, c, i * 128:(i + 1) * 128],
                        rhs=QZ[:, c, j].rearrange("p a q -> p (a q)"), start=(pp == 0), stop=True,
                        skip_group_check=True), [BQ, BK], [BS])
                P.op("act", lambda e, sbk=sbk: e.activation(out=XS[sbk][:], in_=pS[sbk][:], func=AF.Exp), [BS], [BX])
                P.op("dve", lambda e, sbk=sbk, pat=pat, hg=hg: e.tensor_tensor(
                    out=PT[sbk][:].rearrange("p (h q) -> p h q", h=4), in0=XS[sbk][:].rearrange("p (h q) -> p h q", h=4),
                    in1=bias[:, pat * 8 + hg * 4:pat * 8 + hg * 4 + 4, :], op=ALU.mult), [BX, Bbias], [BP])

            def emit_pv(itn, j, hg, n, i, pat, ln):
                sbk = itn % NB
                BP = P.B("PT", sbk)
                if n == 0:
                    state["ob"] = state.get("oc", 0) % 4
                    state["oc"] = state.get("oc", 0) + 1
                ob = state["ob"]
                BO = P.B("pO", ob)
                for hh in range(4):
                    h = hg * 4 + hh
                    P.op("pe", lambda e, sbk=sbk, hh=hh, h=h, i=i, ob=ob, n=n, ln=ln: e.matmul(
                        pO[ob][:, hh * 65:(hh + 1) * 65], lhsT=PT[sbk][:, hh * 128:(hh + 1) * 128],
                        rhs=VA[:, i, h, :], start=(n == 0 and hh == 0), stop=(n == ln - 1),
                        skip_group_check=True), [BP, BV], [BO])
                if n == ln - 1:
                    mb = j % 2
                    Bmo = P.B("mo", mb)
                    P.op("dve", lambda e, ob=ob, hg=hg: e.reciprocal(
                        out=rden[:, hg * 4:(hg + 1) * 4],
                        in_=pO[ob][:, 0:260].rearrange("p (h d) -> p h d", d=65)[:, :, 64]), [BO], [Brd])
                    for hh in range(4):
                        h = hg * 4 + hh
                        P.op("dve", lambda e, ob=ob, hh=hh, h=h, mb=mb: e.tensor_scalar(
                            out=mo[mb][:, h * 64:(h + 1) * 64], in0=pO[ob][:, hh * 65:hh * 65 + 64],
                            scalar1=rden[:, h:h + 1], scalar2=None, op0=ALU.mult), [BO, Brd], [Bmo])
                    if hg == 1:
                        P.dma("sp", mix[s, j * 128:(j + 1) * 128, :], mo[mb][:], [Bmo], [], "mo%d" % mb)

            base = it
            LA = 2
            for q in range(min(LA, len(iters))):
                emit_scores(base + q, *iters[q])
            for n_it in range(len(iters)):
                if n_it + LA < len(iters):
                    emit_scores(base + n_it + LA, *iters[n_it + LA])
                emit_pv(base + n_it, *iters[n_it])
            it = base + len(iters)
        P.barrier()
        P.emit()


def mixer_hgrn2(P, nc, spc, idx, zr, zi, mixT, hgrn_lb, hgrn_onorm, cmask_f, cmask_b, tri_f, tri_b, ident, ones_b):
    with ExitStack() as ph:
        def sb(name, shape, dt):
            return ph.enter_context(nc.sbuf_tensor(_uniq(name), list(shape), dt))

        def pst(name, shape, dt):
            return ph.enter_context(nc.psum_tensor(_uniq(name), list(shape), dt))

        NCH = 64
        qs = sb("hq", [128, T], F32)
        zin = sb("hz", [128, T], F32)
        tA = sb("hA", [128, T], F32)
        tB = sb("hB", [128, T], F32)
        tC = sb("hC", [128, T], F32)
        Qt = [sb("hQt%d" % d, [128, T], BF16) for d in range(2)]
        Kt = [sb("hKt%d" % d, [128, T], BF16) for d in range(2)]
        Qh = [sb("hQh%d" % d, [128, T], BF16) for d in range(2)]
        Khf = sb("hKhf", [128, T], BF16)
        Kh = [sb("hKh%d" % d, [64, NCH, 128], BF16) for d in range(2)]
        Vt = sb("hV", [64, NCH, 128], BF16)
        dec = [sb("hdec%d" % d, [128, NCH], F32) for d in range(2)]
        cm = [sb("hcm%d" % d, [128, T], BF16) for d in range(2)]
        tri = [sb("htri%d" % d, [64, 64], F32) for d in range(2)]
        St = [sb("hS%d" % d, [128, 128], F32) for d in range(2)]
        Sb = [sb("hSb%d" % d, [128, 128], BF16) for d in range(2)]
        aT = [sb("haT%d_%d" % (d, k), [64, 64], BF16) for d in range(2) for k in range(2)]
        par = sb("hpar", [128, 8], F32)
        rstd = sb("hrstd", [128, 512], F32)
        ob16 = Qt[0]
        pa = [pst("hpa%d" % k, [64, 64], F32) for k in range(2)]
        po = [pst("hpo%d" % d, [128, 512], F32) for d in range(2)]
        pss = [pst("hps%d" % d, [128, 128], F32) for d in range(2)]
        ptr = pst("hptr", [64, 1024], BF16)
        pn = pst("hpn", [128, 512], F32)
        Bq, Bz, BA, BB, BC = P.B("hq"), P.B("hz"), P.B("hA"), P.B("hB"), P.B("hC")
        Bc = P.B("hconst")
        Bpar = P.B("hpar")
        P.dma("pool", cm[0][:], cmask_f[0].partition_broadcast(128), [], [Bc], "hc0")
        P.dma("pool", cm[1][:], cmask_b[0].partition_broadcast(128), [], [Bc], "hc1")
        P.dma("sp", tri[0][:], tri_f, [], [Bc], "hc2")
        P.dma("sp", tri[1][:], tri_b, [], [Bc], "hc3")
        for hd in range(4):
            f0 = hd * 128
            for d in range(2):
                P.dma("sp", par[:, d:d + 1], hgrn_lb[0, d, f0:f0 + 128].rearrange("(p o) -> p o", o=1), [], [Bpar], "hp0")
                P.dma("sp", par[:, 2 + d:3 + d], hgrn_lb[1, d, f0:f0 + 128].rearrange("(p o) -> p o", o=1), [], [Bpar], "hp1")
            P.dma("sp", par[:, 6:7], hgrn_onorm[idx].rearrange("(p o) -> p o", o=1), [], [Bpar], "hp2")
            if idx == 0:
                P.op("pool", lambda e: e.memset(par[:, 2:4], 0.0), [Bpar], [Bpar])
            else:
                P.op("dve", lambda e: e.tensor_tensor(out=par[:, 2:4], in0=par[:, 2:4], in1=par[:, 0:2], op=ALU.subtract),
                     [Bpar], [Bpar])
                P.op("act", lambda e: e.activation(out=par[:, 2:4], in_=par[:, 2:4], func=AF.Sigmoid), [Bpar], [Bpar])
            P.op("dve", lambda e: e.tensor_scalar(out=par[:, 4:6], in0=par[:, 2:4], scalar1=-1.0, scalar2=1.0,
                                                  op0=ALU.mult, op1=ALU.add), [Bpar], [Bpar])
            for s in range(spc):
                Bzall = [P.B("hz", blk) for blk in range(4)]
                P.dma("sp", zin[:], zr[s, 0, f0:f0 + 128, :], [], Bzall, "hz")
                P.op("act", lambda e: e.activation(out=qs[:], in_=zin[:], func=AF.Silu), Bzall, [Bq])
                P.dma("pool", Vt[:], zi[s, :, f0:f0 + 128].rearrange("(c p) v -> p c v", p=64), [], [P.B("hV")], "hv")
                NBLK = 4
                BW = T // NBLK
                CPB = BW // 64
                for d in range(2):
                    BQt, BKt, BQh, BKh, Bdec = (P.B("hQt", d), P.B("hKt", d), P.B("hQh", d), P.B("hKh", d),
                                                P.B("hdec", d))
                    mid = 31 if d == 0 else 32
                    last = 63 if d == 0 else 0
                    for blk in range(NBLK):
                        P.dma("sp", zin[:, blk * BW:(blk + 1) * BW], zr[s, 1 + d, f0:f0 + 128, blk * BW:(blk + 1) * BW],
                              [], [P.B("hz", blk)], "hz%d" % blk)

                    def v3(tile_, blk):
                        return tile_[:, blk * BW:(blk + 1) * BW].rearrange("p (c t) -> p c t", t=64)

                    def bl(tile_, blk):
                        return tile_[:, blk * BW:(blk + 1) * BW]

                    ops = []
                    ops.append(("act", lambda blk: (lambda e: e.activation(out=bl(tA, blk), in_=bl(zin, blk), func=AF.Sigmoid)),
                                lambda blk: [P.B("hz", blk)], lambda blk: [P.B("hA", blk)]))
                    ops.append(("dve", lambda blk, d=d: (lambda e: e.tensor_scalar(
                        out=bl(tA, blk), in0=bl(tA, blk), scalar1=par[:, 4 + d:5 + d], scalar2=par[:, 2 + d:3 + d],
                        op0=ALU.mult, op1=ALU.add)), lambda blk: [P.B("hA", blk), Bpar], lambda blk: [P.B("hA", blk)]))
                    ops.append(("act", lambda blk: (lambda e: e.activation(out=bl(tB, blk), in_=bl(tA, blk), func=AF.Ln)),
                                lambda blk: [P.B("hA", blk)], lambda blk: [P.B("hB", blk)]))
                    ops.append(("dve", lambda blk: (lambda e: e.tensor_scalar(
                        out=bl(tA, blk), in0=bl(tA, blk), scalar1=-1.0, scalar2=1.0, op0=ALU.mult, op1=ALU.add)),
                        lambda blk: [P.B("hA", blk)], lambda blk: [P.B("hA", blk)]))
                    if d == 0:
                        ops.append(("dve", lambda blk: (lambda e: e.tensor_tensor_scan(
                            out=bl(zin, blk), data0=bl(cm[0], blk), data1=bl(tB, blk), initial=0.0,
                            op0=ALU.mult, op1=ALU.add)), lambda blk: [P.B("hB", blk), Bc, P.B("hz", blk)],
                            lambda blk: [P.B("hz", blk)]))
                    else:
                        ops.append(("dve", lambda blk: (lambda e: e.tensor_tensor_scan(
                            out=bl(zin, blk)[:, ::-1], data0=bl(cm[1], blk)[:, ::-1], data1=bl(tB, blk)[:, ::-1],
                            initial=0.0, op0=ALU.mult, op1=ALU.add)), lambda blk: [P.B("hB", blk), Bc, P.B("hz", blk)],
                            lambda blk: [P.B("hz", blk)]))
                    ops.append(("dve", lambda blk, mid=mid: (lambda e: e.tensor_tensor(
                        out=v3(tB, blk), in0=v3(zin, blk), in1=v3(zin, blk)[:, :, mid:mid + 1].to_broadcast([128, CPB, 64]),
                        op=ALU.subtract)), lambda blk: [P.B("hz", blk), P.B("hB", blk)], lambda blk: [P.B("hB", blk)]))
                    ops.append(("act", lambda blk: (lambda e: e.activation(out=bl(tC, blk), in_=bl(tB, blk), func=AF.Exp)),
                                lambda blk: [P.B("hB", blk)], lambda blk: [P.B("hC", blk)]))
                    ops.append(("dve", lambda blk, d=d: (lambda e: e.tensor_tensor(
                        out=bl(Qt[d], blk), in0=bl(qs, blk), in1=bl(tC, blk), op=ALU.mult)),
                        lambda blk: [Bq, P.B("hC", blk)], lambda blk: [P.B("hQt", d)]))
                    ops.append(("act", lambda blk: (lambda e: e.activation(out=bl(tC, blk), in_=bl(tB, blk), func=AF.Exp,
                                                                           scale=-1.0)),
                                lambda blk: [P.B("hB", blk), P.B("hC", blk)], lambda blk: [P.B("hC", blk)]))
                    ops.append(("dve", lambda blk, d=d: (lambda e: e.tensor_tensor(
                        out=bl(Kt[d], blk), in0=bl(tA, blk), in1=bl(tC, blk), op=ALU.mult)),
                        lambda blk: [P.B("hA", blk), P.B("hC", blk)], lambda blk: [P.B("hKt", d)]))
                    ops.append(("act", lambda blk: (lambda e: e.activation(out=bl(tC, blk), in_=bl(zin, blk), func=AF.Exp)),
                                lambda blk: [P.B("hz", blk), P.B("hC", blk)], lambda blk: [P.B("hC", blk)]))
                    ops.append(("dve", lambda blk, d=d: (lambda e: e.tensor_tensor(
                        out=bl(Qh[d], blk), in0=bl(qs, blk), in1=bl(tC, blk), op=ALU.mult)),
                        lambda blk: [Bq, P.B("hC", blk)], lambda blk: [P.B("hQh", d)]))
                    ops.append(("dve", lambda blk, d=d, last=last: (lambda e: e.tensor_copy(
                        out=dec[d][:, blk * CPB:(blk + 1) * CPB], in_=v3(tC, blk)[:, :, last])),
                        lambda blk: [P.B("hC", blk)], lambda blk: [Bdec]))
                    ops.append(("dve", lambda blk, last=last: (lambda e: e.tensor_tensor(
                        out=v3(tB, blk), in0=v3(zin, blk), in1=v3(zin, blk)[:, :, last:last + 1].to_broadcast([128, CPB, 64]),
                        op=ALU.subtract)), lambda blk: [P.B("hz", blk), P.B("hB", blk)], lambda blk: [P.B("hB", blk)]))
                    ops.append(("act", lambda blk: (lambda e: e.activation(out=bl(tC, blk), in_=bl(tB, blk), func=AF.Exp,
                                                                           scale=-1.0)),
                                lambda blk: [P.B("hB", blk), P.B("hC", blk)], lambda blk: [P.B("hC", blk)]))
                    ops.append(("dve", lambda blk: (lambda e: e.tensor_tensor(
                        out=bl(Khf, blk), in0=bl(tA, blk), in1=bl(tC, blk), op=ALU.mult)),
                        lambda blk: [P.B("hA", blk), P.B("hC", blk)], lambda blk: [P.B("hKhf", blk)]))
                    for (eng, mk, rd, wr) in ops:
                        for blk in range(NBLK):
                            P.op(eng, mk(blk), rd(blk), wr(blk))
                    Bptr = P.B("hptr")
                    for c8 in range(8):
                        for cc in range(8):
                            c = c8 * 8 + cc
                            P.op("pe", lambda e, c=c, cc=cc: e.transpose(ptr[:, cc * 128:(cc + 1) * 128],
                                                                         Khf[:, c * 64:(c + 1) * 64], ident[:]),
                                 [P.B("hKhf", c8 // 2)], [Bptr])
                        P.op("act", lambda e, d=d, c8=c8: e.copy(out=Kh[d][:, c8 * 8:(c8 + 1) * 8, :],
                                                                 in_=ptr[:].rearrange("p (c v) -> p c v", c=8)),
                             [Bptr], [BKh])
                BAall = [P.B("hA", blk) for blk in range(4)]
                BBall = [P.B("hB", blk) for blk in range(4)]
                Bzall = [P.B("hz", blk) for blk in range(4)]
                BKfall = [P.B("hKhf", blk) for blk in range(4)]
                BO = [BAall, BBall]
                osb = [tA, tB]
                for d in range(2):
                    P.op("pool", lambda e, d=d: e.memset(St[d][:], 0.0), [], [P.B("hS", d)])
                    P.op("pool", lambda e, d=d: e.memset(Sb[d][:], 0.0), [], [P.B("hSb", d)])
                def chunk_of(step, d):
                    return step if d == 0 else NCH - 1 - step

                def emit_pa(step, d):
                    c = chunk_of(step, d)
                    k = step % 2
                    cs = slice(c * 64, (c + 1) * 64)
                    Bpa, BaT = P.B("hpa", d), P.B("haT", d, k)
                    P.op("pe", lambda e, d=d, cs=cs: e.matmul(pa[d][:], lhsT=Kt[d][:, cs], rhs=Qt[d][:, cs],
                                                              start=True, stop=True),
                         [P.B("hKt", d), P.B("hQt", d)], [Bpa])
                    P.op("dve", lambda e, d=d, k=k: e.tensor_tensor(out=aT[d * 2 + k][:], in0=pa[d][:], in1=tri[d][:],
                                                                    op=ALU.mult), [Bpa, Bc], [BaT])

                for d in range(2):
                    emit_pa(0, d)
                for step in range(NCH):
                    if step + 1 < NCH:
                        for d in range(2):
                            emit_pa(step + 1, d)
                    for d in range(2):
                        c = chunk_of(step, d)
                        k = step % 2
                        BaT, Bpo = P.B("haT", d, k), P.B("hpo", d)
                        slot = (step % 8) if d == 0 else 7 - (step % 8)
                        P.op("pe", lambda e, d=d, k=k, c=c, slot=slot: e.matmul(
                            po[d][:, slot * 64:(slot + 1) * 64], lhsT=Vt[:, c, :], rhs=aT[d * 2 + k][:],
                            start=True, stop=False), [P.B("hV"), BaT], [Bpo])
                    for d in range(2):
                        c = chunk_of(step, d)
                        Bpo, Bps = P.B("hpo", d), P.B("hps", d)
                        BS, BSb = P.B("hS", d), P.B("hSb", d)
                        cs = slice(c * 64, (c + 1) * 64)
                        slot = (step % 8) if d == 0 else 7 - (step % 8)
                        P.op("pe", lambda e, d=d, cs=cs, slot=slot: e.matmul(
                            po[d][:, slot * 64:(slot + 1) * 64], lhsT=Sb[d][:], rhs=Qh[d][:, cs],
                            start=False, stop=True), [BSb, P.B("hQh", d)], [Bpo])
                        P.op("pe", lambda e, d=d, c=c: e.matmul(pss[d][:], lhsT=Kh[d][:, c, :], rhs=Vt[:, c, :],
                                                                start=True, stop=True),
                             [P.B("hKh", d), P.B("hV")], [Bps])
                        P.op("dve", lambda e, d=d, c=c: e.scalar_tensor_tensor(out=St[d][:], in0=St[d][:],
                                                                               scalar=dec[d][:, c:c + 1], in1=pss[d][:],
                                                                               op0=ALU.mult, op1=ALU.add),
                             [BS, Bps, P.B("hdec", d)], [BS])
                        P.op("act", lambda e, d=d: e.copy(out=Sb[d][:], in_=St[d][:]), [BS], [BSb])
                        if step % 8 == 7:
                            g8 = step // 8
                            t0 = g8 * 512 if d == 0 else (7 - g8) * 512
                            P.op("act", lambda e, d=d, t0=t0: e.copy(out=osb[d][:, t0:t0 + 512], in_=po[d][:]),
                                 [Bpo], BO[d])
                P.op("dve", lambda e: e.tensor_tensor(out=tA[:], in0=tA[:], in1=tB[:], op=ALU.add), BAall + BBall, BAall)
                P.op("act", lambda e: e.activation(out=ob16[:], in_=tA[:], func=AF.Square), BAall, [P.B("hQt", 0)])
                P.dma("sp", zin[:], zr[s, 3, f0:f0 + 128, :], [], Bzall, "hz")
                P.op("act", lambda e: e.activation(out=tB[:], in_=zin[:], func=AF.Silu), Bzall + BBall, BBall)
                Bpn, Brs = P.B("hpn"), P.B("hrstd")
                for blk in range(8):
                    bs = slice(blk * 512, (blk + 1) * 512)
                    P.op("pe", lambda e, bs=bs: e.matmul(pn[:], lhsT=ones_b[:], rhs=ob16[:, bs], start=True, stop=True),
                         [P.B("hQt", 0)], [Bpn])
                    P.op("dve", lambda e: e.tensor_scalar(out=rstd[:], in0=pn[:], scalar1=1.0 / 128, scalar2=EPS,
                                                          op0=ALU.mult, op1=ALU.add), [Bpn], [Brs])
                    P.op("act", lambda e: e.activation(out=rstd[:], in_=rstd[:], func=AF.Sqrt), [Brs], [Brs])
                    P.op("dve", lambda e: e.reciprocal(out=rstd[:], in_=rstd[:]), [Brs], [Brs])
                    P.op("dve", lambda e, bs=bs: e.tensor_tensor(out=tA[:, bs], in0=tA[:, bs], in1=rstd[:], op=ALU.mult),
                         BAall + [Brs], BAall)
                P.op("dve", lambda e: e.scalar_tensor_tensor(out=Khf[:], in0=tA[:], scalar=par[:, 6:7], in1=tB[:],
                                                             op0=ALU.mult, op1=ALU.mult), BAall + BBall + [Bpar], BKfall)
                P.dma("sp", mixT[s, f0:f0 + 128, :], Khf[:], BKfall, [], "hout")
        P.barrier()
        P.emit()


def mixer_rglru(P, nc, spc, idx, zr, mixT, conv_w, conv_b, rg_wa, rg_ba, rg_wx, rg_bx, rg_lambda):
    with ExitStack() as ph:
        def sb(name, shape, dt):
            return ph.enter_context(nc.sbuf_tensor(_uniq(name), list(shape), dt))

        def pst(name, shape, dt):
            return ph.enter_context(nc.psum_tensor(_uniq(name), list(shape), dt))

        xp = sb("rxp", [128, T + 4], F32)
        yy = sb("ry", [128, T], F32)
        yb = sb("ryb", [128, T], BF16)
        rr = sb("rr", [128, T], F32)
        ii = sb("ri", [128, T], F32)
        a1 = sb("ra", [128, T], F32)
        t1 = sb("rt", [128, T], F32)
        hh = [sb("rh%d" % d, [128, T], F32) for d in range(2)]
        ob = sb("rob", [128, T], BF16)
        wbd32 = sb("rw32", [128, 4, 128], F32)
        wbd = sb("rwbd", [128, 4, 128], BF16)
        par = sb("rpar", [128, 24], F32)
        pm = [pst("rpm%d" % i, [128, 512], F32) for i in range(4)]
        Bxp, By, Byb, Br, Bi, Ba, Bt, Bob = (P.B("rxp"), P.B("ry"), P.B("ryb"), P.B("rr"), P.B("ri"), P.B("ra"),
                                             P.B("rt"), P.B("rob"))
        Bh = [P.B("rh", d) for d in range(2)]
        Bw, Bpar = P.B("rw"), P.B("rpar")
        P.op("pool", lambda e: e.memset(xp[:], 0.0), [], [Bxp])
        col = lambda ap: ap.rearrange("(p o) -> p o", o=1)
        ec = 0
        for ch in range(4):
            c0 = ch * 128
            for k in range(4):
                P.dma("sp", par[:, k:k + 1], col(conv_w[idx, k, c0:c0 + 128]), [], [Bpar], "rp0")
            P.dma("sp", par[:, 4:5], col(conv_b[idx, c0:c0 + 128]), [], [Bpar], "rp1")
            for d in range(2):
                P.dma("sp", par[:, 5 + d:6 + d], col(rg_ba[idx, d, c0:c0 + 128]), [], [Bpar], "rp2")
                P.dma("sp", par[:, 7 + d:8 + d], col(rg_bx[idx, d, c0:c0 + 128]), [], [Bpar], "rp3")
                P.dma("sp", par[:, 9 + d:10 + d], col(rg_lambda[idx, d, c0:c0 + 128]), [], [Bpar], "rp4")
            P.op("act", lambda e: e.activation(out=par[:, 9:11], in_=par[:, 9:11], func=AF.Exp, scale=-1.0), [Bpar], [Bpar])
            P.op("dve", lambda e: e.tensor_scalar(out=par[:, 9:11], in0=par[:, 9:11], scalar1=1.0, scalar2=None, op0=ALU.add),
                 [Bpar], [Bpar])
            P.op("act", lambda e: e.activation(out=par[:, 9:11], in_=par[:, 9:11], func=AF.Ln), [Bpar], [Bpar])
            P.op("dve", lambda e: e.tensor_scalar(out=par[:, 11:13], in0=par[:, 9:11], scalar1=-16.0, scalar2=None,
                                                  op0=ALU.mult), [Bpar], [Bpar])
            P.op("dve", lambda e: e.tensor_scalar(out=par[:, 9:11], in0=par[:, 9:11], scalar1=-8.0, scalar2=None,
                                                  op0=ALU.mult), [Bpar], [Bpar])
            P.op("pool", lambda e: e.memset(wbd32[:], 0.0), [], [Bw])
            for d in range(2):
                for m, wsrc in enumerate((rg_wa, rg_wx)):
                    for blk in range(2):
                        P.dma("sp", wbd32[blk * 64:(blk + 1) * 64, d * 2 + m, blk * 64:(blk + 1) * 64],
                              wsrc[idx, d, ch * 2 + blk], [], [Bw], "rw")
            P.op("dve", lambda e: e.tensor_copy(out=wbd[:], in_=wbd32[:]), [Bw], [Bw])
            for s in range(spc):
                P.dma("sp", xp[:, 2:T + 2], zr[s, 1, c0:c0 + 128, :], [], [Bxp], "rx")
                P.op("dve", lambda e: e.tensor_scalar(out=yy[:], in0=xp[:, 0:T], scalar1=par[:, 0:1], scalar2=par[:, 4:5],
                                                      op0=ALU.mult, op1=ALU.add), [Bxp, Bpar], [By])
                for k in range(1, 4):
                    P.op("dve", lambda e, k=k: e.scalar_tensor_tensor(out=yy[:], in0=xp[:, k:k + T], scalar=par[:, k:k + 1],
                                                                      in1=yy[:], op0=ALU.mult, op1=ALU.add),
                         [Bxp, Bpar, By], [By])
                P.op("act", lambda e: e.copy(out=yb[:], in_=yy[:]), [By], [Byb])
                for d in range(2):
                    for blk in range(8):
                        bs = slice(blk * 512, (blk + 1) * 512)
                        for m, (dst, Bd, bc) in enumerate(((rr, Br, 5 + d), (ii, Bi, 7 + d))):
                            pb = ec % 4
                            ec += 1
                            Bpm = P.B("rpm", pb)
                            P.op("pe", lambda e, pb=pb, d=d, m=m, bs=bs: e.matmul(pm[pb][:], lhsT=wbd[:, d * 2 + m, :],
                                                                                  rhs=yb[:, bs], start=True, stop=True),
                                 [Bw, Byb], [Bpm])
                            P.op("act", lambda e, pb=pb, dst=dst, bs=bs, bc=bc: e.activation(
                                out=dst[:, bs], in_=pm[pb][:], func=AF.Sigmoid, bias=par[:, bc:bc + 1]),
                                [Bpm, Bpar], [Bd])
                    P.op("act", lambda e, d=d: e.activation(out=a1[:], in_=rr[:], func=AF.Exp, scale=par[:, 9 + d:10 + d]),
                         [Br, Bpar], [Ba])
                    P.op("act", lambda e, d=d: e.activation(out=t1[:], in_=rr[:], func=AF.Exp, scale=par[:, 11 + d:12 + d]),
                         [Br, Bpar], [Bt])
                    P.op("dve", lambda e: e.tensor_scalar(out=t1[:], in0=t1[:], scalar1=-1.0, scalar2=1.0,
                                                          op0=ALU.mult, op1=ALU.add), [Bt], [Bt])
                    P.op("act", lambda e: e.activation(out=t1[:], in_=t1[:], func=AF.Sqrt), [Bt], [Bt])
                    first = 0 if d == 0 else T - 1
                    P.op("pool", lambda e, first=first: e.memset(t1[:, first:first + 1], 1.0), [Bt], [Bt])
                    P.op("dve", lambda e: e.tensor_tensor(out=t1[:], in0=t1[:], in1=ii[:], op=ALU.mult), [Bt, Bi], [Bt])
                    P.op("dve", lambda e: e.tensor_tensor(out=t1[:], in0=t1[:], in1=yy[:], op=ALU.mult), [Bt, By], [Bt])
                    if d == 0:
                        P.op("dve", lambda e: e.tensor_tensor_scan(out=hh[0][:], data0=a1[:], data1=t1[:], initial=0.0,
                                                                   op0=ALU.mult, op1=ALU.add), [Ba, Bt], [Bh[0]])
                    else:
                        P.op("dve", lambda e: e.tensor_tensor_scan(out=hh[1][:, ::-1], data0=a1[:, ::-1],
                                                                   data1=t1[:, ::-1], initial=0.0,
                                                                   op0=ALU.mult, op1=ALU.add), [Ba, Bt], [Bh[1]])
                P.dma("sp", rr[:], zr[s, 0, c0:c0 + 128, :], [Br], [Br], "rg")
                P.op("act", lambda e: e.activation(out=ii[:], in_=rr[:], func=AF.Gelu_apprx_tanh), [Br, Bi], [Bi])
                P.op("dve", lambda e: e.tensor_tensor(out=hh[0][:], in0=hh[0][:], in1=hh[1][:], op=ALU.add),
                     [Bh[0], Bh[1]], [Bh[0]])
                P.op("dve", lambda e: e.tensor_tensor(out=ob[:], in0=hh[0][:], in1=ii[:], op=ALU.mult), [Bh[0], Bi], [Bob])
                P.dma("sp", mixT[s, c0:c0 + 128, :], ob[:], [Bob], [], "rout")
        P.barrier()
        P.emit()


def host_tables(na_rpb, t5_bias):
    hidx = np.arange(8)[None, :, None, None]
    na_tab = np.ascontiguousarray(
        np.stack([na_rpb[l][hidx, NA_DR[:, None], NA_DC[:, None]] for l in range(2)]).astype(np.float32))
    t5_tab = np.ascontiguousarray(np.transpose(t5_bias[T5_BK], (0, 3, 1, 2)).astype(np.float32))
    t = np.arange(T)
    consts = {
        "na_mask": NA_MASK, "t5_cnt": T5_CNT,
        "cmask_f": (t % 64 != 0).astype(np.float32)[None, :],
        "cmask_b": (t % 64 != 63).astype(np.float32)[None, :],
        "tri_f": (np.arange(64)[:, None] <= np.arange(64)[None, :]).astype(np.float32),
        "tri_b": (np.arange(64)[:, None] >= np.arange(64)[None, :]).astype(np.float32),
    }
    return na_tab, t5_tab, consts


_NC_CACHE = {}


def run(inputs, xs_per_core, layers=(0, 1, 2, 3), spc=SPC, debug=False, ncores=NCORES, trace=False):
    key = (tuple(layers), spc, debug)
    if key not in _NC_CACHE:
        _NC_CACHE[key] = build(layers, spc, debug)
    nc = _NC_CACHE[key]
    na_tab, t5_tab, consts = host_tables(np.asarray(inputs["na_rpb"]), np.asarray(inputs["t5_bias"]))
    shared = {k: np.ascontiguousarray(np.asarray(inputs[k], dtype=np.float32)) for k in (
        "norm_g", "w_in_even", "w_out_even", "w_in_odd", "w_out_odd", "w_gate", "w_up", "w_down", "hgrn_lb",
        "hgrn_onorm", "conv_w", "conv_b", "rg_wa", "rg_ba", "rg_wx", "rg_bx", "rg_lambda")}
    shared["na_tab"] = na_tab
    shared["t5_tab"] = t5_tab
    shared.update(consts)
    in_maps = []
    for c in range(ncores):
        m = dict(shared)
        m["x"] = xs_per_core[c]
        in_maps.append(m)
    res = run_bass_kernel_spmd(nc, in_maps, core_ids=list(range(ncores)), **({"trace": True} if trace else {}))
    return res


def kernel(**inputs):
    xp = np.asarray(inputs["x_prompt"], dtype=np.float32)
    xs = np.asarray(inputs["x_sample"], dtype=np.float32)
    allx = np.concatenate([xp, xs], axis=0)
    nseq = allx.shape[0]
    slots = np.zeros((NCORES * SPC, T, D), np.float32)
    slots[:nseq] = allx
    xs_per_core = [np.ascontiguousarray(slots[c * SPC:(c + 1) * SPC]) for c in range(NCORES)]
    res = run(inputs, xs_per_core)
    yall = np.concatenate([r["y"] for r in res.results], axis=0)[:nseq]
    return (np.ascontiguousarray(yall[:xp.shape[0]]), np.ascontiguousarray(yall[xp.shape[0]:]))
```

```python
import numpy as np
from contextlib import ExitStack
import concourse.bass as bass
import concourse.mybir as mybir
from concourse.bass_utils import run_bass_kernel_spmd

F32 = mybir.dt.float32
BF16 = mybir.dt.bfloat16
AF = mybir.ActivationFunctionType
ALU = mybir.AluOpType

T = 4096
D = 1024
NT = 32
DFF = 2816
EPS = 1e-6
SPC = 3
NCORES = 8
NEG = -30000.0


class Buf:
    __slots__ = ("w", "r")

    def __init__(self):
        self.w = None
        self.r = []


class Prog:
    def __init__(self, nc, stack):
        self.nc = nc
        self.stack = stack
        self.names = ("pe", "act", "dve", "pool", "sp")
        self.lists = {k: [] for k in self.names}
        self.esem = {}
        self.ecnt = {k: 0 for k in self.names}
        for k in ("pe", "act", "dve", "pool"):
            self.esem[k] = stack.enter_context(nc.semaphore("s_" + k))
        self.seen = {k: {} for k in self.names}
        self.bufs = {}
        self.chans = {}
        self.ninst = 0

    def B(self, *key):
        b = self.bufs.get(key)
        if b is None:
            b = self.bufs[key] = Buf()
        return b

    def chan(self, name):
        c = self.chans.get(name)
        if c is None:
            c = self.chans[name] = [self.stack.enter_context(self.nc.semaphore("c_" + name)), 0]
        return c

    def _deps(self, e, reads, writes):
        evs = []
        for b in reads:
            if b.w is not None:
                evs.append(b.w)
        for b in writes:
            if b.w is not None:
                evs.append(b.w)
            evs.extend(b.r)
        waits = {}
        seen = self.seen[e]
        for (sem, val, src) in evs:
            if src == "pe" and e == "pe":
                continue
            k = id(sem)
            if seen.get(k, 0) >= val:
                continue
            if k not in waits or waits[k][1] < val:
                waits[k] = (sem, val)
        for k, (sem, val) in waits.items():
            seen[k] = val
        return list(waits.values())

    def _commit(self, ev, reads, writes):
        for b in reads:
            b.r.append(ev)
        for b in writes:
            b.w = ev
            b.r = []

    def op(self, e, fn, reads=(), writes=()):
        waits = self._deps(e, reads, writes)
        self.ecnt[e] += 1
        ev = (self.esem[e], self.ecnt[e], e)
        self.lists[e].append((waits, fn, self.esem[e], 1))
        self._commit(ev, reads, writes)
        self.ninst += 1

    def dma(self, q, out, in_, reads, writes, chan, **kw):
        c = self.chan(chan)
        waits = self._deps(q, reads, writes)
        if c[1] > 0:
            k = id(c[0])
            if self.seen[q].get(k, 0) < c[1]:
                waits.append((c[0], c[1]))
                self.seen[q][k] = c[1]
        c[1] += 16
        ev = (c[0], c[1], "dma")
        self.lists[q].append((waits, lambda eng: eng.dma_start(out=out, in_=in_, **kw), c[0], 16))
        self._commit(ev, reads, writes)
        self.ninst += 1

    def barrier(self):
        for e in self.names:
            seen = self.seen[e]
            waits = []
            for k in ("pe", "act", "dve", "pool"):
                if k == e:
                    continue
                s, v = self.esem[k], self.ecnt[k]
                if v > 0 and seen.get(id(s), 0) < v:
                    waits.append((s, v))
                    seen[id(s)] = v
            for name, (s, v) in self.chans.items():
                if v > 0 and seen.get(id(s), 0) < v:
                    waits.append((s, v))
                    seen[id(s)] = v
            if waits:
                self.lists[e].append((waits, None, None, 0))
        for b in self.bufs.values():
            b.w = None
            b.r = []

    def emit(self):
        nc = self.nc
        lists = self.lists
        with nc.Block() as block:
            def mk(name):
                def body(eng):
                    for (waits, fn, sem, inc) in lists[name]:
                        for (s, v) in waits:
                            eng.wait_ge(s, v)
                        if fn is not None:
                            fn(eng).then_inc(sem, inc)
                return body
            block.tensor(mk("pe"))
            block.scalar(mk("act"))
            block.vector(mk("dve"))
            block.gpsimd(mk("pool"))
            block.sync(mk("sp"))
        self.lists = {k: [] for k in self.names}


def na_patterns():
    kk = np.arange(128)
    rk_l, ck = kk // 64, kk % 64
    pats = {}
    pairs = []
    drs, dcs, masks = [], [], []
    for j in range(NT):
        lst = []
        for i in range(NT):
            rq = 2 * j + rk_l[None, :]
            cq = ck[None, :]
            rk = 2 * i + rk_l[:, None]
            ckk = ck[:, None]
            start = np.clip(rq - 4, 0, 56)
            visr = (rk >= start) & (rk < start + 8)
            cs = np.clip(cq - 8, 0, 48)
            visc = (ckk >= cs) & (ckk < cs + 16)
            vis = visr & visc
            if not vis.any():
                continue
            dr = np.clip(rk - rq + 7, 0, 14) + 0 * cq
            dc = np.clip(ckk - cq + 15, 0, 30) + 0 * rq
            dr = np.where(vis, dr, 0)
            dc = np.where(vis, dc, 0)
            key = (vis.tobytes(), dr.astype(np.int8).tobytes(), dc.astype(np.int8).tobytes())
            if key not in pats:
                pats[key] = len(pats)
                drs.append(dr)
                dcs.append(dc)
                masks.append(np.where(vis, 0.0, NEG).astype(np.float32))
            lst.append((i, pats[key]))
        pairs.append(lst)
    return pairs, np.stack(drs), np.stack(dcs), np.stack(masks)


def t5_patterns():
    kk = np.arange(128)
    bks, cnts = [], []
    for dlt in range(-8, 9):
        d = 128 * dlt + kk[:, None] - kk[None, :]
        n = np.abs(d)
        cnt = (n <= 64).astype(np.int32) + ((d % 4 == 0) & (n <= 256)) + ((d % 16 == 0) & (n <= 1024))
        large = 8 + (np.log(np.maximum(n, 1).astype(np.float32) / np.float32(8)) / np.float32(np.log(1024 / 8)) * 8).astype(np.int32)
        large = np.minimum(large, 15)
        bk = np.where(d > 0, 16, 0) + np.where(n < 8, n, large)
        bks.append(bk)
        cnts.append(np.where(cnt > 0, np.log(np.maximum(cnt, 1)), NEG).astype(np.float32))
    return np.stack(bks), np.stack(cnts)


_UID = [0]


def _uniq(name):
    _UID[0] += 1
    return "%s_u%d" % (name, _UID[0])


NA_PAIRS, NA_DR, NA_DC, NA_MASK = na_patterns()
NPAT = NA_MASK.shape[0]
T5_BK, T5_CNT = t5_patterns()


def build(layers=(0, 1, 2, 3), spc=SPC, debug=False):
    nc = bass.Bass("TRN2", target_bir_lowering=False)

    def din(name, shape, dt=F32):
        return nc.dram_tensor(name, list(shape), dt, kind="ExternalInput").ap()

    def dscr(name, shape, dt):
        return nc.dram_tensor(name, list(shape), dt, kind="ExternalOutput" if debug else "Internal").ap()

    x_in = din("x", [spc, T, D])
    norm_g = din("norm_g", [4, 4, D])
    w_in_even = din("w_in_even", [2, D, 4096])
    w_out_even = din("w_out_even", [2, D, D])
    w_in_odd = din("w_in_odd", [2, D, 2560])
    w_out_odd = din("w_out_odd", [2, D, D])
    w_gate = din("w_gate", [4, D, DFF])
    w_up = din("w_up", [4, D, DFF])
    w_down = din("w_down", [4, DFF, D])
    hgrn_lb = din("hgrn_lb", [2, 2, 512])
    hgrn_onorm = din("hgrn_onorm", [2, 128])
    conv_w = din("conv_w", [2, 4, 512])
    conv_b = din("conv_b", [2, 512])
    rg_wa = din("rg_wa", [2, 2, 8, 64, 64])
    rg_ba = din("rg_ba", [2, 2, 512])
    rg_wx = din("rg_wx", [2, 2, 8, 64, 64])
    rg_bx = din("rg_bx", [2, 2, 512])
    rg_lambda = din("rg_lambda", [2, 2, 512])
    na_tab = din("na_tab", [2, NPAT, 8, 128, 128])
    na_mask = din("na_mask", [NPAT, 128, 128])
    t5_tab = din("t5_tab", [17, 8, 128, 128])
    t5_cnt = din("t5_cnt", [17, 128, 128])
    cmask_f = din("cmask_f", [1, T])
    cmask_b = din("cmask_b", [1, T])
    tri_f = din("tri_f", [64, 64])
    tri_b = din("tri_b", [64, 64])
    y_out = nc.dram_tensor("y", [spc, T, D], F32, kind="ExternalOutput").ap()

    xres = [dscr("xres0", [spc, T, D], F32), dscr("xres1", [spc, T, D], F32)]
    zqT = dscr("zqT", [spc, 512, T], BF16)
    zkT = dscr("zkT", [spc, 512, T], BF16)
    zv = dscr("zv", [spc, T, 512], BF16)
    zr = dscr("zr", [spc, 4, 512, T], F32)
    zi = dscr("zi", [spc, T, 512], F32)
    mix = dscr("mix", [spc, T, 512], BF16)
    mixT = dscr("mixT", [spc, 512, T], BF16)

    with ExitStack() as glob:
        P = Prog(nc, glob)

        def sb(stk, name, shape, dt):
            return stk.enter_context(nc.sbuf_tensor(_uniq(name), list(shape), dt))

        def pst(stk, name, shape, dt):
            return stk.enter_context(nc.psum_tensor(_uniq(name), list(shape), dt))

        identf = sb(glob, "identf", [128, 128], F32)
        ident = sb(glob, "ident", [128, 128], BF16)
        ones_b = sb(glob, "ones_b", [128, 128], BF16)
        Bid = P.B("ident")
        P.op("pool", lambda e: e.memset(identf[:], 0.0), [], [Bid])
        P.op("pool", lambda e: e.affine_select(out=identf[:], in_=identf[:], pattern=[[-1, 128]],
                                                compare_op=ALU.not_equal, fill=1.0, base=0,
                                                channel_multiplier=1), [Bid], [Bid])
        P.op("dve", lambda e: e.tensor_copy(out=ident[:], in_=identf[:]), [Bid], [Bid])
        P.op("pool", lambda e: e.memset(ones_b[:], 1.0), [], [Bid])
        P.barrier()
        P.emit()

        nlay = len(layers)
        for li, L in enumerate(layers):
            even = (L % 2 == 0)
            idx = L // 2
            xsrc = x_in if li == 0 else xres[(li - 1) % 2]
            xdst = y_out if li == nlay - 1 else xres[li % 2]
            with ExitStack() as ph:
                ncol = 4096 if even else 2560
                w_in = (w_in_even if even else w_in_odd)[idx]
                win = sb(ph, "win", [128, 8, ncol], BF16)
                g0 = sb(ph, "g0", [128, D], F32)
                Bw = P.B("win")
                wsrc = w_in.rearrange("(c p) n -> p c n", p=128)
                for dc in range(8):
                    P.dma("pool", win[:, dc, :], wsrc[:, dc, :], [], [Bw], "w%d" % dc)
                P.dma("sp", g0[:], norm_g[L, 0].partition_broadcast(128), [], [Bw], "gA")
                xt = [sb(ph, "xtA%d" % i, [128, D], F32) for i in range(4)]
                junk = sb(ph, "junkA", [128, D], F32)
                hb = [sb(ph, "hbA%d" % i, [128, D], BF16) for i in range(4)]
                hT = [sb(ph, "hTA%d" % i, [128, 8, 512], BF16) for i in range(2)]
                ss = sb(ph, "ssA", [128, 8], F32)
                stF = [sb(ph, "stF%d" % i, [128, 4, 512], F32) for i in range(4)]
                stH = [sb(ph, "stH%d" % i, [128, 4, 512], BF16) for i in range(4)]
                pT = [pst(ph, "pTA%d" % i, [128, 1024], BF16) for i in range(2)]
                pm = [pst(ph, "pmA%d" % i, [128, 512], F32) for i in range(4)]
                if even:
                    fm_parts = [(0, "qk", zqT, 0.125), (512, "qk", zkT, 1.0), (1536, "r", 0, 1.0),
                                (2048, "r", 1, 1.0), (2560, "r", 2, 1.0), (3584, "r", 3, 1.0)]
                    tm_parts = [(1024, "v"), (3072, "i")]
                else:
                    fm_parts = [(0, "r", 0, 1.0), (512, "r", 1, 1.0), (1024, "qk", zqT, 0.125),
                                (1536, "qk", zkT, 1.0)]
                    tm_parts = [(2048, "v")]
                hb = hb + [sb(ph, "hbA%d" % i, [128, D], BF16) for i in range(4, 8)]
                groups = [(s_, g_) for s_ in range(spc) for g_ in range(8)]
                ecnt_box = [0]
                scnt = {"F": 0, "H": 0}

                def stageN(gi):
                    s, g = groups[gi]
                    for i in range(4):
                        cnt = gi * 4 + i
                        b = cnt % 4
                        hbi = (gi % 2) * 4 + i
                        tok0 = g * 512 + i * 128
                        Bx, Bh, Bss = P.B("xtA", b), P.B("hbA", hbi), P.B("ssA", b)
                        Bj = P.B("junkA")
                        P.dma("sp", xt[b][:], xsrc[s, tok0:tok0 + 128, :], [], [Bx], "xA%d" % b)
                        sc = ss[:, 2 * b:2 * b + 1]
                        rs = ss[:, 2 * b + 1:2 * b + 2]
                        P.op("pool", lambda e, sc=sc: e.memset(sc, 0.0), [], [Bss])
                        P.op("act", lambda e, b=b, sc=sc: e.activation(out=junk[:], in_=xt[b][:], func=AF.Square,
                                                                       accum_out=sc), [Bx, Bss], [Bj, Bss])
                        P.op("dve", lambda e, sc=sc, rs=rs: e.tensor_scalar(out=rs, in0=sc, scalar1=1.0 / D, scalar2=EPS,
                                                                           op0=ALU.mult, op1=ALU.add), [Bss], [Bss])
                        P.op("act", lambda e, rs=rs: e.activation(out=rs, in_=rs, func=AF.Sqrt), [Bss], [Bss])
                        P.op("dve", lambda e, rs=rs: e.reciprocal(out=rs, in_=rs), [Bss], [Bss])
                        P.op("dve", lambda e, b=b, hbi=hbi, rs=rs: e.scalar_tensor_tensor(
                            out=hb[hbi][:], in0=xt[b][:], scalar=rs, in1=g0[:], op0=ALU.mult, op1=ALU.mult),
                            [Bx, Bss, Bw], [Bh])

                def stageM(gi):
                    s, g = groups[gi]
                    hTg = hT[gi % 2]
                    BhT = P.B("hTA", gi % 2)
                    for i in range(4):
                        cnt = gi * 4 + i
                        pb2 = cnt % 2
                        hbi = (gi % 2) * 4 + i
                        Bh, BpT = P.B("hbA", hbi), P.B("pTA", pb2)
                        for dc in range(8):
                            P.op("pe", lambda e, hbi=hbi, pb2=pb2, dc=dc: e.transpose(
                                pT[pb2][:, dc * 128:(dc + 1) * 128], hb[hbi][:, dc * 128:(dc + 1) * 128], ident[:]),
                                [Bh], [BpT])
                        P.op("act", lambda e, pb2=pb2, i=i, hTg=hTg: e.copy(
                            out=hTg[:, :, i * 128:(i + 1) * 128],
                            in_=pT[pb2][:].rearrange("p (c t) -> p c t", c=8)), [BpT], [BhT])
                    ecnt = ecnt_box[0]
                    for (c0, kind, dst, scale) in fm_parts:
                        if kind == "qk":
                            k = scnt["H"] % 4
                            scnt["H"] += 1
                            st, Bst, chn = stH[k], P.B("stH", k), "sH%d" % k
                            dst_ap = dst[s].rearrange("(c p) t -> p c t", p=128)[:, :, g * 512:(g + 1) * 512]
                        else:
                            k = scnt["F"] % 4
                            scnt["F"] += 1
                            st, Bst, chn = stF[k], P.B("stF", k), "sF%d" % k
                            dst_ap = zr[s, dst].rearrange("(c p) t -> p c t", p=128)[:, :, g * 512:(g + 1) * 512]
                        for fc in range(4):
                            pb = ecnt % 4
                            Bpm = P.B("pmA", pb)
                            col = c0 + fc * 128
                            for dc in range(8):
                                P.op("pe", lambda e, pb=pb, dc=dc, col=col, hTg=hTg: e.matmul(
                                    pm[pb][:], lhsT=win[:, dc, col:col + 128], rhs=hTg[:, dc, :],
                                    start=(dc == 0), stop=(dc == 7)), [Bw, BhT], [Bpm])
                            if ecnt % 2 == 0:
                                P.op("act", lambda e, st=st, fc=fc, pb=pb, scale=scale: e.activation(
                                    out=st[:, fc, :], in_=pm[pb][:], func=AF.Copy, scale=scale), [Bpm], [Bst])
                            else:
                                P.op("dve", lambda e, st=st, fc=fc, pb=pb, scale=scale: e.tensor_scalar(
                                    out=st[:, fc, :], in0=pm[pb][:], scalar1=scale, scalar2=None, op0=ALU.mult),
                                    [Bpm], [Bst])
                            ecnt += 1
                        P.dma("sp", dst_ap, st[:], [Bst], [], chn)
                    for (c0, kind) in tm_parts:
                        if kind == "v":
                            k = scnt["H"] % 4
                            scnt["H"] += 1
                            st, Bst, chn = stH[k], P.B("stH", k), "sH%d" % k
                            dst_ap = zv[s, g * 512:(g + 1) * 512, :].rearrange("(i p) n -> p i n", p=128)
                        else:
                            k = scnt["F"] % 4
                            scnt["F"] += 1
                            st, Bst, chn = stF[k], P.B("stF", k), "sF%d" % k
                            dst_ap = zi[s, g * 512:(g + 1) * 512, :].rearrange("(i p) n -> p i n", p=128)
                        for i in range(4):
                            pb = ecnt % 4
                            Bpm = P.B("pmA", pb)
                            for dc in range(8):
                                P.op("pe", lambda e, pb=pb, dc=dc, i=i, c0=c0, hTg=hTg: e.matmul(
                                    pm[pb][:], lhsT=hTg[:, dc, i * 128:(i + 1) * 128], rhs=win[:, dc, c0:c0 + 512],
                                    start=(dc == 0), stop=(dc == 7)), [Bw, BhT], [Bpm])
                            if ecnt % 2 == 0:
                                P.op("act", lambda e, st=st, i=i, pb=pb: e.copy(out=st[:, i, :], in_=pm[pb][:]),
                                     [Bpm], [Bst])
                            else:
                                P.op("dve", lambda e, st=st, i=i, pb=pb: e.tensor_copy(out=st[:, i, :], in_=pm[pb][:]),
                                     [Bpm], [Bst])
                            ecnt += 1
                        P.dma("sp", dst_ap, st[:], [Bst], [], chn)
                    ecnt_box[0] = ecnt

                stageN(0)
                for gi in range(len(groups)):
                    if gi + 1 < len(groups):
                        stageN(gi + 1)
                    stageM(gi)
                P.barrier()
                P.emit()

            if even:
                mixer_attention(P, nc, spc, "na", idx, zqT, zkT, zv, mix, na_tab, na_mask, ident)
                mixer_hgrn2(P, nc, spc, idx, zr, zi, mixT, hgrn_lb, hgrn_onorm, cmask_f, cmask_b, tri_f, tri_b,
                            ident, ones_b)
            else:
                mixer_attention(P, nc, spc, "dil", idx, zqT, zkT, zv, mix, t5_tab, t5_cnt, ident)
                mixer_rglru(P, nc, spc, idx, zr, mixT, conv_w, conv_b, rg_wa, rg_ba, rg_wx, rg_bx, rg_lambda)

            with ExitStack() as ph:
                w_out = (w_out_even if even else w_out_odd)[idx]
                wout = sb(ph, "wout", [128, 8, D], BF16)
                wg = sb(ph, "wg", [128, 8, DFF], BF16)
                wu = sb(ph, "wu", [128, 8, DFF], BF16)
                wd = sb(ph, "wd", [128, 22, D], BF16)
                gam = sb(ph, "gamC", [128, 3, D], F32)
                Bw = P.B("wC")
                for dc in range(8):
                    P.dma("pool", wout[:, dc, :], w_out.rearrange("(c p) n -> p c n", p=128)[:, dc, :], [], [Bw], "w%d" % dc)
                for dc in range(8):
                    P.dma("pool", wg[:, dc, :], w_gate[L].rearrange("(c p) n -> p c n", p=128)[:, dc, :], [], [Bw], "w%d" % dc)
                    P.dma("pool", wu[:, dc, :], w_up[L].rearrange("(c p) n -> p c n", p=128)[:, dc, :], [], [Bw],
                          "w%d" % ((dc + 4) % 8))
                for fc in range(22):
                    P.dma("pool", wd[:, fc, :], w_down[L].rearrange("(c p) n -> p c n", p=128)[:, fc, :], [], [Bw],
                          "w%d" % (fc % 8))
                for k in range(3):
                    P.dma("sp", gam[:, k, :], norm_g[L, k + 1].partition_broadcast(128), [], [Bw], "gA")
                NX = 3
                xt = [sb(ph, "xtC%d" % i, [128, D], F32) for i in range(2)]
                xn = [sb(ph, "xnC%d" % i, [128, D], F32) for i in range(NX)]
                mtok = [sb(ph, "mtokC%d" % i, [128, 512], BF16) for i in range(2)]
                mfm = [sb(ph, "mfmC%d" % i, [128, 4, 128], BF16) for i in range(2)]
                matt = sb(ph, "mattC", [128, 4, 128], BF16)
                junk = sb(ph, "junkC", [128, D], BF16)
                tmp = [sb(ph, "tmpC%d" % i, [128, 512], F32) for i in range(2)]
                h2 = sb(ph, "h2C", [128, D], BF16)
                h2T = sb(ph, "h2TC", [128, 8, 128], BF16)
                aa = sb(ph, "aC", [128, DFF], BF16)
                aT = sb(ph, "aTC", [128, 22, 128], BF16)
                ss = sb(ph, "ssC", [128, 8 * NX], F32)
                epst = sb(ph, "epsC", [128, 1], F32)
                pT = [pst(ph, "pTC%d" % i, [128, 1024], BF16) for i in range(2)]
                pout = [pst(ph, "poC%d" % i, [128, 512], F32) for i in range(2)]
                pdn = [pst(ph, "pdC%d" % i, [128, 512], F32) for i in range(2)]
                pgu = [pst(ph, "pguC%d" % i, [128, 512], F32) for i in range(2)]
                Bj, Bh2, Bh2T, Ba, BaT, Bmatt = (P.B("junkC"), P.B("h2C"), P.B("h2TC"), P.B("aC"), P.B("aTC"),
                                                 P.B("mattC"))
                BpT = [P.B("pTC", i) for i in range(2)]
                Bpo = [P.B("poC", i) for i in range(2)]
                Bpd = [P.B("pdC", i) for i in range(2)]
                Bpg = [P.B("pguC", i) for i in range(2)]
                Btmp = [P.B("tmpC", i) for i in range(2)]
                Beps = P.B("epsC")
                P.op("pool", lambda e: e.memset(epst[:], EPS), [], [Beps])
                ptc = [0]
                tiles = [(s, j) for s in range(spc) for j in range(NT)]

                def rstd_chain(Bss, sq0, sq1, out_col):
                    if sq1 is not None:
                        P.op("dve", lambda e: e.tensor_tensor(out=sq0, in0=sq0, in1=sq1, op=ALU.add), [Bss], [Bss])
                    P.op("act", lambda e: e.activation(out=out_col, in_=sq0, func=AF.Sqrt, bias=epst[:], scale=1.0 / D),
                         [Bss, Beps], [Bss])
                    P.op("dve", lambda e: e.reciprocal(out=out_col, in_=out_col), [Bss], [Bss])

                def stage1(n):
                    s, j = tiles[n]
                    b = n % 2
                    k3 = n % NX
                    tok0 = j * 128
                    Bx, Bmt, Bmf, Bxn, Bss = P.B("xtC", b), P.B("mtokC", b), P.B("mfmC", b), P.B("xnC", k3), P.B("ssC", k3)
                    sc = ss[:, 8 * k3:8 * k3 + 8]
                    P.dma("sp", xt[b][:], xsrc[s, tok0:tok0 + 128, :], [], [Bx], "xA%d" % b)
                    P.dma("sp", mtok[b][:], mix[s, tok0:tok0 + 128, :], [], [Bmt], "mtC%d" % b)
                    P.dma("sp", mfm[b][:], mixT[s].rearrange("(c p) t -> p c t", p=128)[:, :, tok0:tok0 + 128],
                          [], [Bmf], "mfC%d" % b)
                    P.op("pool", lambda e: e.memset(sc, 0.0), [], [Bss])
                    pt = ptc[0] % 2
                    ptc[0] += 1
                    for c in range(4):
                        P.op("pe", lambda e, pt=pt, c=c, b=b: e.transpose(pT[pt][:, c * 128:(c + 1) * 128],
                                                                          mtok[b][:, c * 128:(c + 1) * 128], ident[:]),
                             [Bmt], [BpT[pt]])
                    P.op("act", lambda e, pt=pt: e.copy(out=matt[:], in_=pT[pt][:, 0:512].rearrange("p (c t) -> p c t", c=4)),
                         [BpT[pt]], [Bmatt])
                    if even:
                        chunks = [matt[:, c, :] for c in range(4)] + [mfm[b][:, c, :] for c in range(4)]
                    else:
                        chunks = [mfm[b][:, c, :] for c in range(4)] + [matt[:, c, :] for c in range(4)]
                    for half in range(2):
                        for c in range(8):
                            P.op("pe", lambda e, half=half, c=c, ch=chunks[c]: e.matmul(
                                pout[half][:], lhsT=ch, rhs=wout[:, c, half * 512:(half + 1) * 512],
                                start=(c == 0), stop=(c == 7)), [Bmatt, Bmf, Bw], [Bpo[half]])
                    for half in range(2):
                        P.op("act", lambda e, half=half: e.activation(out=junk[:, half * 512:(half + 1) * 512],
                                                                      in_=pout[half][:], func=AF.Square,
                                                                      accum_out=sc[:, half:half + 1]),
                             [Bpo[half], Bss], [Bj, Bss])
                    rstd_chain(Bss, sc[:, 0:1], sc[:, 1:2], sc[:, 2:3])
                    for half in range(2):
                        P.op("dve", lambda e, half=half: e.scalar_tensor_tensor(
                            out=tmp[half][:], in0=pout[half][:], scalar=sc[:, 2:3],
                            in1=gam[:, 0, half * 512:(half + 1) * 512], op0=ALU.mult, op1=ALU.mult),
                            [Bpo[half], Bss, Bw], [Btmp[half]])
                        P.op("dve", lambda e, half=half, b=b, k3=k3: e.tensor_tensor(
                            out=xn[k3][:, half * 512:(half + 1) * 512], in0=tmp[half][:],
                            in1=xt[b][:, half * 512:(half + 1) * 512], op=ALU.add), [Btmp[half], Bx], [Bxn])
                    P.op("act", lambda e, k3=k3: e.activation(out=junk[:], in_=xn[k3][:], func=AF.Square, accum_out=sc[:, 3:4]),
                         [Bxn, Bss], [Bj, Bss])
                    rstd_chain(Bss, sc[:, 3:4], None, sc[:, 4:5])
                    P.op("dve", lambda e, k3=k3: e.scalar_tensor_tensor(out=h2[:], in0=xn[k3][:], scalar=sc[:, 4:5],
                                                                        in1=gam[:, 1, :], op0=ALU.mult, op1=ALU.mult),
                         [Bxn, Bss, Bw], [Bh2])

                def stage2(n):
                    pt = ptc[0] % 2
                    ptc[0] += 1
                    for dc in range(8):
                        P.op("pe", lambda e, pt=pt, dc=dc: e.transpose(pT[pt][:, dc * 128:(dc + 1) * 128],
                                                                       h2[:, dc * 128:(dc + 1) * 128], ident[:]),
                             [Bh2], [BpT[pt]])
                    P.op("act", lambda e, pt=pt: e.copy(out=h2T[:], in_=pT[pt][:].rearrange("p (c t) -> p c t", c=8)),
                         [BpT[pt]], [Bh2T])
                    for fb in range(6):
                        f0 = fb * 512
                        wdt = min(512, DFF - f0)
                        for dc in range(8):
                            P.op("pe", lambda e, dc=dc, f0=f0, wdt=wdt: e.matmul(
                                pgu[0][:, 0:wdt], lhsT=h2T[:, dc, :], rhs=wg[:, dc, f0:f0 + wdt],
                                start=(dc == 0), stop=(dc == 7)), [Bh2T, Bw], [Bpg[0]])
                        for dc in range(8):
                            P.op("pe", lambda e, dc=dc, f0=f0, wdt=wdt: e.matmul(
                                pgu[1][:, 0:wdt], lhsT=h2T[:, dc, :], rhs=wu[:, dc, f0:f0 + wdt],
                                start=(dc == 0), stop=(dc == 7)), [Bh2T, Bw], [Bpg[1]])
                        tb = fb % 2
                        P.op("act", lambda e, tb=tb, wdt=wdt: e.activation(out=tmp[tb][:, 0:wdt], in_=pgu[0][:, 0:wdt],
                                                                           func=AF.Silu), [Bpg[0]], [Btmp[tb]])
                        P.op("dve", lambda e, tb=tb, f0=f0, wdt=wdt: e.tensor_tensor(
                            out=aa[:, f0:f0 + wdt], in0=tmp[tb][:, 0:wdt], in1=pgu[1][:, 0:wdt], op=ALU.mult),
                            [Btmp[tb], Bpg[1]], [Ba])

                def stage3(n):
                    s, j = tiles[n]
                    k3 = n % NX
                    tok0 = j * 128
                    Bxn, Bss = P.B("xnC", k3), P.B("ssC", k3)
                    sc = ss[:, 8 * k3:8 * k3 + 8]
                    for r0 in range(0, 22, 8):
                        nch = min(8, 22 - r0)
                        pt = ptc[0] % 2
                        ptc[0] += 1
                        for c in range(nch):
                            fc = r0 + c
                            P.op("pe", lambda e, pt=pt, c=c, fc=fc: e.transpose(pT[pt][:, c * 128:(c + 1) * 128],
                                                                                aa[:, fc * 128:(fc + 1) * 128], ident[:]),
                                 [Ba], [BpT[pt]])
                        if r0 != 8:
                            P.op("act", lambda e, pt=pt, r0=r0, nch=nch: e.copy(
                                out=aT[:, r0:r0 + nch, :],
                                in_=pT[pt][:, 0:nch * 128].rearrange("p (c t) -> p c t", c=nch)), [BpT[pt]], [BaT])
                        else:
                            P.op("dve", lambda e, pt=pt, r0=r0, nch=nch: e.tensor_copy(
                                out=aT[:, r0:r0 + nch, :],
                                in_=pT[pt][:, 0:nch * 128].rearrange("p (c t) -> p c t", c=nch)), [BpT[pt]], [BaT])

                def stage3b(n):
                    s, j = tiles[n]
                    k3 = n % NX
                    tok0 = j * 128
                    Bxn, Bss = P.B("xnC", k3), P.B("ssC", k3)
                    sc = ss[:, 8 * k3:8 * k3 + 8]
                    for half in range(2):
                        for fc in range(22):
                            P.op("pe", lambda e, half=half, fc=fc: e.matmul(
                                pdn[half][:], lhsT=aT[:, fc, :], rhs=wd[:, fc, half * 512:(half + 1) * 512],
                                start=(fc == 0), stop=(fc == 21)), [BaT, Bw], [Bpd[half]])
                    for half in range(2):
                        P.op("act", lambda e, half=half: e.activation(out=junk[:, half * 512:(half + 1) * 512],
                                                                      in_=pdn[half][:], func=AF.Square,
                                                                      accum_out=sc[:, 5 + half:6 + half]),
                             [Bpd[half], Bss], [Bj, Bss])
                    rstd_chain(Bss, sc[:, 5:6], sc[:, 6:7], sc[:, 7:8])
                    for half in range(2):
                        P.op("dve", lambda e, half=half: e.scalar_tensor_tensor(
                            out=tmp[half][:], in0=pdn[half][:], scalar=sc[:, 7:8],
                            in1=gam[:, 2, half * 512:(half + 1) * 512], op0=ALU.mult, op1=ALU.mult),
                            [Bpd[half], Bss, Bw], [Btmp[half]])
                        P.op("dve", lambda e, half=half, k3=k3: e.tensor_tensor(
                            out=xn[k3][:, half * 512:(half + 1) * 512], in0=tmp[half][:],
                            in1=xn[k3][:, half * 512:(half + 1) * 512], op=ALU.add), [Btmp[half], Bxn], [Bxn])
                    P.dma("sp", xdst[s, tok0:tok0 + 128, :], xn[k3][:], [Bxn], [], "xoC%d" % k3)

                ntile = len(tiles)
                stage1(0)
                for n in range(ntile):
                    stage2(n)
                    stage3(n)
                    if n + 1 < ntile:
                        stage1(n + 1)
                    stage3b(n)
                P.barrier()
                P.emit()
    return nc


def mixer_attention(P, nc, spc, kind, idx, zqT, zkT, zv, mix, tab, msk, ident):
    with ExitStack() as ph:
        def sb(name, shape, dt):
            return ph.enter_context(nc.sbuf_tensor(_uniq(name), list(shape), dt))

        def pst(name, shape, dt):
            return ph.enter_context(nc.psum_tensor(_uniq(name), list(shape), dt))

        if kind == "na":
            npat = NPAT
            pairs = NA_PAIRS
            tabl = tab[idx]
        else:
            npat = 17
            pairs = [[(i, i - j + 8) for i in range(max(0, j - 8), min(NT, j + 9))] for j in range(NT)]
            tabl = tab
        bias = sb("biasE", [128, npat * 8, 128], BF16)
        tb32 = [sb("tb32_%d" % i, [128, 8, 128], F32) for i in range(2)]
        mk32 = [sb("mk32_%d" % i, [128, 128], F32) for i in range(2)]
        Bbias = P.B("biasT")
        for p in range(npat):
            k = p % 2
            Bt, Bm = P.B("tb32", k), P.B("mk32", k)
            P.dma("sp", tb32[k][:], tabl[p].rearrange("h k q -> k h q"), [], [Bt], "tb%d" % k)
            P.dma("sp", mk32[k][:], msk[p], [], [Bm], "mk%d" % k)
            P.op("dve", lambda e, k=k, p=p: e.tensor_tensor(
                out=tb32[k][:], in0=tb32[k][:],
                in1=mk32[k][:].unsqueeze(1).to_broadcast([128, 8, 128]), op=ALU.add), [Bt, Bm], [Bt])
            P.op("act", lambda e, k=k, p=p: e.activation(out=bias[:, p * 8:(p + 1) * 8, :], in_=tb32[k][:], func=AF.Exp),
                 [Bt], [Bbias])
        QZ = sb("QZ", [128, 4, NT, 2, 128], BF16)
        KT = sb("KT", [128, 4, T], BF16)
        VA = sb("VA", [128, NT, 8, 65], BF16)
        NB = 4
        XS = [sb("XS%d" % i, [128, 512], BF16) for i in range(NB)]
        PT = [sb("PT%d" % i, [128, 512], BF16) for i in range(NB)]
        rden = sb("rden", [128, 8], F32)
        mo = [sb("mo%d" % i, [128, 512], BF16) for i in range(2)]
        pS = [pst("pS%d" % i, [128, 512], F32) for i in range(NB)]
        pO = [pst("pO%d" % i, [128, 512], F32) for i in range(4)]
        BQ, BK, BV, Brd = P.B("QZ"), P.B("KT"), P.B("VA"), P.B("rden")
        P.op("pool", lambda e: e.memset(VA[:, :, :, 64:65], 1.0), [], [BV])
        P.op("pool", lambda e: e.memset(QZ[:, 0:2], 0.0), [], [BQ])
        P.op("pool", lambda e: e.memset(QZ[:, 2:4], 0.0), [], [BQ])
        it = 0
        oc = 0
        for s in range(spc):
            for c in range(4):
                P.dma("sp", QZ[0:64, c, :, 0, :], zqT[s, c * 128:c * 128 + 64, :].rearrange("p (j q) -> p j q", q=128),
                      [], [BQ], "ldq%d" % c)
                P.dma("sp", QZ[64:128, c, :, 1, :], zqT[s, c * 128 + 64:c * 128 + 128, :].rearrange("p (j q) -> p j q", q=128),
                      [], [BQ], "ldqb%d" % c)
                P.dma("sp", KT[:, c, :], zkT[s, c * 128:(c + 1) * 128, :], [], [BK], "ldk%d" % c)
            for i4 in range(NT):
                P.dma("sp", VA[:, i4, :, 0:64],
                      zv[s, i4 * 128:(i4 + 1) * 128, :].rearrange("p (h d) -> p h d", d=64),
                      [], [BV], "ldv%d" % (i4 % 8))
            iters = []
            for j in range(NT):
                for hg in range(2):
                    lst = pairs[j]
                    for n, (i, pat) in enumerate(lst):
                        iters.append((j, hg, n, i, pat, len(lst)))
            state = {}

            def emit_scores(itn, j, hg, n, i, pat, ln):
                sbk = itn % NB
                BS, BX, BP = P.B("pS", sbk), P.B("XS", sbk), P.B("PT", sbk)
                for pp in range(2):
                    c = hg * 2 + pp
                    P.op("pe", lambda e, sbk=sbk, pp=pp, c=c, i=i, j=j: e.matmul(
                        pS[sbk][:, pp * 256:(pp + 1) * 256], lhsT=KT[:, c, i * 128:(i + 1) * 128],
                        rhs=QZ[:, c, j].rearrange("p a q -> p (a q)"), start=(pp == 0), stop=True,
                        skip_group_check=True), [BQ, BK], [BS])
                P.op("act", lambda e, sbk=sbk: e.activation(out=XS[sbk][:], in_=pS[sbk][:], func=AF.Exp), [BS], [BX])
                P.op("dve", lambda e, sbk=sbk, pat=pat, hg=hg: e.tensor_tensor(
                    out=PT[sbk][:].rearrange("p (h q) -> p h q", h=4), in0=XS[sbk][:].rearrange("p (h q) -> p h q", h=4),
                    in1=bias[:, pat * 8 + hg * 4:pat * 8 + hg * 4 + 4, :], op=ALU.mult), [BX, Bbias], [BP])

            def emit_pv(itn, j, hg, n, i, pat, ln):
                sbk = itn % NB
                BP = P.B("PT", sbk)
                if n == 0:
                    state["ob"] = state.get("oc", 0) % 4
                    state["oc"] = state.get("oc", 0) + 1
                ob = state["ob"]
                BO = P.B("pO", ob)
                for hh in range(4):
                    h = hg * 4 + hh
                    P.op("pe", lambda e, sbk=sbk, hh=hh, h=h, i=i, ob=ob, n=n, ln=ln: e.matmul(
                        pO[ob][:, hh * 65:(hh + 1) * 65], lhsT=PT[sbk][:, hh * 128:(hh + 1) * 128],
                        rhs=VA[:, i, h, :], start=(n == 0 and hh == 0), stop=(n == ln - 1),
                        skip_group_check=True), [BP, BV], [BO])
                if n == ln - 1:
                    state["pending"] = (lambda ob=ob, hg=hg, j=j, BO=BO: emit_norm(ob, hg, j, BO))

            def emit_norm(ob, hg, j, BO):
                if True:
                    mb = j % 2
                    Bmo = P.B("mo", mb)
                    P.op("dve", lambda e, ob=ob, hg=hg: e.reciprocal(
                        out=rden[:, hg * 4:(hg + 1) * 4],
                        in_=pO[ob][:, 0:260].rearrange("p (h d) -> p h d", d=65)[:, :, 64]), [BO], [Brd])
                    for hh in range(4):
                        h = hg * 4 + hh
                        P.op("dve", lambda e, ob=ob, hh=hh, h=h, mb=mb: e.tensor_scalar(
                            out=mo[mb][:, h * 64:(h + 1) * 64], in0=pO[ob][:, hh * 65:hh * 65 + 64],
                            scalar1=rden[:, h:h + 1], scalar2=None, op0=ALU.mult), [BO, Brd], [Bmo])
                    if hg == 1:
                        P.dma("sp", mix[s, j * 128:(j + 1) * 128, :], mo[mb][:], [Bmo], [], "mo%d" % mb)

            base = it
            LA = 2
            for q in range(min(LA, len(iters))):
                emit_scores(base + q, *iters[q])
            for n_it in range(len(iters)):
                if n_it + LA < len(iters):
                    emit_scores(base + n_it + LA, *iters[n_it + LA])
                pend = state.pop("pending", None)
                emit_pv(base + n_it, *iters[n_it])
                if pend is not None:
                    pend()
            pend = state.pop("pending", None)
            if pend is not None:
                pend()
            it = base + len(iters)
        P.barrier()
        P.emit()


def mixer_hgrn2(P, nc, spc, idx, zr, zi, mixT, hgrn_lb, hgrn_onorm, cmask_f, cmask_b, tri_f, tri_b, ident, ones_b):
    with ExitStack() as ph:
        def sb(name, shape, dt):
            return ph.enter_context(nc.sbuf_tensor(_uniq(name), list(shape), dt))

        def pst(name, shape, dt):
            return ph.enter_context(nc.psum_tensor(_uniq(name), list(shape), dt))

        NCH = 64
        qs = sb("hq", [128, T], F32)
        zin = sb("hz", [128, T], F32)
        tA = sb("hA", [128, T], F32)
        tB = sb("hB", [128, T], F32)
        tC = sb("hC", [128, T], F32)
        rstd2 = [tC[:, 0:512], tC[:, 1024:1536]]
        Qt = [sb("hQt%d" % d, [128, T], BF16) for d in range(2)]
        Kt = [sb("hKt%d" % d, [128, T], BF16) for d in range(2)]
        Qh = [sb("hQh%d" % d, [128, T], BF16) for d in range(2)]
        Khf = sb("hKhf", [128, T], BF16)
        Kh = [sb("hKh%d" % d, [64, NCH, 128], BF16) for d in range(2)]
        Vt = sb("hV", [64, NCH, 128], BF16)
        dec = [sb("hdec%d" % d, [128, NCH], F32) for d in range(2)]
        cm = [sb("hcm%d" % d, [128, T], BF16) for d in range(2)]
        tri = [sb("htri%d" % d, [64, 64], F32) for d in range(2)]
        St = [sb("hS%d" % d, [128, 128], F32) for d in range(2)]
        Sb = [sb("hSb%d" % d, [128, 128], BF16) for d in range(2)]
        aT = [sb("haT%d_%d" % (d, k), [64, 64], BF16) for d in range(2) for k in range(2)]
        par = sb("hpar", [128, 8], F32)
        epsh = sb("hepsh", [128, 1], F32)
        ob16 = Qt[0]
        pa = [pst("hpa%d" % k, [64, 64], F32) for k in range(2)]
        po = [pst("hpo%d" % d, [128, 512], F32) for d in range(2)]
        pss = [pst("hps%d" % d, [128, 128], F32) for d in range(2)]
        ptr = pst("hptr", [64, 1024], BF16)
        pn = pst("hpn", [128, 512], F32)
        Bq, Bz, BA, BB, BC = P.B("hq"), P.B("hz"), P.B("hA"), P.B("hB"), P.B("hC")
        Bc = P.B("hconst")
        Bpar = P.B("hpar")
        P.op("pool", lambda e: e.memset(epsh[:], EPS), [], [P.B("hepsh")])
        P.dma("pool", cm[0][:], cmask_f[0].partition_broadcast(128), [], [Bc], "hc0")
        P.dma("pool", cm[1][:], cmask_b[0].partition_broadcast(128), [], [Bc], "hc1")
        P.dma("sp", tri[0][:], tri_f, [], [Bc], "hc2")
        P.dma("sp", tri[1][:], tri_b, [], [Bc], "hc3")
        for hd in range(4):
            f0 = hd * 128
            for d in range(2):
                P.dma("sp", par[:, d:d + 1], hgrn_lb[0, d, f0:f0 + 128].rearrange("(p o) -> p o", o=1), [], [Bpar], "hp0")
                P.dma("sp", par[:, 2 + d:3 + d], hgrn_lb[1, d, f0:f0 + 128].rearrange("(p o) -> p o", o=1), [], [Bpar], "hp1")
            P.dma("sp", par[:, 6:7], hgrn_onorm[idx].rearrange("(p o) -> p o", o=1), [], [Bpar], "hp2")
            if idx == 0:
                P.op("pool", lambda e: e.memset(par[:, 2:4], 0.0), [Bpar], [Bpar])
            else:
                P.op("dve", lambda e: e.tensor_tensor(out=par[:, 2:4], in0=par[:, 2:4], in1=par[:, 0:2], op=ALU.subtract),
                     [Bpar], [Bpar])
                P.op("act", lambda e: e.activation(out=par[:, 2:4], in_=par[:, 2:4], func=AF.Sigmoid), [Bpar], [Bpar])
            P.op("dve", lambda e: e.tensor_scalar(out=par[:, 4:6], in0=par[:, 2:4], scalar1=-1.0, scalar2=1.0,
                                                  op0=ALU.mult, op1=ALU.add), [Bpar], [Bpar])
            for s in range(spc):
                Bzall = [P.B("hz", blk) for blk in range(4)]
                P.dma("sp", zin[:], zr[s, 0, f0:f0 + 128, :], [], Bzall, "hz")
                P.op("act", lambda e: e.activation(out=qs[:], in_=zin[:], func=AF.Silu), Bzall, [Bq])
                P.dma("pool", Vt[:], zi[s, :, f0:f0 + 128].rearrange("(c p) v -> p c v", p=64), [], [P.B("hV")], "hv")
                NBLK = 4
                BW = T // NBLK
                CPB = BW // 64
                for d in range(2):
                    BQt, BKt, BQh, BKh, Bdec = (P.B("hQt", d), P.B("hKt", d), P.B("hQh", d), P.B("hKh", d),
                                                P.B("hdec", d))
                    mid = 31 if d == 0 else 32
                    last = 63 if d == 0 else 0
                    for blk in range(NBLK):
                        P.dma("sp", zin[:, blk * BW:(blk + 1) * BW], zr[s, 1 + d, f0:f0 + 128, blk * BW:(blk + 1) * BW],
                              [], [P.B("hz", blk)], "hz%d" % blk)

                    def v3(tile_, blk):
                        return tile_[:, blk * BW:(blk + 1) * BW].rearrange("p (c t) -> p c t", t=64)

                    def bl(tile_, blk):
                        return tile_[:, blk * BW:(blk + 1) * BW]

                    ops = []
                    ops.append(("act", lambda blk: (lambda e: e.activation(out=bl(tA, blk), in_=bl(zin, blk), func=AF.Sigmoid)),
                                lambda blk: [P.B("hz", blk)], lambda blk: [P.B("hA", blk)]))
                    ops.append(("dve", lambda blk, d=d: (lambda e: e.tensor_scalar(
                        out=bl(tA, blk), in0=bl(tA, blk), scalar1=par[:, 4 + d:5 + d], scalar2=par[:, 2 + d:3 + d],
                        op0=ALU.mult, op1=ALU.add)), lambda blk: [P.B("hA", blk), Bpar], lambda blk: [P.B("hA", blk)]))
                    ops.append(("act", lambda blk: (lambda e: e.activation(out=bl(tB, blk), in_=bl(tA, blk), func=AF.Ln)),
                                lambda blk: [P.B("hA", blk)], lambda blk: [P.B("hB", blk)]))
                    ops.append(("dve", lambda blk: (lambda e: e.tensor_scalar(
                        out=bl(tA, blk), in0=bl(tA, blk), scalar1=-1.0, scalar2=1.0, op0=ALU.mult, op1=ALU.add)),
                        lambda blk: [P.B("hA", blk)], lambda blk: [P.B("hA", blk)]))
                    if d == 0:
                        ops.append(("dve", lambda blk: (lambda e: e.tensor_tensor_scan(
                            out=bl(zin, blk), data0=bl(cm[0], blk), data1=bl(tB, blk), initial=0.0,
                            op0=ALU.mult, op1=ALU.add)), lambda blk: [P.B("hB", blk), Bc, P.B("hz", blk)],
                            lambda blk: [P.B("hz", blk)]))
                    else:
                        ops.append(("dve", lambda blk: (lambda e: e.tensor_tensor_scan(
                            out=bl(zin, blk)[:, ::-1], data0=bl(cm[1], blk)[:, ::-1], data1=bl(tB, blk)[:, ::-1],
                            initial=0.0, op0=ALU.mult, op1=ALU.add)), lambda blk: [P.B("hB", blk), Bc, P.B("hz", blk)],
                            lambda blk: [P.B("hz", blk)]))
                    ops.append(("dve", lambda blk, mid=mid: (lambda e: e.tensor_tensor(
                        out=v3(tB, blk), in0=v3(zin, blk), in1=v3(zin, blk)[:, :, mid:mid + 1].to_broadcast([128, CPB, 64]),
                        op=ALU.subtract)), lambda blk: [P.B("hz", blk), P.B("hB", blk)], lambda blk: [P.B("hB", blk)]))
                    ops.append(("act", lambda blk: (lambda e: e.activation(out=bl(tC, blk), in_=bl(tB, blk), func=AF.Exp)),
                                lambda blk: [P.B("hB", blk)], lambda blk: [P.B("hC", blk)]))
                    ops.append(("dve", lambda blk, d=d: (lambda e: e.tensor_tensor(
                        out=bl(Qt[d], blk), in0=bl(qs, blk), in1=bl(tC, blk), op=ALU.mult)),
                        lambda blk: [Bq, P.B("hC", blk)], lambda blk: [P.B("hQt", d)]))
                    ops.append(("act", lambda blk: (lambda e: e.activation(out=bl(tC, blk), in_=bl(tB, blk), func=AF.Exp,
                                                                           scale=-1.0)),
                                lambda blk: [P.B("hB", blk), P.B("hC", blk)], lambda blk: [P.B("hC", blk)]))
                    ops.append(("dve", lambda blk, d=d: (lambda e: e.tensor_tensor(
                        out=bl(Kt[d], blk), in0=bl(tA, blk), in1=bl(tC, blk), op=ALU.mult)),
                        lambda blk: [P.B("hA", blk), P.B("hC", blk)], lambda blk: [P.B("hKt", d)]))
                    ops.append(("act", lambda blk: (lambda e: e.activation(out=bl(tC, blk), in_=bl(zin, blk), func=AF.Exp)),
                                lambda blk: [P.B("hz", blk), P.B("hC", blk)], lambda blk: [P.B("hC", blk)]))
                    ops.append(("dve", lambda blk, d=d: (lambda e: e.tensor_tensor(
                        out=bl(Qh[d], blk), in0=bl(qs, blk), in1=bl(tC, blk), op=ALU.mult)),
                        lambda blk: [Bq, P.B("hC", blk)], lambda blk: [P.B("hQh", d)]))
                    ops.append(("dve", lambda blk, d=d, last=last: (lambda e: e.tensor_copy(
                        out=dec[d][:, blk * CPB:(blk + 1) * CPB], in_=v3(tC, blk)[:, :, last])),
                        lambda blk: [P.B("hC", blk)], lambda blk: [Bdec]))
                    ops.append(("dve", lambda blk, last=last: (lambda e: e.tensor_tensor(
                        out=v3(tB, blk), in0=v3(zin, blk), in1=v3(zin, blk)[:, :, last:last + 1].to_broadcast([128, CPB, 64]),
                        op=ALU.subtract)), lambda blk: [P.B("hz", blk), P.B("hB", blk)], lambda blk: [P.B("hB", blk)]))
                    ops.append(("act", lambda blk: (lambda e: e.activation(out=bl(tC, blk), in_=bl(tB, blk), func=AF.Exp,
                                                                           scale=-1.0)),
                                lambda blk: [P.B("hB", blk), P.B("hC", blk)], lambda blk: [P.B("hC", blk)]))
                    ops.append(("dve", lambda blk: (lambda e: e.tensor_tensor(
                        out=bl(Khf, blk), in0=bl(tA, blk), in1=bl(tC, blk), op=ALU.mult)),
                        lambda blk: [P.B("hA", blk), P.B("hC", blk)], lambda blk: [P.B("hKhf", blk)]))
                    for (eng, mk, rd, wr) in ops:
                        for blk in range(NBLK):
                            P.op(eng, mk(blk), rd(blk), wr(blk))
                    Bptr = P.B("hptr")
                    for c8 in range(8):
                        for cc in range(8):
                            c = c8 * 8 + cc
                            P.op("pe", lambda e, c=c, cc=cc: e.transpose(ptr[:, cc * 128:(cc + 1) * 128],
                                                                         Khf[:, c * 64:(c + 1) * 64], ident[:]),
                                 [P.B("hKhf", c8 // 2)], [Bptr])
                        P.op("act", lambda e, d=d, c8=c8: e.copy(out=Kh[d][:, c8 * 8:(c8 + 1) * 8, :],
                                                                 in_=ptr[:].rearrange("p (c v) -> p c v", c=8)),
                             [Bptr], [BKh])
                BAall = [P.B("hA", blk) for blk in range(4)]
                BBall = [P.B("hB", blk) for blk in range(4)]
                Bzall = [P.B("hz", blk) for blk in range(4)]
                BKfall = [P.B("hKhf", blk) for blk in range(4)]
                BO = [BAall, BBall]
                osb = [tA, tB]
                for d in range(2):
                    P.op("pool", lambda e, d=d: e.memset(St[d][:], 0.0), [], [P.B("hS", d)])
                    P.op("pool", lambda e, d=d: e.memset(Sb[d][:], 0.0), [], [P.B("hSb", d)])
                def chunk_of(step, d):
                    return step if d == 0 else NCH - 1 - step

                def emit_pa(step, d):
                    c = chunk_of(step, d)
                    k = step % 2
                    cs = slice(c * 64, (c + 1) * 64)
                    Bpa, BaT = P.B("hpa", d), P.B("haT", d, k)
                    P.op("pe", lambda e, d=d, cs=cs: e.matmul(pa[d][:], lhsT=Kt[d][:, cs], rhs=Qt[d][:, cs],
                                                              start=True, stop=True),
                         [P.B("hKt", d), P.B("hQt", d)], [Bpa])
                    P.op("dve", lambda e, d=d, k=k: e.tensor_tensor(out=aT[d * 2 + k][:], in0=pa[d][:], in1=tri[d][:],
                                                                    op=ALU.mult), [Bpa, Bc], [BaT])

                for d in range(2):
                    emit_pa(0, d)
                for step in range(NCH):
                    if step + 1 < NCH:
                        for d in range(2):
                            emit_pa(step + 1, d)
                    for d in range(2):
                        c = chunk_of(step, d)
                        k = step % 2
                        BaT, Bpo = P.B("haT", d, k), P.B("hpo", d)
                        slot = (step % 8) if d == 0 else 7 - (step % 8)
                        P.op("pe", lambda e, d=d, k=k, c=c, slot=slot: e.matmul(
                            po[d][:, slot * 64:(slot + 1) * 64], lhsT=Vt[:, c, :], rhs=aT[d * 2 + k][:],
                            start=True, stop=False), [P.B("hV"), BaT], [Bpo])
                    for d in range(2):
                        c = chunk_of(step, d)
                        Bpo, Bps = P.B("hpo", d), P.B("hps", d)
                        BS, BSb = P.B("hS", d), P.B("hSb", d)
                        cs = slice(c * 64, (c + 1) * 64)
                        slot = (step % 8) if d == 0 else 7 - (step % 8)
                        P.op("pe", lambda e, d=d, cs=cs, slot=slot: e.matmul(
                            po[d][:, slot * 64:(slot + 1) * 64], lhsT=Sb[d][:], rhs=Qh[d][:, cs],
                            start=False, stop=True), [BSb, P.B("hQh", d)], [Bpo])
                        P.op("pe", lambda e, d=d, c=c: e.matmul(pss[d][:], lhsT=Kh[d][:, c, :], rhs=Vt[:, c, :],
                                                                start=True, stop=True),
                             [P.B("hKh", d), P.B("hV")], [Bps])
                        P.op("dve", lambda e, d=d, c=c: e.scalar_tensor_tensor(out=St[d][:], in0=St[d][:],
                                                                               scalar=dec[d][:, c:c + 1], in1=pss[d][:],
                                                                               op0=ALU.mult, op1=ALU.add),
                             [BS, Bps, P.B("hdec", d)], [BS])
                        P.op("act", lambda e, d=d: e.copy(out=Sb[d][:], in_=St[d][:]), [BS], [BSb])
                        if step % 8 == 7:
                            g8 = step // 8
                            t0 = g8 * 512 if d == 0 else (7 - g8) * 512
                            P.op("act", lambda e, d=d, t0=t0: e.copy(out=osb[d][:, t0:t0 + 512], in_=po[d][:]),
                                 [Bpo], BO[d])
                P.op("dve", lambda e: e.tensor_tensor(out=tA[:], in0=tA[:], in1=tB[:], op=ALU.add), BAall + BBall, BAall)
                P.op("act", lambda e: e.activation(out=ob16[:], in_=tA[:], func=AF.Square), BAall, [P.B("hQt", 0)])
                P.dma("sp", zin[:], zr[s, 3, f0:f0 + 128, :], [], Bzall, "hz")
                P.op("act", lambda e: e.activation(out=tB[:], in_=zin[:], func=AF.Silu), Bzall + BBall, BBall)
                for blk in range(8):
                    bs = slice(blk * 512, (blk + 1) * 512)
                    k = blk % 2
                    Bpn, Brs = P.B("hpo", k), P.B("hC", k)
                    A4 = [P.B("hA", blk // 2)]
                    P.op("pe", lambda e, bs=bs, k=k: e.matmul(po[k][:], lhsT=ones_b[:], rhs=ob16[:, bs], start=True, stop=True),
                         [P.B("hQt", 0)], [Bpn])
                    P.op("act", lambda e, k=k: e.activation(out=rstd2[k], in_=po[k][:], func=AF.Sqrt, bias=epsh[:],
                                                            scale=1.0 / 128), [Bpn, P.B("hepsh")], [Brs])
                    P.op("dve", lambda e, k=k: e.reciprocal(out=rstd2[k], in_=rstd2[k]), [Brs], [Brs])
                    P.op("dve", lambda e, bs=bs, k=k: e.tensor_tensor(out=tA[:, bs], in0=tA[:, bs], in1=rstd2[k],
                                                                      op=ALU.mult), A4 + [Brs], A4)
                P.op("dve", lambda e: e.scalar_tensor_tensor(out=Khf[:], in0=tA[:], scalar=par[:, 6:7], in1=tB[:],
                                                             op0=ALU.mult, op1=ALU.mult), BAall + BBall + [Bpar], BKfall)
                P.dma("sp", mixT[s, f0:f0 + 128, :], Khf[:], BKfall, [], "hout")
        P.barrier()
        P.emit()


def mixer_rglru(P, nc, spc, idx, zr, mixT, conv_w, conv_b, rg_wa, rg_ba, rg_wx, rg_bx, rg_lambda):
    with ExitStack() as ph:
        def sb(name, shape, dt):
            return ph.enter_context(nc.sbuf_tensor(_uniq(name), list(shape), dt))

        def pst(name, shape, dt):
            return ph.enter_context(nc.psum_tensor(_uniq(name), list(shape), dt))

        xp = sb("rxp", [128, T + 4], F32)
        yy = sb("ry", [128, T], F32)
        yb = sb("ryb", [128, T], BF16)
        rr = sb("rr", [128, T], F32)
        ii = sb("ri", [128, T], F32)
        a1 = sb("ra", [128, T], F32)
        t1 = sb("rt", [128, T], F32)
        hh = [sb("rh%d" % d, [128, T], F32) for d in range(2)]
        ob = sb("rob", [128, T], BF16)
        wbd32 = sb("rw32", [128, 4, 128], F32)
        wbd = sb("rwbd", [128, 4, 128], BF16)
        par = sb("rpar", [128, 24], F32)
        pm = [pst("rpm%d" % i, [128, 512], F32) for i in range(4)]
        Bxp, By, Byb, Br, Bi, Ba, Bt, Bob = (P.B("rxp"), P.B("ry"), P.B("ryb"), P.B("rr"), P.B("ri"), P.B("ra"),
                                             P.B("rt"), P.B("rob"))
        Bh = [P.B("rh", d) for d in range(2)]
        Bw, Bpar = P.B("rw"), P.B("rpar")
        P.op("pool", lambda e: e.memset(xp[:], 0.0), [], [Bxp])
        col = lambda ap: ap.rearrange("(p o) -> p o", o=1)
        ec = 0
        for ch in range(4):
            c0 = ch * 128
            for k in range(4):
                P.dma("sp", par[:, k:k + 1], col(conv_w[idx, k, c0:c0 + 128]), [], [Bpar], "rp0")
            P.dma("sp", par[:, 4:5], col(conv_b[idx, c0:c0 + 128]), [], [Bpar], "rp1")
            for d in range(2):
                P.dma("sp", par[:, 5 + d:6 + d], col(rg_ba[idx, d, c0:c0 + 128]), [], [Bpar], "rp2")
                P.dma("sp", par[:, 7 + d:8 + d], col(rg_bx[idx, d, c0:c0 + 128]), [], [Bpar], "rp3")
                P.dma("sp", par[:, 9 + d:10 + d], col(rg_lambda[idx, d, c0:c0 + 128]), [], [Bpar], "rp4")
            P.op("act", lambda e: e.activation(out=par[:, 9:11], in_=par[:, 9:11], func=AF.Exp, scale=-1.0), [Bpar], [Bpar])
            P.op("dve", lambda e: e.tensor_scalar(out=par[:, 9:11], in0=par[:, 9:11], scalar1=1.0, scalar2=None, op0=ALU.add),
                 [Bpar], [Bpar])
            P.op("act", lambda e: e.activation(out=par[:, 9:11], in_=par[:, 9:11], func=AF.Ln), [Bpar], [Bpar])
            P.op("dve", lambda e: e.tensor_scalar(out=par[:, 11:13], in0=par[:, 9:11], scalar1=-16.0, scalar2=None,
                                                  op0=ALU.mult), [Bpar], [Bpar])
            P.op("dve", lambda e: e.tensor_scalar(out=par[:, 9:11], in0=par[:, 9:11], scalar1=-8.0, scalar2=None,
                                                  op0=ALU.mult), [Bpar], [Bpar])
            P.op("pool", lambda e: e.memset(wbd32[:], 0.0), [], [Bw])
            for d in range(2):
                for m, wsrc in enumerate((rg_wa, rg_wx)):
                    for blk in range(2):
                        P.dma("sp", wbd32[blk * 64:(blk + 1) * 64, d * 2 + m, blk * 64:(blk + 1) * 64],
                              wsrc[idx, d, ch * 2 + blk], [], [Bw], "rw")
            P.op("dve", lambda e: e.tensor_copy(out=wbd[:], in_=wbd32[:]), [Bw], [Bw])
            for s in range(spc):
                P.dma("sp", xp[:, 2:T + 2], zr[s, 1, c0:c0 + 128, :], [], [Bxp], "rx")
                P.op("dve", lambda e: e.tensor_scalar(out=yy[:], in0=xp[:, 0:T], scalar1=par[:, 0:1], scalar2=par[:, 4:5],
                                                      op0=ALU.mult, op1=ALU.add), [Bxp, Bpar], [By])
                for k in range(1, 4):
                    P.op("dve", lambda e, k=k: e.scalar_tensor_tensor(out=yy[:], in0=xp[:, k:k + T], scalar=par[:, k:k + 1],
                                                                      in1=yy[:], op0=ALU.mult, op1=ALU.add),
                         [Bxp, Bpar, By], [By])
                P.op("act", lambda e: e.copy(out=yb[:], in_=yy[:]), [By], [Byb])
                for d in range(2):
                    for blk in range(8):
                        bs = slice(blk * 512, (blk + 1) * 512)
                        for m, (dst, Bd, bc) in enumerate(((rr, Br, 5 + d), (ii, Bi, 7 + d))):
                            pb = ec % 4
                            ec += 1
                            Bpm = P.B("rpm", pb)
                            P.op("pe", lambda e, pb=pb, d=d, m=m, bs=bs: e.matmul(pm[pb][:], lhsT=wbd[:, d * 2 + m, :],
                                                                                  rhs=yb[:, bs], start=True, stop=True),
                                 [Bw, Byb], [Bpm])
                            P.op("act", lambda e, pb=pb, dst=dst, bs=bs, bc=bc: e.activation(
                                out=dst[:, bs], in_=pm[pb][:], func=AF.Sigmoid, bias=par[:, bc:bc + 1]),
                                [Bpm, Bpar], [Bd])
                    P.op("act", lambda e, d=d: e.activation(out=a1[:], in_=rr[:], func=AF.Exp, scale=par[:, 9 + d:10 + d]),
                         [Br, Bpar], [Ba])
                    P.op("act", lambda e, d=d: e.activation(out=t1[:], in_=rr[:], func=AF.Exp, scale=par[:, 11 + d:12 + d]),
                         [Br, Bpar], [Bt])
                    P.op("dve", lambda e: e.tensor_scalar(out=t1[:], in0=t1[:], scalar1=-1.0, scalar2=1.0,
                                                          op0=ALU.mult, op1=ALU.add), [Bt], [Bt])
                    P.op("act", lambda e: e.activation(out=t1[:], in_=t1[:], func=AF.Sqrt), [Bt], [Bt])
                    first = 0 if d == 0 else T - 1
                    P.op("pool", lambda e, first=first: e.memset(t1[:, first:first + 1], 1.0), [Bt], [Bt])
                    P.op("dve", lambda e: e.tensor_tensor(out=t1[:], in0=t1[:], in1=ii[:], op=ALU.mult), [Bt, Bi], [Bt])
                    P.op("dve", lambda e: e.tensor_tensor(out=t1[:], in0=t1[:], in1=yy[:], op=ALU.mult), [Bt, By], [Bt])
                    if d == 0:
                        P.op("dve", lambda e: e.tensor_tensor_scan(out=hh[0][:], data0=a1[:], data1=t1[:], initial=0.0,
                                                                   op0=ALU.mult, op1=ALU.add), [Ba, Bt], [Bh[0]])
                    else:
                        P.op("dve", lambda e: e.tensor_tensor_scan(out=hh[1][:, ::-1], data0=a1[:, ::-1],
                                                                   data1=t1[:, ::-1], initial=0.0,
                                                                   op0=ALU.mult, op1=ALU.add), [Ba, Bt], [Bh[1]])
                P.dma("sp", rr[:], zr[s, 0, c0:c0 + 128, :], [Br], [Br], "rg")
                P.op("act", lambda e: e.activation(out=ii[:], in_=rr[:], func=AF.Gelu_apprx_tanh), [Br, Bi], [Bi])
                P.op("dve", lambda e: e.tensor_tensor(out=hh[0][:], in0=hh[0][:], in1=hh[1][:], op=ALU.add),
                     [Bh[0], Bh[1]], [Bh[0]])
                P.op("dve", lambda e: e.tensor_tensor(out=ob[:], in0=hh[0][:], in1=ii[:], op=ALU.mult), [Bh[0], Bi], [Bob])
                P.dma("sp", mixT[s, c0:c0 + 128, :], ob[:], [Bob], [], "rout")
        P.barrier()
        P.emit()


def host_tables(na_rpb, t5_bias):
    hidx = np.arange(8)[None, :, None, None]
    na_tab = np.ascontiguousarray(
        np.stack([na_rpb[l][hidx, NA_DR[:, None], NA_DC[:, None]] for l in range(2)]).astype(np.float32))
    t5_tab = np.ascontiguousarray(np.transpose(t5_bias[T5_BK], (0, 3, 1, 2)).astype(np.float32))
    t = np.arange(T)
    consts = {
        "na_mask": NA_MASK, "t5_cnt": T5_CNT,
        "cmask_f": (t % 64 != 0).astype(np.float32)[None, :],
        "cmask_b": (t % 64 != 63).astype(np.float32)[None, :],
        "tri_f": (np.arange(64)[:, None] <= np.arange(64)[None, :]).astype(np.float32),
        "tri_b": (np.arange(64)[:, None] >= np.arange(64)[None, :]).astype(np.float32),
    }
    return na_tab, t5_tab, consts


_NC_CACHE = {}


def run(inputs, xs_per_core, layers=(0, 1, 2, 3), spc=SPC, debug=False, ncores=NCORES, trace=False):
    key = (tuple(layers), spc, debug)
    if key not in _NC_CACHE:
        _NC_CACHE[key] = build(layers, spc, debug)
    nc = _NC_CACHE[key]
    na_tab, t5_tab, consts = host_tables(np.asarray(inputs["na_rpb"]), np.asarray(inputs["t5_bias"]))
    shared = {k: np.ascontiguousarray(np.asarray(inputs[k], dtype=np.float32)) for k in (
        "norm_g", "w_in_even", "w_out_even", "w_in_odd", "w_out_odd", "w_gate", "w_up", "w_down", "hgrn_lb",
        "hgrn_onorm", "conv_w", "conv_b", "rg_wa", "rg_ba", "rg_wx", "rg_bx", "rg_lambda")}
    shared["na_tab"] = na_tab
    shared["t5_tab"] = t5_tab
    shared.update(consts)
    in_maps = []
    for c in range(ncores):
        m = dict(shared)
        m["x"] = xs_per_core[c]
        in_maps.append(m)
    res = run_bass_kernel_spmd(nc, in_maps, core_ids=list(range(ncores)), **({"trace": True} if trace else {}))
    return res


def kernel(**inputs):
    xp = np.asarray(inputs["x_prompt"], dtype=np.float32)
    xs = np.asarray(inputs["x_sample"], dtype=np.float32)
    allx = np.concatenate([xp, xs], axis=0)
    nseq = allx.shape[0]
    slots = np.zeros((NCORES * SPC, T, D), np.float32)
    slots[:nseq] = allx
    xs_per_core = [np.ascontiguousarray(slots[c * SPC:(c + 1) * SPC]) for c in range(NCORES)]
    res = run(inputs, xs_per_core)
    yall = np.concatenate([r["y"] for r in res.results], axis=0)[:nseq]
    return (np.ascontiguousarray(yall[:xp.shape[0]]), np.ascontiguousarray(yall[xp.shape[0]:]))
```

```python
import numpy as np
from contextlib import ExitStack
import concourse.bass as bass
import concourse.mybir as mybir
from concourse.bass_utils import run_bass_kernel_spmd

F32 = mybir.dt.float32
BF16 = mybir.dt.bfloat16
AF = mybir.ActivationFunctionType
ALU = mybir.AluOpType

T = 4096
D = 1024
NT = 32
DFF = 2816
EPS = 1e-6
SPC = 3
NCORES = 8
NEG = -30000.0


class Buf:
    __slots__ = ("w", "r")

    def __init__(self):
        self.w = None
        self.r = []


class Prog:
    def __init__(self, nc, stack):
        self.nc = nc
        self.stack = stack
        self.names = ("pe", "act", "dve", "pool", "sp")
        self.lists = {k: [] for k in self.names}
        self.esem = {}
        self.ecnt = {k: 0 for k in self.names}
        for k in ("pe", "act", "dve", "pool"):
            self.esem[k] = stack.enter_context(nc.semaphore("s_" + k))
        self.seen = {k: {} for k in self.names}
        self.bufs = {}
        self.chans = {}
        self.ninst = 0

    def B(self, *key):
        b = self.bufs.get(key)
        if b is None:
            b = self.bufs[key] = Buf()
        return b

    def chan(self, name):
        c = self.chans.get(name)
        if c is None:
            c = self.chans[name] = [self.stack.enter_context(self.nc.semaphore("c_" + name)), 0]
        return c

    def _deps(self, e, reads, writes):
        evs = []
        for b in reads:
            if b.w is not None:
                evs.append(b.w)
        for b in writes:
            if b.w is not None:
                evs.append(b.w)
            evs.extend(b.r)
        waits = {}
        seen = self.seen[e]
        for (sem, val, src) in evs:
            if src == "pe" and e == "pe":
                continue
            k = id(sem)
            if seen.get(k, 0) >= val:
                continue
            if k not in waits or waits[k][1] < val:
                waits[k] = (sem, val)
        for k, (sem, val) in waits.items():
            seen[k] = val
        return list(waits.values())

    def _commit(self, ev, reads, writes):
        for b in reads:
            b.r.append(ev)
        for b in writes:
            b.w = ev
            b.r = []

    def op(self, e, fn, reads=(), writes=()):
        waits = self._deps(e, reads, writes)
        self.ecnt[e] += 1
        ev = (self.esem[e], self.ecnt[e], e)
        self.lists[e].append((waits, fn, self.esem[e], 1))
        self._commit(ev, reads, writes)
        self.ninst += 1

    def dma(self, q, out, in_, reads, writes, chan, **kw):
        c = self.chan(chan)
        waits = self._deps(q, reads, writes)
        if c[1] > 0:
            k = id(c[0])
            if self.seen[q].get(k, 0) < c[1]:
                waits.append((c[0], c[1]))
                self.seen[q][k] = c[1]
        c[1] += 16
        ev = (c[0], c[1], "dma")
        self.lists[q].append((waits, lambda eng: eng.dma_start(out=out, in_=in_, **kw), c[0], 16))
        self._commit(ev, reads, writes)
        self.ninst += 1

    def barrier(self):
        for e in self.names:
            seen = self.seen[e]
            waits = []
            for k in ("pe", "act", "dve", "pool"):
                if k == e:
                    continue
                s, v = self.esem[k], self.ecnt[k]
                if v > 0 and seen.get(id(s), 0) < v:
                    waits.append((s, v))
                    seen[id(s)] = v
            for name, (s, v) in self.chans.items():
                if v > 0 and seen.get(id(s), 0) < v:
                    waits.append((s, v))
                    seen[id(s)] = v
            if waits:
                self.lists[e].append((waits, None, None, 0))
        for b in self.bufs.values():
            b.w = None
            b.r = []

    def emit(self):
        nc = self.nc
        lists = self.lists
        with nc.Block() as block:
            def mk(name):
                def body(eng):
                    for (waits, fn, sem, inc) in lists[name]:
                        for (s, v) in waits:
                            eng.wait_ge(s, v)
                        if fn is not None:
                            fn(eng).then_inc(sem, inc)
                return body
            block.tensor(mk("pe"))
            block.scalar(mk("act"))
            block.vector(mk("dve"))
            block.gpsimd(mk("pool"))
            block.sync(mk("sp"))
        self.lists = {k: [] for k in self.names}


def na_patterns():
    kk = np.arange(128)
    rk_l, ck = kk // 64, kk % 64
    pats = {}
    pairs = []
    drs, dcs, masks = [], [], []
    for j in range(NT):
        lst = []
        for i in range(NT):
            rq = 2 * j + rk_l[None, :]
            cq = ck[None, :]
            rk = 2 * i + rk_l[:, None]
            ckk = ck[:, None]
            start = np.clip(rq - 4, 0, 56)
            visr = (rk >= start) & (rk < start + 8)
            cs = np.clip(cq - 8, 0, 48)
            visc = (ckk >= cs) & (ckk < cs + 16)
            vis = visr & visc
            if not vis.any():
                continue
            dr = np.clip(rk - rq + 7, 0, 14) + 0 * cq
            dc = np.clip(ckk - cq + 15, 0, 30) + 0 * rq
            dr = np.where(vis, dr, 0)
            dc = np.where(vis, dc, 0)
            key = (vis.tobytes(), dr.astype(np.int8).tobytes(), dc.astype(np.int8).tobytes())
            if key not in pats:
                pats[key] = len(pats)
                drs.append(dr)
                dcs.append(dc)
                masks.append(np.where(vis, 0.0, NEG).astype(np.float32))
            lst.append((i, pats[key]))
        pairs.append(lst)
    return pairs, np.stack(drs), np.stack(dcs), np.stack(masks)


def t5_patterns():
    kk = np.arange(128)
    bks, cnts = [], []
    for dlt in range(-8, 9):
        d = 128 * dlt + kk[:, None] - kk[None, :]
        n = np.abs(d)
        cnt = (n <= 64).astype(np.int32) + ((d % 4 == 0) & (n <= 256)) + ((d % 16 == 0) & (n <= 1024))
        large = 8 + (np.log(np.maximum(n, 1).astype(np.float32) / np.float32(8)) / np.float32(np.log(1024 / 8)) * 8).astype(np.int32)
        large = np.minimum(large, 15)
        bk = np.where(d > 0, 16, 0) + np.where(n < 8, n, large)
        bks.append(bk)
        cnts.append(np.where(cnt > 0, np.log(np.maximum(cnt, 1)), NEG).astype(np.float32))
    return np.stack(bks), np.stack(cnts)


_UID = [0]


def _uniq(name):
    _UID[0] += 1
    return "%s_u%d" % (name, _UID[0])


NA_PAIRS, NA_DR, NA_DC, NA_MASK = na_patterns()
NPAT = NA_MASK.shape[0]
T5_BK, T5_CNT = t5_patterns()


def build(layers=(0, 1, 2, 3), spc=SPC, debug=False):
    nc = bass.Bass("TRN2", target_bir_lowering=False)

    def din(name, shape, dt=F32):
        return nc.dram_tensor(name, list(shape), dt, kind="ExternalInput").ap()

    def dscr(name, shape, dt):
        return nc.dram_tensor(name, list(shape), dt, kind="ExternalOutput" if debug else "Internal").ap()

    x_in = din("x", [spc, T, D])
    norm_g = din("norm_g", [4, 4, D])
    w_in_even = din("w_in_even", [2, D, 4096])
    w_out_even = din("w_out_even", [2, D, D])
    w_in_odd = din("w_in_odd", [2, D, 2560])
    w_out_odd = din("w_out_odd", [2, D, D])
    w_gate = din("w_gate", [4, D, DFF])
    w_up = din("w_up", [4, D, DFF])
    w_down = din("w_down", [4, DFF, D])
    hgrn_lb = din("hgrn_lb", [2, 2, 512])
    hgrn_onorm = din("hgrn_onorm", [2, 128])
    conv_w = din("conv_w", [2, 4, 512])
    conv_b = din("conv_b", [2, 512])
    rg_wa = din("rg_wa", [2, 2, 8, 64, 64])
    rg_ba = din("rg_ba", [2, 2, 512])
    rg_wx = din("rg_wx", [2, 2, 8, 64, 64])
    rg_bx = din("rg_bx", [2, 2, 512])
    rg_lambda = din("rg_lambda", [2, 2, 512])
    na_tab = din("na_tab", [2, NPAT, 8, 128, 128])
    na_mask = din("na_mask", [NPAT, 128, 128])
    t5_tab = din("t5_tab", [17, 8, 128, 128])
    t5_cnt = din("t5_cnt", [17, 128, 128])
    cmask_f = din("cmask_f", [1, T])
    cmask_b = din("cmask_b", [1, T])
    tri_f = din("tri_f", [64, 64])
    tri_b = din("tri_b", [64, 64])
    y_out = nc.dram_tensor("y", [spc, T, D], F32, kind="ExternalOutput").ap()

    xres = [dscr("xres0", [spc, T, D], F32), dscr("xres1", [spc, T, D], F32)]
    zqT = dscr("zqT", [spc, 512, T], BF16)
    zkT = dscr("zkT", [spc, 512, T], BF16)
    zv = dscr("zv", [spc, T, 512], BF16)
    zr = dscr("zr", [spc, 4, 512, T], F32)
    zi = dscr("zi", [spc, T, 512], F32)
    mix = dscr("mix", [spc, T, 512], BF16)
    mixT = dscr("mixT", [spc, 512, T], BF16)

    with ExitStack() as glob:
        P = Prog(nc, glob)

        def sb(stk, name, shape, dt):
            return stk.enter_context(nc.sbuf_tensor(_uniq(name), list(shape), dt))

        def pst(stk, name, shape, dt):
            return stk.enter_context(nc.psum_tensor(_uniq(name), list(shape), dt))

        identf = sb(glob, "identf", [128, 128], F32)
        ident = sb(glob, "ident", [128, 128], BF16)
        ones_b = sb(glob, "ones_b", [128, 128], BF16)
        Bid = P.B("ident")
        P.op("pool", lambda e: e.memset(identf[:], 0.0), [], [Bid])
        P.op("pool", lambda e: e.affine_select(out=identf[:], in_=identf[:], pattern=[[-1, 128]],
                                                compare_op=ALU.not_equal, fill=1.0, base=0,
                                                channel_multiplier=1), [Bid], [Bid])
        P.op("dve", lambda e: e.tensor_copy(out=ident[:], in_=identf[:]), [Bid], [Bid])
        P.op("pool", lambda e: e.memset(ones_b[:], 1.0), [], [Bid])
        P.barrier()
        P.emit()

        nlay = len(layers)
        for li, L in enumerate(layers):
            even = (L % 2 == 0)
            idx = L // 2
            xsrc = x_in if li == 0 else xres[(li - 1) % 2]
            xdst = y_out if li == nlay - 1 else xres[li % 2]
            with ExitStack() as ph:
                ncol = 4096 if even else 2560
                w_in = (w_in_even if even else w_in_odd)[idx]
                win = sb(ph, "win", [128, 8, ncol], BF16)
                g0 = sb(ph, "g0", [128, D], F32)
                Bw = P.B("win")
                wsrc = w_in.rearrange("(c p) n -> p c n", p=128)
                for dc in range(8):
                    P.dma("pool", win[:, dc, :], wsrc[:, dc, :], [], [Bw], "w%d" % dc)
                P.dma("sp", g0[:], norm_g[L, 0].partition_broadcast(128), [], [Bw], "gA")
                xt = [sb(ph, "xtA%d" % i, [128, D], F32) for i in range(4)]
                junk = sb(ph, "junkA", [128, D], F32)
                hb = [sb(ph, "hbA%d" % i, [128, D], BF16) for i in range(4)]
                hT = [sb(ph, "hTA%d" % i, [128, 8, 512], BF16) for i in range(2)]
                ss = sb(ph, "ssA", [128, 8], F32)
                stF = [sb(ph, "stF%d" % i, [128, 4, 512], F32) for i in range(4)]
                stH = [sb(ph, "stH%d" % i, [128, 4, 512], BF16) for i in range(4)]
                pT = [pst(ph, "pTA%d" % i, [128, 1024], BF16) for i in range(2)]
                pm = [pst(ph, "pmA%d" % i, [128, 512], F32) for i in range(4)]
                if even:
                    fm_parts = [(0, "qk", zqT, 0.125), (512, "qk", zkT, 1.0), (1536, "r", 0, 1.0),
                                (2048, "r", 1, 1.0), (2560, "r", 2, 1.0), (3584, "r", 3, 1.0)]
                    tm_parts = [(1024, "v"), (3072, "i")]
                else:
                    fm_parts = [(0, "r", 0, 1.0), (512, "r", 1, 1.0), (1024, "qk", zqT, 0.125),
                                (1536, "qk", zkT, 1.0)]
                    tm_parts = [(2048, "v")]
                hb = hb + [sb(ph, "hbA%d" % i, [128, D], BF16) for i in range(4, 8)]
                groups = [(s_, g_) for s_ in range(spc) for g_ in range(8)]
                ecnt_box = [0]
                scnt = {"F": 0, "H": 0}

                def stageN(gi):
                    s, g = groups[gi]
                    for i in range(4):
                        cnt = gi * 4 + i
                        b = cnt % 4
                        hbi = (gi % 2) * 4 + i
                        tok0 = g * 512 + i * 128
                        Bx, Bh, Bss = P.B("xtA", b), P.B("hbA", hbi), P.B("ssA", b)
                        Bj = P.B("junkA")
                        P.dma("sp", xt[b][:], xsrc[s, tok0:tok0 + 128, :], [], [Bx], "xA%d" % b)
                        sc = ss[:, 2 * b:2 * b + 1]
                        rs = ss[:, 2 * b + 1:2 * b + 2]
                        P.op("pool", lambda e, sc=sc: e.memset(sc, 0.0), [], [Bss])
                        P.op("act", lambda e, b=b, sc=sc: e.activation(out=junk[:], in_=xt[b][:], func=AF.Square,
                                                                       accum_out=sc), [Bx, Bss], [Bj, Bss])
                        P.op("dve", lambda e, sc=sc, rs=rs: e.tensor_scalar(out=rs, in0=sc, scalar1=1.0 / D, scalar2=EPS,
                                                                           op0=ALU.mult, op1=ALU.add), [Bss], [Bss])
                        P.op("act", lambda e, rs=rs: e.activation(out=rs, in_=rs, func=AF.Sqrt), [Bss], [Bss])
                        P.op("dve", lambda e, rs=rs: e.reciprocal(out=rs, in_=rs), [Bss], [Bss])
                        P.op("dve", lambda e, b=b, hbi=hbi, rs=rs: e.scalar_tensor_tensor(
                            out=hb[hbi][:], in0=xt[b][:], scalar=rs, in1=g0[:], op0=ALU.mult, op1=ALU.mult),
                            [Bx, Bss, Bw], [Bh])

                def stageM(gi):
                    s, g = groups[gi]
                    hTg = hT[gi % 2]
                    BhT = P.B("hTA", gi % 2)
                    for i in range(4):
                        cnt = gi * 4 + i
                        pb2 = cnt % 2
                        hbi = (gi % 2) * 4 + i
                        Bh, BpT = P.B("hbA", hbi), P.B("pTA", pb2)
                        for dc in range(8):
                            P.op("pe", lambda e, hbi=hbi, pb2=pb2, dc=dc: e.transpose(
                                pT[pb2][:, dc * 128:(dc + 1) * 128], hb[hbi][:, dc * 128:(dc + 1) * 128], ident[:]),
                                [Bh], [BpT])
                        P.op("act", lambda e, pb2=pb2, i=i, hTg=hTg: e.copy(
                            out=hTg[:, :, i * 128:(i + 1) * 128],
                            in_=pT[pb2][:].rearrange("p (c t) -> p c t", c=8)), [BpT], [BhT])
                    ecnt = ecnt_box[0]
                    for (c0, kind, dst, scale) in fm_parts:
                        if kind == "qk":
                            k = scnt["H"] % 4
                            scnt["H"] += 1
                            st, Bst, chn = stH[k], P.B("stH", k), "sH%d" % k
                            dst_ap = dst[s].rearrange("(c p) t -> p c t", p=128)[:, :, g * 512:(g + 1) * 512]
                        else:
                            k = scnt["F"] % 4
                            scnt["F"] += 1
                            st, Bst, chn = stF[k], P.B("stF", k), "sF%d" % k
                            dst_ap = zr[s, dst].rearrange("(c p) t -> p c t", p=128)[:, :, g * 512:(g + 1) * 512]
                        for fc in range(4):
                            pb = ecnt % 4
                            Bpm = P.B("pmA", pb)
                            col = c0 + fc * 128
                            for dc in range(8):
                                P.op("pe", lambda e, pb=pb, dc=dc, col=col, hTg=hTg: e.matmul(
                                    pm[pb][:], lhsT=win[:, dc, col:col + 128], rhs=hTg[:, dc, :],
                                    start=(dc == 0), stop=(dc == 7)), [Bw, BhT], [Bpm])
                            if ecnt % 2 == 0:
                                P.op("act", lambda e, st=st, fc=fc, pb=pb, scale=scale: e.activation(
                                    out=st[:, fc, :], in_=pm[pb][:], func=AF.Copy, scale=scale), [Bpm], [Bst])
                            else:
                                P.op("dve", lambda e, st=st, fc=fc, pb=pb, scale=scale: e.tensor_scalar(
                                    out=st[:, fc, :], in0=pm[pb][:], scalar1=scale, scalar2=None, op0=ALU.mult),
                                    [Bpm], [Bst])
                            ecnt += 1
                        P.dma("sp", dst_ap, st[:], [Bst], [], chn)
                    for (c0, kind) in tm_parts:
                        if kind == "v":
                            k = scnt["H"] % 4
                            scnt["H"] += 1
                            st, Bst, chn = stH[k], P.B("stH", k), "sH%d" % k
                            dst_ap = zv[s, g * 512:(g + 1) * 512, :].rearrange("(i p) n -> p i n", p=128)
                        else:
                            k = scnt["F"] % 4
                            scnt["F"] += 1
                            st, Bst, chn = stF[k], P.B("stF", k), "sF%d" % k
                            dst_ap = zi[s, g * 512:(g + 1) * 512, :].rearrange("(i p) n -> p i n", p=128)
                        for i in range(4):
                            pb = ecnt % 4
                            Bpm = P.B("pmA", pb)
                            for dc in range(8):
                                P.op("pe", lambda e, pb=pb, dc=dc, i=i, c0=c0, hTg=hTg: e.matmul(
                                    pm[pb][:], lhsT=hTg[:, dc, i * 128:(i + 1) * 128], rhs=win[:, dc, c0:c0 + 512],
                                    start=(dc == 0), stop=(dc == 7)), [Bw, BhT], [Bpm])
                            if ecnt % 2 == 0:
                                P.op("act", lambda e, st=st, i=i, pb=pb: e.copy(out=st[:, i, :], in_=pm[pb][:]),
                                     [Bpm], [Bst])
                            else:
                                P.op("dve", lambda e, st=st, i=i, pb=pb: e.tensor_copy(out=st[:, i, :], in_=pm[pb][:]),
                                     [Bpm], [Bst])
                            ecnt += 1
                        P.dma("sp", dst_ap, st[:], [Bst], [], chn)
                    ecnt_box[0] = ecnt

                stageN(0)
                for gi in range(len(groups)):
                    if gi + 1 < len(groups):
                        stageN(gi + 1)
                    stageM(gi)
                P.barrier()
                P.emit()

            if even:
                mixer_attention(P, nc, spc, "na", idx, zqT, zkT, zv, mix, na_tab, na_mask, ident)
                mixer_hgrn2(P, nc, spc, idx, zr, zi, mixT, hgrn_lb, hgrn_onorm, cmask_f, cmask_b, tri_f, tri_b,
                            ident, ones_b)
            else:
                mixer_attention(P, nc, spc, "dil", idx, zqT, zkT, zv, mix, t5_tab, t5_cnt, ident)
                mixer_rglru(P, nc, spc, idx, zr, mixT, conv_w, conv_b, rg_wa, rg_ba, rg_wx, rg_bx, rg_lambda)

            with ExitStack() as ph:
                w_out = (w_out_even if even else w_out_odd)[idx]
                wout = sb(ph, "wout", [128, 8, D], BF16)
                wg = sb(ph, "wg", [128, 8, DFF], BF16)
                wu = sb(ph, "wu", [128, 8, DFF], BF16)
                wd = sb(ph, "wd", [128, 22, D], BF16)
                gam = sb(ph, "gamC", [128, 3, D], F32)
                Bw = P.B("wC")
                for dc in range(8):
                    P.dma("pool", wout[:, dc, :], w_out.rearrange("(c p) n -> p c n", p=128)[:, dc, :], [], [Bw], "w%d" % dc)
                for dc in range(8):
                    P.dma("pool", wg[:, dc, :], w_gate[L].rearrange("(c p) n -> p c n", p=128)[:, dc, :], [], [Bw], "w%d" % dc)
                    P.dma("pool", wu[:, dc, :], w_up[L].rearrange("(c p) n -> p c n", p=128)[:, dc, :], [], [Bw],
                          "w%d" % ((dc + 4) % 8))
                for fc in range(22):
                    P.dma("pool", wd[:, fc, :], w_down[L].rearrange("(c p) n -> p c n", p=128)[:, fc, :], [], [Bw],
                          "w%d" % (fc % 8))
                for k in range(3):
                    P.dma("sp", gam[:, k, :], norm_g[L, k + 1].partition_broadcast(128), [], [Bw], "gA")
                NX = 3
                xt = [sb(ph, "xtC%d" % i, [128, D], F32) for i in range(2)]
                xn = [sb(ph, "xnC%d" % i, [128, D], F32) for i in range(NX)]
                mtok = [sb(ph, "mtokC%d" % i, [128, 512], BF16) for i in range(2)]
                mfm = [sb(ph, "mfmC%d" % i, [128, 4, 128], BF16) for i in range(2)]
                matt = sb(ph, "mattC", [128, 4, 128], BF16)
                junk = sb(ph, "junkC", [128, D], BF16)
                tmp = [sb(ph, "tmpC%d" % i, [128, 512], F32) for i in range(2)]
                h2 = sb(ph, "h2C", [128, D], BF16)
                h2T = sb(ph, "h2TC", [128, 8, 128], BF16)
                aa = sb(ph, "aC", [128, DFF], BF16)
                aT = sb(ph, "aTC", [128, 22, 128], BF16)
                ss = sb(ph, "ssC", [128, 8 * NX], F32)
                epst = sb(ph, "epsC", [128, 1], F32)
                pT = [pst(ph, "pTC%d" % i, [128, 1024], BF16) for i in range(2)]
                pout = [pst(ph, "poC%d" % i, [128, 512], F32) for i in range(2)]
                pdn = [pst(ph, "pdC%d" % i, [128, 512], F32) for i in range(2)]
                pgu = [pst(ph, "pguC%d" % i, [128, 512], F32) for i in range(2)]
                Bj, Bh2, Bh2T, Ba, BaT, Bmatt = (P.B("junkC"), P.B("h2C"), P.B("h2TC"), P.B("aC"), P.B("aTC"),
                                                 P.B("mattC"))
                BpT = [P.B("pTC", i) for i in range(2)]
                Bpo = [P.B("poC", i) for i in range(2)]
                Bpd = [P.B("pdC", i) for i in range(2)]
                Bpg = [P.B("pguC", i) for i in range(2)]
                Btmp = [P.B("tmpC", i) for i in range(2)]
                Beps = P.B("epsC")
                P.op("pool", lambda e: e.memset(epst[:], EPS), [], [Beps])
                ptc = [0]
                tiles = [(s, j) for s in range(spc) for j in range(NT)]

                def rstd_chain(Bss, sq0, sq1, out_col):
                    if sq1 is not None:
                        P.op("dve", lambda e: e.tensor_tensor(out=sq0, in0=sq0, in1=sq1, op=ALU.add), [Bss], [Bss])
                    P.op("act", lambda e: e.activation(out=out_col, in_=sq0, func=AF.Sqrt, bias=epst[:], scale=1.0 / D),
                         [Bss, Beps], [Bss])
                    P.op("dve", lambda e: e.reciprocal(out=out_col, in_=out_col), [Bss], [Bss])

                def stage1(n):
                    s, j = tiles[n]
                    b = n % 2
                    k3 = n % NX
                    tok0 = j * 128
                    Bx, Bmt, Bmf, Bxn, Bss = P.B("xtC", b), P.B("mtokC", b), P.B("mfmC", b), P.B("xnC", k3), P.B("ssC", k3)
                    sc = ss[:, 8 * k3:8 * k3 + 8]
                    P.dma("sp", xt[b][:], xsrc[s, tok0:tok0 + 128, :], [], [Bx], "xA%d" % b)
                    P.dma("sp", mtok[b][:], mix[s, tok0:tok0 + 128, :], [], [Bmt], "mtC%d" % b)
                    P.dma("sp", mfm[b][:], mixT[s].rearrange("(c p) t -> p c t", p=128)[:, :, tok0:tok0 + 128],
                          [], [Bmf], "mfC%d" % b)
                    P.op("pool", lambda e: e.memset(sc, 0.0), [], [Bss])
                    pt = ptc[0] % 2
                    ptc[0] += 1
                    for c in range(4):
                        P.op("pe", lambda e, pt=pt, c=c, b=b: e.transpose(pT[pt][:, c * 128:(c + 1) * 128],
                                                                          mtok[b][:, c * 128:(c + 1) * 128], ident[:]),
                             [Bmt], [BpT[pt]])
                    P.op("act", lambda e, pt=pt: e.copy(out=matt[:], in_=pT[pt][:, 0:512].rearrange("p (c t) -> p c t", c=4)),
                         [BpT[pt]], [Bmatt])
                    if even:
                        chunks = [matt[:, c, :] for c in range(4)] + [mfm[b][:, c, :] for c in range(4)]
                    else:
                        chunks = [mfm[b][:, c, :] for c in range(4)] + [matt[:, c, :] for c in range(4)]
                    for half in range(2):
                        for c in range(8):
                            P.op("pe", lambda e, half=half, c=c, ch=chunks[c]: e.matmul(
                                pout[half][:], lhsT=ch, rhs=wout[:, c, half * 512:(half + 1) * 512],
                                start=(c == 0), stop=(c == 7)), [Bmatt, Bmf, Bw], [Bpo[half]])
                    for half in range(2):
                        P.op("act", lambda e, half=half: e.activation(out=junk[:, half * 512:(half + 1) * 512],
                                                                      in_=pout[half][:], func=AF.Square,
                                                                      accum_out=sc[:, half:half + 1]),
                             [Bpo[half], Bss], [Bj, Bss])
                    rstd_chain(Bss, sc[:, 0:1], sc[:, 1:2], sc[:, 2:3])
                    for half in range(2):
                        P.op("dve", lambda e, half=half: e.scalar_tensor_tensor(
                            out=tmp[half][:], in0=pout[half][:], scalar=sc[:, 2:3],
                            in1=gam[:, 0, half * 512:(half + 1) * 512], op0=ALU.mult, op1=ALU.mult),
                            [Bpo[half], Bss, Bw], [Btmp[half]])
                        P.op("dve", lambda e, half=half, b=b, k3=k3: e.tensor_tensor(
                            out=xn[k3][:, half * 512:(half + 1) * 512], in0=tmp[half][:],
                            in1=xt[b][:, half * 512:(half + 1) * 512], op=ALU.add), [Btmp[half], Bx], [Bxn])
                    P.op("act", lambda e, k3=k3: e.activation(out=junk[:], in_=xn[k3][:], func=AF.Square, accum_out=sc[:, 3:4]),
                         [Bxn, Bss], [Bj, Bss])
                    rstd_chain(Bss, sc[:, 3:4], None, sc[:, 4:5])
                    P.op("dve", lambda e, k3=k3: e.scalar_tensor_tensor(out=h2[:], in0=xn[k3][:], scalar=sc[:, 4:5],
                                                                        in1=gam[:, 1, :], op0=ALU.mult, op1=ALU.mult),
                         [Bxn, Bss, Bw], [Bh2])

                def stage2(n):
                    pt = ptc[0] % 2
                    ptc[0] += 1
                    for dc in range(8):
                        P.op("pe", lambda e, pt=pt, dc=dc: e.transpose(pT[pt][:, dc * 128:(dc + 1) * 128],
                                                                       h2[:, dc * 128:(dc + 1) * 128], ident[:]),
                             [Bh2], [BpT[pt]])
                    P.op("act", lambda e, pt=pt: e.copy(out=h2T[:], in_=pT[pt][:].rearrange("p (c t) -> p c t", c=8)),
                         [BpT[pt]], [Bh2T])
                    for fb in range(6):
                        f0 = fb * 512
                        wdt = min(512, DFF - f0)
                        for dc in range(8):
                            P.op("pe", lambda e, dc=dc, f0=f0, wdt=wdt: e.matmul(
                                pgu[0][:, 0:wdt], lhsT=h2T[:, dc, :], rhs=wg[:, dc, f0:f0 + wdt],
                                start=(dc == 0), stop=(dc == 7)), [Bh2T, Bw], [Bpg[0]])
                        for dc in range(8):
                            P.op("pe", lambda e, dc=dc, f0=f0, wdt=wdt: e.matmul(
                                pgu[1][:, 0:wdt], lhsT=h2T[:, dc, :], rhs=wu[:, dc, f0:f0 + wdt],
                                start=(dc == 0), stop=(dc == 7)), [Bh2T, Bw], [Bpg[1]])
                        tb = fb % 2
                        P.op("act", lambda e, tb=tb, wdt=wdt: e.activation(out=tmp[tb][:, 0:wdt], in_=pgu[0][:, 0:wdt],
                                                                           func=AF.Silu), [Bpg[0]], [Btmp[tb]])
                        P.op("dve", lambda e, tb=tb, f0=f0, wdt=wdt: e.tensor_tensor(
                            out=aa[:, f0:f0 + wdt], in0=tmp[tb][:, 0:wdt], in1=pgu[1][:, 0:wdt], op=ALU.mult),
                            [Btmp[tb], Bpg[1]], [Ba])

                def stage3(n):
                    s, j = tiles[n]
                    k3 = n % NX
                    tok0 = j * 128
                    Bxn, Bss = P.B("xnC", k3), P.B("ssC", k3)
                    sc = ss[:, 8 * k3:8 * k3 + 8]
                    for r0 in range(0, 22, 8):
                        nch = min(8, 22 - r0)
                        pt = ptc[0] % 2
                        ptc[0] += 1
                        for c in range(nch):
                            fc = r0 + c
                            P.op("pe", lambda e, pt=pt, c=c, fc=fc: e.transpose(pT[pt][:, c * 128:(c + 1) * 128],
                                                                                aa[:, fc * 128:(fc + 1) * 128], ident[:]),
                                 [Ba], [BpT[pt]])
                        if r0 != 8:
                            P.op("act", lambda e, pt=pt, r0=r0, nch=nch: e.copy(
                                out=aT[:, r0:r0 + nch, :],
                                in_=pT[pt][:, 0:nch * 128].rearrange("p (c t) -> p c t", c=nch)), [BpT[pt]], [BaT])
                        else:
                            P.op("dve", lambda e, pt=pt, r0=r0, nch=nch: e.tensor_copy(
                                out=aT[:, r0:r0 + nch, :],
                                in_=pT[pt][:, 0:nch * 128].rearrange("p (c t) -> p c t", c=nch)), [BpT[pt]], [BaT])

                def stage3b(n):
                    s, j = tiles[n]
                    k3 = n % NX
                    tok0 = j * 128
                    Bxn, Bss = P.B("xnC", k3), P.B("ssC", k3)
                    sc = ss[:, 8 * k3:8 * k3 + 8]
                    for half in range(2):
                        for fc in range(22):
                            P.op("pe", lambda e, half=half, fc=fc: e.matmul(
                                pdn[half][:], lhsT=aT[:, fc, :], rhs=wd[:, fc, half * 512:(half + 1) * 512],
                                start=(fc == 0), stop=(fc == 21)), [BaT, Bw], [Bpd[half]])
                    for half in range(2):
                        P.op("act", lambda e, half=half: e.activation(out=junk[:, half * 512:(half + 1) * 512],
                                                                      in_=pdn[half][:], func=AF.Square,
                                                                      accum_out=sc[:, 5 + half:6 + half]),
                             [Bpd[half], Bss], [Bj, Bss])
                    rstd_chain(Bss, sc[:, 5:6], sc[:, 6:7], sc[:, 7:8])
                    for half in range(2):
                        P.op("dve", lambda e, half=half: e.scalar_tensor_tensor(
                            out=tmp[half][:], in0=pdn[half][:], scalar=sc[:, 7:8],
                            in1=gam[:, 2, half * 512:(half + 1) * 512], op0=ALU.mult, op1=ALU.mult),
                            [Bpd[half], Bss, Bw], [Btmp[half]])
                        P.op("dve", lambda e, half=half, k3=k3: e.tensor_tensor(
                            out=xn[k3][:, half * 512:(half + 1) * 512], in0=tmp[half][:],
                            in1=xn[k3][:, half * 512:(half + 1) * 512], op=ALU.add), [Btmp[half], Bxn], [Bxn])
                    P.dma("sp", xdst[s, tok0:tok0 + 128, :], xn[k3][:], [Bxn], [], "xoC%d" % k3)

                ntile = len(tiles)
                stage1(0)
                for n in range(ntile):
                    stage2(n)
                    stage3(n)
                    if n + 1 < ntile:
                        stage1(n + 1)
                    stage3b(n)
                P.barrier()
                P.emit()
    return nc


def mixer_attention(P, nc, spc, kind, idx, zqT, zkT, zv, mix, tab, msk, ident):
    with ExitStack() as ph:
        def sb(name, shape, dt):
            return ph.enter_context(nc.sbuf_tensor(_uniq(name), list(shape), dt))

        def pst(name, shape, dt):
            return ph.enter_context(nc.psum_tensor(_uniq(name), list(shape), dt))

        if kind == "na":
            npat = NPAT
            pairs = NA_PAIRS
            tabl = tab[idx]
        else:
            npat = 17
            pairs = [[(i, i - j + 8) for i in range(max(0, j - 8), min(NT, j + 9))] for j in range(NT)]
            tabl = tab
        bias = sb("biasE", [128, npat * 8, 128], BF16)
        tb32 = [sb("tb32_%d" % i, [128, 8, 128], F32) for i in range(2)]
        mk32 = [sb("mk32_%d" % i, [128, 128], F32) for i in range(2)]
        Bbias = P.B("biasT")
        for p in range(npat):
            k = p % 2
            Bt, Bm = P.B("tb32", k), P.B("mk32", k)
            P.dma("sp", tb32[k][:], tabl[p].rearrange("h k q -> k h q"), [], [Bt], "tb%d" % k)
            P.dma("sp", mk32[k][:], msk[p], [], [Bm], "mk%d" % k)
            P.op("dve", lambda e, k=k, p=p: e.tensor_tensor(
                out=tb32[k][:], in0=tb32[k][:],
                in1=mk32[k][:].unsqueeze(1).to_broadcast([128, 8, 128]), op=ALU.add), [Bt, Bm], [Bt])
            P.op("act", lambda e, k=k, p=p: e.activation(out=bias[:, p * 8:(p + 1) * 8, :], in_=tb32[k][:], func=AF.Exp),
                 [Bt], [Bbias])
        QZ = sb("QZ", [128, 4, NT, 2, 128], BF16)
        KT = sb("KT", [128, 4, T], BF16)
        VA = sb("VA", [128, NT, 8, 65], BF16)
        NB = 4
        XS = [sb("XS%d" % i, [128, 512], BF16) for i in range(NB)]
        PT = [sb("PT%d" % i, [128, 512], BF16) for i in range(NB)]
        rden = sb("rden", [128, 8], F32)
        mo = [sb("mo%d" % i, [128, 512], BF16) for i in range(2)]
        pS = [pst("pS%d" % i, [128, 512], F32) for i in range(NB)]
        pO = [pst("pO%d" % i, [128, 512], F32) for i in range(4)]
        BQ, BK, BV, Brd = P.B("QZ"), P.B("KT"), P.B("VA"), P.B("rden")
        P.op("pool", lambda e: e.memset(VA[:, :, :, 64:65], 1.0), [], [BV])
        P.op("pool", lambda e: e.memset(QZ[:, 0:2], 0.0), [], [BQ])
        P.op("pool", lambda e: e.memset(QZ[:, 2:4], 0.0), [], [BQ])
        it = 0
        oc = 0
        for s in range(spc):
            for c in range(4):
                P.dma("sp", QZ[0:64, c, :, 0, :], zqT[s, c * 128:c * 128 + 64, :].rearrange("p (j q) -> p j q", q=128),
                      [], [BQ], "ldq%d" % c)
                P.dma("sp", QZ[64:128, c, :, 1, :], zqT[s, c * 128 + 64:c * 128 + 128, :].rearrange("p (j q) -> p j q", q=128),
                      [], [BQ], "ldqb%d" % c)
                P.dma("sp", KT[:, c, :], zkT[s, c * 128:(c + 1) * 128, :], [], [BK], "ldk%d" % c)
            for i4 in range(NT):
                P.dma("sp", VA[:, i4, :, 0:64],
                      zv[s, i4 * 128:(i4 + 1) * 128, :].rearrange("p (h d) -> p h d", d=64),
                      [], [BV], "ldv%d" % (i4 % 8))
            iters = []
            for j in range(NT):
                for hg in range(2):
                    lst = pairs[j]
                    for n, (i, pat) in enumerate(lst):
                        iters.append((j, hg, n, i, pat, len(lst)))
            state = {}

            def emit_scores(itn, j, hg, n, i, pat, ln):
                sbk = itn % NB
                BS, BX, BP = P.B("pS", sbk), P.B("XS", sbk), P.B("PT", sbk)
                for pp in range(2):
                    c = hg * 2 + pp
                    P.op("pe", lambda e, sbk=sbk, pp=pp, c=c, i=i, j=j: e.matmul(
                        pS[sbk][:, pp * 256:(pp + 1) * 256], lhsT=KT[:, c, i * 128:(i + 1) * 128],
                        rhs=QZ[:, c, j].rearrange("p a q -> p (a q)"), start=(pp == 0), stop=True,
                        skip_group_check=True), [BQ, BK], [BS])
                P.op("act", lambda e, sbk=sbk: e.activation(out=XS[sbk][:], in_=pS[sbk][:], func=AF.Exp), [BS], [BX])
                P.op("dve", lambda e, sbk=sbk, pat=pat, hg=hg: e.tensor_tensor(
                    out=PT[sbk][:].rearrange("p (h q) -> p h q", h=4), in0=XS[sbk][:].rearrange("p (h q) -> p h q", h=4),
                    in1=bias[:, pat * 8 + hg * 4:pat * 8 + hg * 4 + 4, :], op=ALU.mult), [BX, Bbias], [BP])

            def emit_pv(itn, j, hg, n, i, pat, ln):
                sbk = itn % NB
                BP = P.B("PT", sbk)
                if n == 0:
                    state["ob"] = state.get("oc", 0) % 4
                    state["oc"] = state.get("oc", 0) + 1
                ob = state["ob"]
                BO = P.B("pO", ob)
                for hh in range(4):
                    h = hg * 4 + hh
                    P.op("pe", lambda e, sbk=sbk, hh=hh, h=h, i=i, ob=ob, n=n, ln=ln: e.matmul(
                        pO[ob][:, hh * 65:(hh + 1) * 65], lhsT=PT[sbk][:, hh * 128:(hh + 1) * 128],
                        rhs=VA[:, i, h, :], start=(n == 0 and hh == 0), stop=(n == ln - 1),
                        skip_group_check=True), [BP, BV], [BO])
                if n == ln - 1:
                    state["pending"] = (lambda ob=ob, hg=hg, j=j, BO=BO: emit_norm(ob, hg, j, BO))

            def emit_norm(ob, hg, j, BO):
                if True:
                    mb = j % 2
                    Bmo = P.B("mo", mb)
                    P.op("dve", lambda e, ob=ob, hg=hg: e.reciprocal(
                        out=rden[:, hg * 4:(hg + 1) * 4],
                        in_=pO[ob][:, 0:260].rearrange("p (h d) -> p h d", d=65)[:, :, 64]), [BO], [Brd])
                    for hh in range(4):
                        h = hg * 4 + hh
                        P.op("dve", lambda e, ob=ob, hh=hh, h=h, mb=mb: e.tensor_scalar(
                            out=mo[mb][:, h * 64:(h + 1) * 64], in0=pO[ob][:, hh * 65:hh * 65 + 64],
                            scalar1=rden[:, h:h + 1], scalar2=None, op0=ALU.mult), [BO, Brd], [Bmo])
                    if hg == 1:
                        P.dma("sp", mix[s, j * 128:(j + 1) * 128, :], mo[mb][:], [Bmo], [], "mo%d" % mb)

            base = it
            LA = 2
            for q in range(min(LA, len(iters))):
                emit_scores(base + q, *iters[q])
            for n_it in range(len(iters)):
                if n_it + LA < len(iters):
                    emit_scores(base + n_it + LA, *iters[n_it + LA])
                pend = state.pop("pending", None)
                emit_pv(base + n_it, *iters[n_it])
                if pend is not None:
                    pend()
            pend = state.pop("pending", None)
            if pend is not None:
                pend()
            it = base + len(iters)
        P.barrier()
        P.emit()


def mixer_hgrn2(P, nc, spc, idx, zr, zi, mixT, hgrn_lb, hgrn_onorm, cmask_f, cmask_b, tri_f, tri_b, ident, ones_b):
    with ExitStack() as ph:
        def sb(name, shape, dt):
            return ph.enter_context(nc.sbuf_tensor(_uniq(name), list(shape), dt))

        def pst(name, shape, dt):
            return ph.enter_context(nc.psum_tensor(_uniq(name), list(shape), dt))

        NCH = 64
        qs = sb("hq", [128, T], F32)
        zin = sb("hz", [128, T], F32)
        tA = sb("hA", [128, T], F32)
        tB = sb("hB", [128, T], F32)
        tC = sb("hC", [128, T], F32)
        rstd2 = [tC[:, 0:512], tC[:, 1024:1536]]
        Qt = [sb("hQt%d" % d, [128, T], BF16) for d in range(2)]
        Kt = [sb("hKt%d" % d, [128, T], BF16) for d in range(2)]
        Qh = [sb("hQh%d" % d, [128, T], BF16) for d in range(2)]
        Khf = sb("hKhf", [128, T], BF16)
        Kh = [sb("hKh%d" % d, [64, NCH, 128], BF16) for d in range(2)]
        Vt = sb("hV", [64, NCH, 128], BF16)
        dec = [sb("hdec%d" % d, [128, NCH], F32) for d in range(2)]
        cm = [sb("hcm%d" % d, [128, T], BF16) for d in range(2)]
        tri = [sb("htri%d" % d, [64, 64], F32) for d in range(2)]
        St = [sb("hS%d" % d, [128, 128], F32) for d in range(2)]
        Sb = [sb("hSb%d" % d, [128, 128], BF16) for d in range(2)]
        aT = [sb("haT%d_%d" % (d, k), [64, 64], BF16) for d in range(2) for k in range(2)]
        par = sb("hpar", [128, 8], F32)
        epsh = sb("hepsh", [128, 1], F32)
        ob16 = Qt[0]
        pa = [pst("hpa%d" % k, [64, 64], F32) for k in range(2)]
        po = [pst("hpo%d" % d, [128, 512], F32) for d in range(2)]
        pss = [pst("hps%d" % d, [128, 128], F32) for d in range(2)]
        ptr = pst("hptr", [64, 1024], BF16)
        pn = pst("hpn", [128, 512], F32)
        Bq, Bz, BA, BB, BC = P.B("hq"), P.B("hz"), P.B("hA"), P.B("hB"), P.B("hC")
        Bc = P.B("hconst")
        Bpar = P.B("hpar")
        P.op("pool", lambda e: e.memset(epsh[:], EPS), [], [P.B("hepsh")])
        P.dma("pool", cm[0][:], cmask_f[0].partition_broadcast(128), [], [Bc], "hc0")
        P.dma("pool", cm[1][:], cmask_b[0].partition_broadcast(128), [], [Bc], "hc1")
        P.dma("sp", tri[0][:], tri_f, [], [Bc], "hc2")
        P.dma("sp", tri[1][:], tri_b, [], [Bc], "hc3")
        for hd in range(4):
            f0 = hd * 128
            for d in range(2):
                P.dma("sp", par[:, d:d + 1], hgrn_lb[0, d, f0:f0 + 128].rearrange("(p o) -> p o", o=1), [], [Bpar], "hp0")
                P.dma("sp", par[:, 2 + d:3 + d], hgrn_lb[1, d, f0:f0 + 128].rearrange("(p o) -> p o", o=1), [], [Bpar], "hp1")
            P.dma("sp", par[:, 6:7], hgrn_onorm[idx].rearrange("(p o) -> p o", o=1), [], [Bpar], "hp2")
            if idx == 0:
                P.op("pool", lambda e: e.memset(par[:, 2:4], 0.0), [Bpar], [Bpar])
            else:
                P.op("dve", lambda e: e.tensor_tensor(out=par[:, 2:4], in0=par[:, 2:4], in1=par[:, 0:2], op=ALU.subtract),
                     [Bpar], [Bpar])
                P.op("act", lambda e: e.activation(out=par[:, 2:4], in_=par[:, 2:4], func=AF.Sigmoid), [Bpar], [Bpar])
            P.op("dve", lambda e: e.tensor_scalar(out=par[:, 4:6], in0=par[:, 2:4], scalar1=-1.0, scalar2=1.0,
                                                  op0=ALU.mult, op1=ALU.add), [Bpar], [Bpar])
            for s in range(spc):
                Bzall = [P.B("hz", blk) for blk in range(4)]
                BCall = [P.B("hC", blk) for blk in range(4)]
                P.dma("sp", tC[:], zr[s, 0, f0:f0 + 128, :], [], BCall, "hq")
                P.op("act", lambda e: e.activation(out=qs[:], in_=tC[:], func=AF.Silu), BCall, [Bq])
                P.dma("pool", Vt[:], zi[s, :, f0:f0 + 128].rearrange("(c p) v -> p c v", p=64), [], [P.B("hV")], "hv")
                NBLK = 4
                BW = T // NBLK
                CPB = BW // 64
                for d in range(2):
                    BQt, BKt, BQh, BKh, Bdec = (P.B("hQt", d), P.B("hKt", d), P.B("hQh", d), P.B("hKh", d),
                                                P.B("hdec", d))
                    mid = 31 if d == 0 else 32
                    last = 63 if d == 0 else 0
                    for blk in range(NBLK):
                        P.dma("sp", zin[:, blk * BW:(blk + 1) * BW], zr[s, 1 + d, f0:f0 + 128, blk * BW:(blk + 1) * BW],
                              [], [P.B("hz", blk)], "hz%d" % blk)

                    def v3(tile_, blk):
                        return tile_[:, blk * BW:(blk + 1) * BW].rearrange("p (c t) -> p c t", t=64)

                    def bl(tile_, blk):
                        return tile_[:, blk * BW:(blk + 1) * BW]

                    ops = []
                    ops.append(("act", lambda blk: (lambda e: e.activation(out=bl(tA, blk), in_=bl(zin, blk), func=AF.Sigmoid)),
                                lambda blk: [P.B("hz", blk)], lambda blk: [P.B("hA", blk)]))
                    ops.append(("dve", lambda blk, d=d: (lambda e: e.tensor_scalar(
                        out=bl(tA, blk), in0=bl(tA, blk), scalar1=par[:, 4 + d:5 + d], scalar2=par[:, 2 + d:3 + d],
                        op0=ALU.mult, op1=ALU.add)), lambda blk: [P.B("hA", blk), Bpar], lambda blk: [P.B("hA", blk)]))
                    ops.append(("act", lambda blk: (lambda e: e.activation(out=bl(tB, blk), in_=bl(tA, blk), func=AF.Ln)),
                                lambda blk: [P.B("hA", blk)], lambda blk: [P.B("hB", blk)]))
                    ops.append(("dve", lambda blk: (lambda e: e.tensor_scalar(
                        out=bl(tA, blk), in0=bl(tA, blk), scalar1=-1.0, scalar2=1.0, op0=ALU.mult, op1=ALU.add)),
                        lambda blk: [P.B("hA", blk)], lambda blk: [P.B("hA", blk)]))
                    if d == 0:
                        ops.append(("dve", lambda blk: (lambda e: e.tensor_tensor_scan(
                            out=bl(zin, blk), data0=bl(cm[0], blk), data1=bl(tB, blk), initial=0.0,
                            op0=ALU.mult, op1=ALU.add)), lambda blk: [P.B("hB", blk), Bc, P.B("hz", blk)],
                            lambda blk: [P.B("hz", blk)]))
                    else:
                        ops.append(("dve", lambda blk: (lambda e: e.tensor_tensor_scan(
                            out=bl(zin, blk)[:, ::-1], data0=bl(cm[1], blk)[:, ::-1], data1=bl(tB, blk)[:, ::-1],
                            initial=0.0, op0=ALU.mult, op1=ALU.add)), lambda blk: [P.B("hB", blk), Bc, P.B("hz", blk)],
                            lambda blk: [P.B("hz", blk)]))
                    ops.append(("dve", lambda blk, mid=mid: (lambda e: e.tensor_tensor(
                        out=v3(tB, blk), in0=v3(zin, blk), in1=v3(zin, blk)[:, :, mid:mid + 1].to_broadcast([128, CPB, 64]),
                        op=ALU.subtract)), lambda blk: [P.B("hz", blk), P.B("hB", blk)], lambda blk: [P.B("hB", blk)]))
                    ops.append(("act", lambda blk: (lambda e: e.activation(out=bl(tC, blk), in_=bl(tB, blk), func=AF.Exp)),
                                lambda blk: [P.B("hB", blk)], lambda blk: [P.B("hC", blk)]))
                    ops.append(("dve", lambda blk, d=d: (lambda e: e.tensor_tensor(
                        out=bl(Qt[d], blk), in0=bl(qs, blk), in1=bl(tC, blk), op=ALU.mult)),
                        lambda blk: [Bq, P.B("hC", blk)], lambda blk: [P.B("hQt", d)]))
                    ops.append(("act", lambda blk: (lambda e: e.activation(out=bl(tC, blk), in_=bl(tB, blk), func=AF.Exp,
                                                                           scale=-1.0)),
                                lambda blk: [P.B("hB", blk), P.B("hC", blk)], lambda blk: [P.B("hC", blk)]))
                    ops.append(("dve", lambda blk, d=d: (lambda e: e.tensor_tensor(
                        out=bl(Kt[d], blk), in0=bl(tA, blk), in1=bl(tC, blk), op=ALU.mult)),
                        lambda blk: [P.B("hA", blk), P.B("hC", blk)], lambda blk: [P.B("hKt", d)]))
                    ops.append(("act", lambda blk: (lambda e: e.activation(out=bl(tC, blk), in_=bl(zin, blk), func=AF.Exp)),
                                lambda blk: [P.B("hz", blk), P.B("hC", blk)], lambda blk: [P.B("hC", blk)]))
                    ops.append(("dve", lambda blk, d=d: (lambda e: e.tensor_tensor(
                        out=bl(Qh[d], blk), in0=bl(qs, blk), in1=bl(tC, blk), op=ALU.mult)),
                        lambda blk: [Bq, P.B("hC", blk)], lambda blk: [P.B("hQh", d)]))
                    ops.append(("dve", lambda blk, d=d, last=last: (lambda e: e.tensor_copy(
                        out=dec[d][:, blk * CPB:(blk + 1) * CPB], in_=v3(tC, blk)[:, :, last])),
                        lambda blk: [P.B("hC", blk)], lambda blk: [Bdec]))
                    ops.append(("dve", lambda blk, last=last: (lambda e: e.tensor_tensor(
                        out=v3(tB, blk), in0=v3(zin, blk), in1=v3(zin, blk)[:, :, last:last + 1].to_broadcast([128, CPB, 64]),
                        op=ALU.subtract)), lambda blk: [P.B("hz", blk), P.B("hB", blk)], lambda blk: [P.B("hB", blk)]))
                    ops.append(("act", lambda blk: (lambda e: e.activation(out=bl(tC, blk), in_=bl(tB, blk), func=AF.Exp,
                                                                           scale=-1.0)),
                                lambda blk: [P.B("hB", blk), P.B("hC", blk)], lambda blk: [P.B("hC", blk)]))
                    ops.append(("dve", lambda blk: (lambda e: e.tensor_tensor(
                        out=bl(Khf, blk), in0=bl(tA, blk), in1=bl(tC, blk), op=ALU.mult)),
                        lambda blk: [P.B("hA", blk), P.B("hC", blk)], lambda blk: [P.B("hKhf", blk)]))
                    for (eng, mk, rd, wr) in ops:
                        for blk in range(NBLK):
                            P.op(eng, mk(blk), rd(blk), wr(blk))
                    Bptr = P.B("hptr")
                    for c8 in range(8):
                        for cc in range(8):
                            c = c8 * 8 + cc
                            P.op("pe", lambda e, c=c, cc=cc: e.transpose(ptr[:, cc * 128:(cc + 1) * 128],
                                                                         Khf[:, c * 64:(c + 1) * 64], ident[:]),
                                 [P.B("hKhf", c8 // 2)], [Bptr])
                        P.op("act", lambda e, d=d, c8=c8: e.copy(out=Kh[d][:, c8 * 8:(c8 + 1) * 8, :],
                                                                 in_=ptr[:].rearrange("p (c v) -> p c v", c=8)),
                             [Bptr], [BKh])
                BAall = [P.B("hA", blk) for blk in range(4)]
                BBall = [P.B("hB", blk) for blk in range(4)]
                Bzall = [P.B("hz", blk) for blk in range(4)]
                BKfall = [P.B("hKhf", blk) for blk in range(4)]
                BO = [BAall, BBall]
                osb = [tA, tB]
                for d in range(2):
                    P.op("pool", lambda e, d=d: e.memset(St[d][:], 0.0), [], [P.B("hS", d)])
                    P.op("pool", lambda e, d=d: e.memset(Sb[d][:], 0.0), [], [P.B("hSb", d)])
                def chunk_of(step, d):
                    return step if d == 0 else NCH - 1 - step

                def emit_pa(step, d):
                    c = chunk_of(step, d)
                    k = step % 2
                    cs = slice(c * 64, (c + 1) * 64)
                    Bpa, BaT = P.B("hpa", d), P.B("haT", d, k)
                    P.op("pe", lambda e, d=d, cs=cs: e.matmul(pa[d][:], lhsT=Kt[d][:, cs], rhs=Qt[d][:, cs],
                                                              start=True, stop=True),
                         [P.B("hKt", d), P.B("hQt", d)], [Bpa])
                    P.op("dve", lambda e, d=d, k=k: e.tensor_tensor(out=aT[d * 2 + k][:], in0=pa[d][:], in1=tri[d][:],
                                                                    op=ALU.mult), [Bpa, Bc], [BaT])

                for d in range(2):
                    emit_pa(0, d)
                for step in range(NCH):
                    if step + 1 < NCH:
                        for d in range(2):
                            emit_pa(step + 1, d)
                    for d in range(2):
                        c = chunk_of(step, d)
                        k = step % 2
                        BaT, Bpo = P.B("haT", d, k), P.B("hpo", d)
                        slot = (step % 8) if d == 0 else 7 - (step % 8)
                        P.op("pe", lambda e, d=d, k=k, c=c, slot=slot: e.matmul(
                            po[d][:, slot * 64:(slot + 1) * 64], lhsT=Vt[:, c, :], rhs=aT[d * 2 + k][:],
                            start=True, stop=False), [P.B("hV"), BaT], [Bpo])
                    for d in range(2):
                        c = chunk_of(step, d)
                        Bpo, Bps = P.B("hpo", d), P.B("hps", d)
                        BS, BSb = P.B("hS", d), P.B("hSb", d)
                        cs = slice(c * 64, (c + 1) * 64)
                        slot = (step % 8) if d == 0 else 7 - (step % 8)
                        P.op("pe", lambda e, d=d, cs=cs, slot=slot: e.matmul(
                            po[d][:, slot * 64:(slot + 1) * 64], lhsT=Sb[d][:], rhs=Qh[d][:, cs],
                            start=False, stop=True), [BSb, P.B("hQh", d)], [Bpo])
                        P.op("pe", lambda e, d=d, c=c: e.matmul(pss[d][:], lhsT=Kh[d][:, c, :], rhs=Vt[:, c, :],
                                                                start=True, stop=True),
                             [P.B("hKh", d), P.B("hV")], [Bps])
                        P.op("dve", lambda e, d=d, c=c: e.scalar_tensor_tensor(out=St[d][:], in0=St[d][:],
                                                                               scalar=dec[d][:, c:c + 1], in1=pss[d][:],
                                                                               op0=ALU.mult, op1=ALU.add),
                             [BS, Bps, P.B("hdec", d)], [BS])
                        P.op("act", lambda e, d=d: e.copy(out=Sb[d][:], in_=St[d][:]), [BS], [BSb])
                        if step % 8 == 7:
                            g8 = step // 8
                            t0 = g8 * 512 if d == 0 else (7 - g8) * 512
                            P.op("act", lambda e, d=d, t0=t0: e.copy(out=osb[d][:, t0:t0 + 512], in_=po[d][:]),
                                 [Bpo], BO[d])
                P.op("dve", lambda e: e.tensor_tensor(out=tA[:], in0=tA[:], in1=tB[:], op=ALU.add), BAall + BBall, BAall)
                P.op("act", lambda e: e.activation(out=ob16[:], in_=tA[:], func=AF.Square), BAall, [P.B("hQt", 0)])
                P.dma("sp", zin[:], zr[s, 3, f0:f0 + 128, :], [], Bzall, "hz")
                P.op("act", lambda e: e.activation(out=tB[:], in_=zin[:], func=AF.Silu), Bzall + BBall, BBall)
                for blk in range(8):
                    bs = slice(blk * 512, (blk + 1) * 512)
                    k = blk % 2
                    Bpn, Brs = P.B("hpo", k), P.B("hC", k)
                    A4 = [P.B("hA", blk // 2)]
                    P.op("pe", lambda e, bs=bs, k=k: e.matmul(po[k][:], lhsT=ones_b[:], rhs=ob16[:, bs], start=True, stop=True),
                         [P.B("hQt", 0)], [Bpn])
                    P.op("act", lambda e, k=k: e.activation(out=rstd2[k], in_=po[k][:], func=AF.Sqrt, bias=epsh[:],
                                                            scale=1.0 / 128), [Bpn, P.B("hepsh")], [Brs])
                    P.op("dve", lambda e, k=k: e.reciprocal(out=rstd2[k], in_=rstd2[k]), [Brs], [Brs])
                    P.op("dve", lambda e, bs=bs, k=k: e.tensor_tensor(out=tA[:, bs], in0=tA[:, bs], in1=rstd2[k],
                                                                      op=ALU.mult), A4 + [Brs], A4)
                P.op("dve", lambda e: e.scalar_tensor_tensor(out=Khf[:], in0=tA[:], scalar=par[:, 6:7], in1=tB[:],
                                                             op0=ALU.mult, op1=ALU.mult), BAall + BBall + [Bpar], BKfall)
                P.dma("sp", mixT[s, f0:f0 + 128, :], Khf[:], BKfall, [], "hout")
        P.barrier()
        P.emit()


def mixer_rglru(P, nc, spc, idx, zr, mixT, conv_w, conv_b, rg_wa, rg_ba, rg_wx, rg_bx, rg_lambda):
    with ExitStack() as ph:
        def sb(name, shape, dt):
            return ph.enter_context(nc.sbuf_tensor(_uniq(name), list(shape), dt))

        def pst(name, shape, dt):
            return ph.enter_context(nc.psum_tensor(_uniq(name), list(shape), dt))

        xp = sb("rxp", [128, T + 4], F32)
        yy = sb("ry", [128, T], F32)
        yb = sb("ryb", [128, T], BF16)
        rr = sb("rr", [128, T], F32)
        ii = sb("ri", [128, T], F32)
        a1 = sb("ra", [128, T], F32)
        t1 = sb("rt", [128, T], F32)
        hh = [sb("rh%d" % d, [128, T], F32) for d in range(2)]
        ob = sb("rob", [128, T], BF16)
        wbd32 = sb("rw32", [128, 4, 128], F32)
        wbd = sb("rwbd", [128, 4, 128], BF16)
        par = sb("rpar", [128, 24], F32)
        pm = [pst("rpm%d" % i, [128, 512], F32) for i in range(4)]
        Bxp, By, Byb, Br, Bi, Ba, Bt, Bob = (P.B("rxp"), P.B("ry"), P.B("ryb"), P.B("rr"), P.B("ri"), P.B("ra"),
                                             P.B("rt"), P.B("rob"))
        Bh = [P.B("rh", d) for d in range(2)]
        Bw, Bpar = P.B("rw"), P.B("rpar")
        P.op("pool", lambda e: e.memset(xp[:], 0.0), [], [Bxp])
        col = lambda ap: ap.rearrange("(p o) -> p o", o=1)
        ec = 0
        for ch in range(4):
            c0 = ch * 128
            for k in range(4):
                P.dma("sp", par[:, k:k + 1], col(conv_w[idx, k, c0:c0 + 128]), [], [Bpar], "rp0")
            P.dma("sp", par[:, 4:5], col(conv_b[idx, c0:c0 + 128]), [], [Bpar], "rp1")
            for d in range(2):
                P.dma("sp", par[:, 5 + d:6 + d], col(rg_ba[idx, d, c0:c0 + 128]), [], [Bpar], "rp2")
                P.dma("sp", par[:, 7 + d:8 + d], col(rg_bx[idx, d, c0:c0 + 128]), [], [Bpar], "rp3")
                P.dma("sp", par[:, 9 + d:10 + d], col(rg_lambda[idx, d, c0:c0 + 128]), [], [Bpar], "rp4")
            P.op("act", lambda e: e.activation(out=par[:, 9:11], in_=par[:, 9:11], func=AF.Exp, scale=-1.0), [Bpar], [Bpar])
            P.op("dve", lambda e: e.tensor_scalar(out=par[:, 9:11], in0=par[:, 9:11], scalar1=1.0, scalar2=None, op0=ALU.add),
                 [Bpar], [Bpar])
            P.op("act", lambda e: e.activation(out=par[:, 9:11], in_=par[:, 9:11], func=AF.Ln), [Bpar], [Bpar])
            P.op("dve", lambda e: e.tensor_scalar(out=par[:, 11:13], in0=par[:, 9:11], scalar1=-16.0, scalar2=None,
                                                  op0=ALU.mult), [Bpar], [Bpar])
            P.op("dve", lambda e: e.tensor_scalar(out=par[:, 9:11], in0=par[:, 9:11], scalar1=-8.0, scalar2=None,
                                                  op0=ALU.mult), [Bpar], [Bpar])
            P.op("pool", lambda e: e.memset(wbd32[:], 0.0), [], [Bw])
            for d in range(2):
                for m, wsrc in enumerate((rg_wa, rg_wx)):
                    for blk in range(2):
                        P.dma("sp", wbd32[blk * 64:(blk + 1) * 64, d * 2 + m, blk * 64:(blk + 1) * 64],
                              wsrc[idx, d, ch * 2 + blk], [], [Bw], "rw")
            P.op("dve", lambda e: e.tensor_copy(out=wbd[:], in_=wbd32[:]), [Bw], [Bw])
            for s in range(spc):
                P.dma("sp", xp[:, 2:T + 2], zr[s, 1, c0:c0 + 128, :], [], [Bxp], "rx")
                P.op("dve", lambda e: e.tensor_scalar(out=yy[:], in0=xp[:, 0:T], scalar1=par[:, 0:1], scalar2=par[:, 4:5],
                                                      op0=ALU.mult, op1=ALU.add), [Bxp, Bpar], [By])
                for k in range(1, 4):
                    P.op("dve", lambda e, k=k: e.scalar_tensor_tensor(out=yy[:], in0=xp[:, k:k + T], scalar=par[:, k:k + 1],
                                                                      in1=yy[:], op0=ALU.mult, op1=ALU.add),
                         [Bxp, Bpar, By], [By])
                P.op("act", lambda e: e.copy(out=yb[:], in_=yy[:]), [By], [Byb])
                for d in range(2):
                    for blk in range(8):
                        bs = slice(blk * 512, (blk + 1) * 512)
                        for m, (dst, Bd, bc) in enumerate(((rr, Br, 5 + d), (ii, Bi, 7 + d))):
                            pb = ec % 4
                            ec += 1
                            Bpm = P.B("rpm", pb)
                            P.op("pe", lambda e, pb=pb, d=d, m=m, bs=bs: e.matmul(pm[pb][:], lhsT=wbd[:, d * 2 + m, :],
                                                                                  rhs=yb[:, bs], start=True, stop=True),
                                 [Bw, Byb], [Bpm])
                            P.op("act", lambda e, pb=pb, dst=dst, bs=bs, bc=bc: e.activation(
                                out=dst[:, bs], in_=pm[pb][:], func=AF.Sigmoid, bias=par[:, bc:bc + 1]),
                                [Bpm, Bpar], [Bd])
                    P.op("act", lambda e, d=d: e.activation(out=a1[:], in_=rr[:], func=AF.Exp, scale=par[:, 9 + d:10 + d]),
                         [Br, Bpar], [Ba])
                    P.op("act", lambda e, d=d: e.activation(out=t1[:], in_=rr[:], func=AF.Exp, scale=par[:, 11 + d:12 + d]),
                         [Br, Bpar], [Bt])
                    P.op("dve", lambda e: e.tensor_scalar(out=t1[:], in0=t1[:], scalar1=-1.0, scalar2=1.0,
                                                          op0=ALU.mult, op1=ALU.add), [Bt], [Bt])
                    P.op("act", lambda e: e.activation(out=t1[:], in_=t1[:], func=AF.Sqrt), [Bt], [Bt])
                    first = 0 if d == 0 else T - 1
                    P.op("pool", lambda e, first=first: e.memset(t1[:, first:first + 1], 1.0), [Bt], [Bt])
                    P.op("dve", lambda e: e.tensor_tensor(out=t1[:], in0=t1[:], in1=ii[:], op=ALU.mult), [Bt, Bi], [Bt])
                    P.op("dve", lambda e: e.tensor_tensor(out=t1[:], in0=t1[:], in1=yy[:], op=ALU.mult), [Bt, By], [Bt])
                    if d == 0:
                        P.op("dve", lambda e: e.tensor_tensor_scan(out=hh[0][:], data0=a1[:], data1=t1[:], initial=0.0,
                                                                   op0=ALU.mult, op1=ALU.add), [Ba, Bt], [Bh[0]])
                    else:
                        P.op("dve", lambda e: e.tensor_tensor_scan(out=hh[1][:, ::-1], data0=a1[:, ::-1],
                                                                   data1=t1[:, ::-1], initial=0.0,
                                                                   op0=ALU.mult, op1=ALU.add), [Ba, Bt], [Bh[1]])
                P.dma("sp", rr[:], zr[s, 0, c0:c0 + 128, :], [Br], [Br], "rg")
                P.op("act", lambda e: e.activation(out=ii[:], in_=rr[:], func=AF.Gelu_apprx_tanh), [Br, Bi], [Bi])
                P.op("dve", lambda e: e.tensor_tensor(out=hh[0][:], in0=hh[0][:], in1=hh[1][:], op=ALU.add),
                     [Bh[0], Bh[1]], [Bh[0]])
                P.op("dve", lambda e: e.tensor_tensor(out=ob[:], in0=hh[0][:], in1=ii[:], op=ALU.mult), [Bh[0], Bi], [Bob])
                P.dma("sp", mixT[s, c0:c0 + 128, :], ob[:], [Bob], [], "rout")
        P.barrier()
        P.emit()


def host_tables(na_rpb, t5_bias):
    hidx = np.arange(8)[None, :, None, None]
    na_tab = np.ascontiguousarray(
        np.stack([na_rpb[l][hidx, NA_DR[:, None], NA_DC[:, None]] for l in range(2)]).astype(np.float32))
    t5_tab = np.ascontiguousarray(np.transpose(t5_bias[T5_BK], (0, 3, 1, 2)).astype(np.float32))
    t = np.arange(T)
    consts = {
        "na_mask": NA_MASK, "t5_cnt": T5_CNT,
        "cmask_f": (t % 64 != 0).astype(np.float32)[None, :],
        "cmask_b": (t % 64 != 63).astype(np.float32)[None, :],
        "tri_f": (np.arange(64)[:, None] <= np.arange(64)[None, :]).astype(np.float32),
        "tri_b": (np.arange(64)[:, None] >= np.arange(64)[None, :]).astype(np.float32),
    }
    return na_tab, t5_tab, consts


_NC_CACHE = {}


def run(inputs, xs_per_core, layers=(0, 1, 2, 3), spc=SPC, debug=False, ncores=NCORES, trace=False):
    key = (tuple(layers), spc, debug)
    if key not in _NC_CACHE:
        _NC_CACHE[key] = build(layers, spc, debug)
    nc = _NC_CACHE[key]
    na_tab, t5_tab, consts = host_tables(np.asarray(inputs["na_rpb"]), np.asarray(inputs["t5_bias"]))
    shared = {k: np.ascontiguousarray(np.asarray(inputs[k], dtype=np.float32)) for k in (
        "norm_g", "w_in_even", "w_out_even", "w_in_odd", "w_out_odd", "w_gate", "w_up", "w_down", "hgrn_lb",
        "hgrn_onorm", "conv_w", "conv_b", "rg_wa", "rg_ba", "rg_wx", "rg_bx", "rg_lambda")}
    shared["na_tab"] = na_tab
    shared["t5_tab"] = t5_tab
    shared.update(consts)
    in_maps = []
    for c in range(ncores):
        m = dict(shared)
        m["x"] = xs_per_core[c]
        in_maps.append(m)
    res = run_bass_kernel_spmd(nc, in_maps, core_ids=list(range(ncores)), **({"trace": True} if trace else {}))
    return res


def kernel(**inputs):
    xp = np.asarray(inputs["x_prompt"], dtype=np.float32)
    xs = np.asarray(inputs["x_sample"], dtype=np.float32)
    allx = np.concatenate([xp, xs], axis=0)
    nseq = allx.shape[0]
    slots = np.zeros((NCORES * SPC, T, D), np.float32)
    slots[:nseq] = allx
    xs_per_core = [np.ascontiguousarray(slots[c * SPC:(c + 1) * SPC]) for c in range(NCORES)]
    res = run(inputs, xs_per_core)
    yall = np.concatenate([r["y"] for r in res.results], axis=0)[:nseq]
    return (np.ascontiguousarray(yall[:xp.shape[0]]), np.ascontiguousarray(yall[xp.shape[0]:]))
```
